# Optimizing a Trainium2 kernel written in Bass

```python
import jax
import jax.numpy as jnp
from jax import lax
import numpy as np

D_MODEL = 2048
BATCH = 2
SEQ = 8192
DEPTH = 4

GRID_W = 64
HEAD_DIM = 128
NA_HEADS = 8
NA_WIN_ROWS = 8
NA_WIN_COLS = 16
NA_BIAS_ROWS = 2 * NA_WIN_ROWS - 1
NA_BIAS_COLS = 2 * NA_WIN_COLS - 1
GQA_Q_HEADS = 8
GQA_KV_HEADS = 2
Q_BLOCK = 128
ROPE_THETA = 10000.0
CONV_CH = 1024
CONV_WIDTH = 3
FFN_HIDDEN = -(-8 * D_MODEL // (3 * 256)) * 256
RMS_EPS = 1e-6

NA_WIDTH = NA_HEADS * HEAD_DIM
GQA_Q_WIDTH = GQA_Q_HEADS * HEAD_DIM
GQA_KV_WIDTH = GQA_KV_HEADS * HEAD_DIM
IN_SIZES = (NA_WIDTH, NA_WIDTH, NA_WIDTH, GQA_Q_WIDTH, GQA_KV_WIDTH, GQA_KV_WIDTH,
            CONV_CH, CONV_CH, CONV_CH, D_MODEL, D_MODEL, D_MODEL)
IN_COLS = sum(IN_SIZES)
IN_OFFSETS = tuple(int(o) for o in np.cumsum(IN_SIZES)[:-1])

kernel_name = 'hybrid_na_gqa_shortconv_encoder'


def rms_norm(x, g):
    xf = x.astype(jnp.float32)
    y = xf * lax.rsqrt(jnp.mean(xf * xf, axis=-1, keepdims=True) + RMS_EPS)
    return (y * g.astype(jnp.float32)).astype(x.dtype)


def neighbourhood_attention(q, k, v, rpb):
    bsz, s_len, n_h, d = q.shape
    rows = s_len // GRID_W
    kr = min(NA_WIN_ROWS, rows)
    kc = NA_WIN_COLS
    qg = q.reshape(bsz, rows, GRID_W, n_h, d)
    kg = k.reshape(bsz, rows, GRID_W, n_h, d)
    vg = v.reshape(bsz, rows, GRID_W, n_h, d)
    cols = jnp.arange(GRID_W)
    col_start = jnp.clip(cols - kc // 2, 0, GRID_W - kc)
    col_idx = col_start[:, None] + jnp.arange(kc)[None, :]
    dc = col_idx - cols[:, None]
    scale = d ** -0.5

    def row_step(r):
        row_start = jnp.clip(r - kr // 2, 0, rows - kr)
        k_band = lax.dynamic_slice_in_dim(kg, row_start, kr, axis=1)
        v_band = lax.dynamic_slice_in_dim(vg, row_start, kr, axis=1)
        k_nb = k_band[:, :, col_idx]
        v_nb = v_band[:, :, col_idx]
        q_row = lax.dynamic_index_in_dim(qg, r, axis=1, keepdims=False)
        dr = row_start + jnp.arange(kr) - r
        bias = rpb[:, (dr + NA_WIN_ROWS - 1)[None, :, None],
                   (dc + NA_WIN_COLS - 1)[:, None, :]]
        s = jnp.einsum('bwhd,brwchd->bhwrc', q_row, k_nb).astype(jnp.float32) * scale
        s = s + bias.astype(jnp.float32)[None]
        p = jax.nn.softmax(s.reshape(bsz, n_h, GRID_W, kr * kc), axis=-1)
        p = p.reshape(bsz, n_h, GRID_W, kr, kc).astype(v.dtype)
        return jnp.einsum('bhwrc,brwchd->bwhd', p, v_nb)

    out = lax.map(row_step, jnp.arange(rows))
    return out.transpose(1, 0, 2, 3, 4).reshape(bsz, s_len, n_h * d)


def axial_rope(x, pos_row, pos_col):
    d = x.shape[-1]
    half = d // 2
    quarter = half // 2
    inv_freq = 1.0 / (ROPE_THETA ** (jnp.arange(quarter, dtype=jnp.float32) / quarter))
    xf = x.astype(jnp.float32)

    def rot(xp, pos):
        ang = pos.astype(jnp.float32)[:, None] * inv_freq[None, :]
        c = jnp.cos(ang)[None, :, None, :]
        s = jnp.sin(ang)[None, :, None, :]
        x1, x2 = xp[..., :quarter], xp[..., quarter:]
        return jnp.concatenate([x1 * c - x2 * s, x1 * s + x2 * c], axis=-1)

    out = jnp.concatenate([rot(xf[..., :half], pos_row), rot(xf[..., half:], pos_col)], axis=-1)
    return out.astype(x.dtype)


def gqa_attention(q, k, v, q_norm_g, k_norm_g):
    bsz, s_len, n_q, d = q.shape
    n_kv = k.shape[2]
    grp = n_q // n_kv
    t = jnp.arange(s_len)
    pos_row, pos_col = t // GRID_W, t % GRID_W
    q = axial_rope(rms_norm(q, q_norm_g), pos_row, pos_col)
    k = axial_rope(rms_norm(k, k_norm_g), pos_row, pos_col)
    n_blk = s_len // Q_BLOCK
    qb = q.reshape(bsz, n_blk, Q_BLOCK, n_kv, grp, d).transpose(1, 0, 2, 3, 4, 5)
    scale = d ** -0.5

    def block_step(q_blk):
        s = jnp.einsum('bqkgd,bskd->bkgqs', q_blk, k).astype(jnp.float32) * scale
        p = jax.nn.softmax(s, axis=-1).astype(v.dtype)
        return jnp.einsum('bkgqs,bskd->bqkgd', p, v)

    out = lax.map(block_step, qb)
    return out.transpose(1, 0, 2, 3, 4, 5).reshape(bsz, s_len, n_q * d)


def short_conv_mixer(h, b_gate, c_gate, conv_w, conv_b):
    s_len = h.shape[1]
    u = c_gate * h
    pad = CONV_WIDTH // 2
    up = jnp.pad(u, ((0, 0), (pad, pad), (0, 0)))
    y = conv_b + sum(up[:, j:j + s_len] * conv_w[j] for j in range(CONV_WIDTH))
    return b_gate * y


def hybrid_layer(x, w_in, na_rpb, q_norm_g, k_norm_g, conv_w, conv_b, w_br_na, w_br_gqa,
                 w_br_conv, w_out, pre_mix_g, post_mix_g, pre_ffn_g, post_ffn_g,
                 w_ffn_gate, w_ffn_up, w_ffn_down):
    bsz, s_len, _ = x.shape
    h = rms_norm(x, pre_mix_g)
    proj = h @ w_in
    (na_q, na_k, na_v, g_q, g_k, g_v, cv_h, cv_bg, cv_cg,
     gate_na, gate_gqa, gate_conv) = jnp.split(proj, IN_OFFSETS, axis=-1)

    def heads(t, n):
        return t.reshape(bsz, s_len, n, HEAD_DIM)

    y_na = neighbourhood_attention(heads(na_q, NA_HEADS), heads(na_k, NA_HEADS),
                                   heads(na_v, NA_HEADS), na_rpb) @ w_br_na
    y_gqa = gqa_attention(heads(g_q, GQA_Q_HEADS), heads(g_k, GQA_KV_HEADS),
                          heads(g_v, GQA_KV_HEADS), q_norm_g, k_norm_g) @ w_br_gqa
    y_conv = short_conv_mixer(cv_h, cv_bg, cv_cg, conv_w, conv_b) @ w_br_conv
    merged = (jax.nn.sigmoid(gate_na) * y_na + jax.nn.sigmoid(gate_gqa) * y_gqa
              + jax.nn.sigmoid(gate_conv) * y_conv)
    x = x + rms_norm(merged @ w_out, post_mix_g)

    h = rms_norm(x, pre_ffn_g)
    f = (jax.nn.silu(h @ w_ffn_gate) * (h @ w_ffn_up)) @ w_ffn_down
    return x + rms_norm(f, post_ffn_g)


def setup_inputs(seed: int = 0) -> dict:
    key = jax.random.key(seed)
    ks = jax.random.split(key, 19)

    def nrm(k, shape, scale):
        return jax.random.normal(k, shape, jnp.float32) * scale

    def gain(k, shape):
        return 1.0 + nrm(k, shape, 0.05)

    L, D = DEPTH, D_MODEL
    return {
        'x': nrm(ks[0], (BATCH, SEQ, D), 1.0),
        'w_in': nrm(ks[1], (L, D, IN_COLS), D ** -0.5),
        'na_rpb': nrm(ks[2], (L, NA_HEADS, NA_BIAS_ROWS, NA_BIAS_COLS), 0.5),
        'q_norm_g': gain(ks[3], (L, HEAD_DIM)),
        'k_norm_g': gain(ks[4], (L, HEAD_DIM)),
        'conv_w': nrm(ks[5], (L, CONV_WIDTH, CONV_CH), CONV_WIDTH ** -0.5),
        'conv_b': nrm(ks[6], (L, CONV_CH), 0.02),
        'w_br_na': nrm(ks[7], (L, NA_WIDTH, D), NA_WIDTH ** -0.5),
        'w_br_gqa': nrm(ks[8], (L, GQA_Q_WIDTH, D), GQA_Q_WIDTH ** -0.5),
        'w_br_conv': nrm(ks[9], (L, CONV_CH, D), CONV_CH ** -0.5),
        'w_out': nrm(ks[10], (L, D, D), D ** -0.5),
        'pre_mix_g': gain(ks[11], (L, D)),
        'post_mix_g': gain(ks[12], (L, D)),
        'pre_ffn_g': gain(ks[13], (L, D)),
        'post_ffn_g': gain(ks[14], (L, D)),
        'w_ffn_gate': nrm(ks[15], (L, D, FFN_HIDDEN), D ** -0.5),
        'w_ffn_up': nrm(ks[16], (L, D, FFN_HIDDEN), D ** -0.5),
        'w_ffn_down': nrm(ks[17], (L, FFN_HIDDEN, D), FFN_HIDDEN ** -0.5),
    }


def reference(x, w_in, na_rpb, q_norm_g, k_norm_g, conv_w, conv_b, w_br_na, w_br_gqa,
              w_br_conv, w_out, pre_mix_g, post_mix_g, pre_ffn_g, post_ffn_g,
              w_ffn_gate, w_ffn_up, w_ffn_down):
    for l in range(DEPTH):
        x = hybrid_layer(x, w_in[l], na_rpb[l], q_norm_g[l], k_norm_g[l], conv_w[l], conv_b[l],
                         w_br_na[l], w_br_gqa[l], w_br_conv[l], w_out[l],
                         pre_mix_g[l], post_mix_g[l], pre_ffn_g[l], post_ffn_g[l],
                         w_ffn_gate[l], w_ffn_up[l], w_ffn_down[l])
    return x
```

```python
import numpy as np
import ml_dtypes
from contextlib import ExitStack
import concourse.bass as bass
import concourse.mybir as mybir
from concourse.bass_utils import run_bass_kernel_spmd

F32 = mybir.dt.float32
BF16 = mybir.dt.bfloat16
AF = mybir.ActivationFunctionType
ALU = mybir.AluOpType
NPBF = ml_dtypes.bfloat16

NCORE = 8
NTOK = 2048
D = 2048
KC = 16
FF = 5632
FC = 44
INC = 13824
SCALE = 128.0 ** -0.5
EPS = 1e-6
NEG = -30000.0
ENGS = ["pe", "act", "dve", "pool", "sp"]
DEPTH = 4


class Tok:
    __slots__ = ("sem", "key", "val")

    def __init__(self, sem, key, val):
        self.sem, self.key, self.val = sem, key, val


class _Rec:
    def __getattr__(self, name):
        return lambda *a, **k: (name, a, k)


_REC = _Rec()


class Ctx:
    def __init__(self, nc, es):
        self.nc, self.es = nc, es
        self.esem = {e: es.enter_context(nc.semaphore("m_" + e)) for e in ["pe", "act", "dve", "pool"]}
        self.ecnt = {e: 0 for e in self.esem}
        self.dsem = {}
        self.bar = es.enter_context(nc.semaphore("bar"))
        self.barcnt = 0
        self.seen = {e: {} for e in ENGS}
        self.scr = es.enter_context(nc.sbuf_tensor("scr", [128, 8], F32))
        self.first = True

    def dma_sem(self, name):
        if name not in self.dsem:
            self.dsem[name] = [self.es.enter_context(self.nc.semaphore("d_" + name)), 0]
        return self.dsem[name]


class Phase:
    def __init__(self, ctx):
        self.c = ctx
        self.ops = {e: [] for e in ENGS}
        self.dtoks = {e: {} for e in ENGS}

    def op(self, eng, fn, waits=(), sig=False, chain=True):
        tok = None
        waits = list(waits)
        if eng != "pe":
            sig = True
            if chain and self.c.ecnt[eng] > 0:
                waits.append(Tok(self.c.esem[eng], "m_" + eng, self.c.ecnt[eng]))
        if sig:
            self.c.ecnt[eng] += 1
            tok = Tok(self.c.esem[eng], "m_" + eng, self.c.ecnt[eng])
        name, a, k = fn(_REC)
        self.ops[eng].append((lambda e, name=name, a=a, k=k: getattr(e, name)(*a, **k),
                              tuple(w for w in waits if w is not None), tok, 1))
        return tok

    def dma(self, eng, out, in_, sem, waits=()):
        s = self.c.dma_sem(sem)
        s[1] += 16
        tok = Tok(s[0], "d_" + sem, s[1])
        self.ops[eng].append((lambda e, o=out, i=in_: e.dma_start(out=o, in_=i),
                              tuple(w for w in waits if w is not None), tok, 16))
        self.dtoks[eng][tok.key] = tok
        return tok

    def coll(self, ins, outs, waits=()):
        s = self.c.dma_sem("cc")
        s[1] += 1
        tok = Tok(s[0], "d_cc", s[1])
        fn = lambda e, i=ins, o=outs: e.collective_compute(
            "AllGather", ALU.bypass, replica_groups=[[0, 1, 2, 3], [4, 5, 6, 7]], ins=[i], outs=[o])
        self.ops["pool"].append((fn, tuple(w for w in waits if w is not None), tok, 1))
        self.dtoks["pool"][tok.key] = tok
        return tok

    def run(self):
        c = self.c
        nc = c.nc
        c.barcnt += 4
        barv = c.barcnt

        def emit(eng, e):
            seen = c.seen[eng]

            def wait(t):
                if seen.get(t.key, 0) < t.val:
                    e.wait_ge(t.sem, t.val)
                    seen[t.key] = t.val

            for fn, waits, tok, inc in self.ops[eng]:
                for w in waits:
                    wait(w)
                ins = fn(e)
                if tok is not None:
                    ins.then_inc(tok.sem, inc)
            for t in self.dtoks[eng].values():
                wait(t)
            if eng == "sp":
                e.sem_inc(c.bar, 1)
            elif eng == "act":
                e.memzero(c.scr[:, 0:1]).then_inc(c.bar, 1)
            elif eng == "dve":
                e.memset(c.scr[:, 1:2], 0.0).then_inc(c.bar, 1)
            elif eng == "pool":
                e.memset(c.scr[:, 2:3], 0.0).then_inc(c.bar, 1)
            e.wait_ge(c.bar, barv)

        with nc.Block() as block:
            @block.tensor
            def _(e):
                emit("pe", e)

            @block.scalar
            def _(e):
                emit("act", e)

            @block.vector
            def _(e):
                emit("dve", e)

            @block.gpsimd
            def _(e):
                emit("pool", e)

            @block.sync
            def _(e):
                emit("sp", e)


class Ring:
    def __init__(self, aps):
        self.aps = list(aps)
        self.free = [None] * len(self.aps)
        self.i = 0

    def get(self):
        k = self.i % len(self.aps)
        self.i += 1
        return k, self.aps[k], self.free[k]


def gemm(ph, tiles, wring, wsem, psring, pe_waits=()):
    active = []

    def advance():
        for g in list(active):
            try:
                next(g)
            except StopIteration:
                active.remove(g)

    tiles = list(tiles)

    def issue(tile):
        ws, wap, wfree = wring.get()
        wt = None
        for dst_fn, src in tile["pieces"]:
            wt = ph.dma("pool", dst_fn(wap), src, f"{wsem}{ws}", waits=[wfree])
        return ws, wap, wt

    pending = issue(tiles[0]) if tiles else None
    for ti, tile in enumerate(tiles):
        ws, wap, wt = pending
        if ti + 1 < len(tiles):
            pending = issue(tiles[ti + 1])
        units = tile["units"]
        last_tok = None
        for ui, unit in enumerate(units):
            pk, pap, pfree = psring.get()
            mms = unit["mm"]
            tok = None
            nmm = sum(len(kl) for _, kl in mms)
            cnt = 0
            for bank, klist in mms:
                for ki, (lhs_fn, rhs) in enumerate(klist):
                    cnt += 1
                    lastmm = cnt == nmm
                    tok = ph.op("pe", lambda e, o=pap[:, bank, 0:rhs.shape[-1] if len(rhs.shape) == 2 else 512], l=lhs_fn(wap), r=rhs,
                                a=(ki == 0), b=(ki == len(klist) - 1): e.matmul(o, lhsT=l, rhs=r, start=a, stop=b),
                                waits=[wt, pfree] + list(pe_waits) if cnt == 1 else (), sig=lastmm)
            last_tok = tok

            def setfree(t, k=pk):
                psring.free[k] = t

            advance()
            g = unit["evac"](pap, tok, setfree)
            active.append(g)
            try:
                next(g)
            except StopIteration:
                active.remove(g)
        wring.free[ws] = last_tok
    while active:
        advance()


def build(L, dbg=False, stop_after=None, skip=()):
    nc = bass.Bass("TRN2", target_bir_lowering=False)
    es = ExitStack()
    uid = [0]

    def uname(n):
        uid[0] += 1
        return f"{n}_{uid[0]}"

    def din(name, shape, dt=F32):
        return nc.dram_tensor(name, list(shape), dt, kind="ExternalInput").ap()

    def dscr(name, shape, dt):
        if dbg and name in ("brT_d", "mT_d", "xmid", "qTg_d", "qTna_d", "kTna_d", "vna_d", "hT_d", "uT_d", "bgT_d"):
            return nc.dram_tensor(name, list(shape), dt, kind="ExternalOutput").ap()
        return nc.dram_tensor(name, list(shape), dt).ap()

    xT_in = din("xT", [KC, 128, NTOK])
    w_in = din("w_in", [L, D, INC])
    w_br = din("w_br", [L, 3, 1024, D])
    w_out = din("w_out", [L, D, D])
    small_ffn = stop_after is not None and stop_after <= 9
    w_gate = din("w_gate", [L, 128, 128] if small_ffn else [L, D, FF])
    w_up = din("w_up", [L, 128, 128] if small_ffn else [L, D, FF])
    w_down = din("w_down", [L, 128, 128] if small_ffn else [L, FF, D])
    gains_d = din("gains", [L, 128, 4 * KC])
    qkg_d = din("qkg", [L, 128, 2])
    convp_d = din("convp", [L, 128, 32])
    nab_d = din("nab", [L, 5, 8, 128, 768], BF16)
    ropeC_d = din("ropeC", [128, NTOK])
    ropeS_d = din("ropeS", [128, NTOK])
    ident_d = din("ident", [128, 128], BF16)
    rotT_d = din("rotT", [128, 128])
    sel_d = din("sel", [128, 8])
    yT = nc.dram_tensor("yT", [KC, 128, NTOK], F32, kind="ExternalOutput").ap()

    xmid = dscr("xmid", [KC, 128, NTOK], F32)
    xres = dscr("xres", [KC, 128, NTOK], F32)
    hT_d = dscr("hT_d", [KC, 128, NTOK], BF16)
    qTna_d = dscr("qTna_d", [8, 128, NTOK], BF16)
    kTna_d = dscr("kTna_d", [8, 128, NTOK], BF16)
    vna_d = dscr("vna_d", [NTOK, 1024], BF16)
    qTg_d = dscr("qTg_d", [8, 128, NTOK], BF16)
    send_t = [nc.dram_tensor(f"send{i}_d", [256, 2048], BF16) for i in range(4)]
    recv_t = [nc.dram_tensor(f"recv{i}_d", [1024, 2048], BF16) for i in range(4)]
    ub_d = nc.dram_tensor("ub_d", [256, 8], F32)
    uball_d = nc.dram_tensor("uball_d", [1024, 8], F32)
    uT_d = dscr("uT_d", [8, 128, NTOK], F32)
    bgT_d = dscr("bgT_d", [8, 128, NTOK], F32)
    brT_d = dscr("brT_d", [3, 8, 128, NTOK], BF16)
    mT_d = dscr("mT_d", [KC, 128, NTOK], BF16)
    aT_d = dscr("aT_d", [FC, 128, NTOK], BF16)
    kTg_send = send_t[0].ap()
    vg_send = send_t[1].ap().rearrange("r (a c) -> (r a) c", c=256)
    hk_send = send_t[2].ap().rearrange("r (a c) -> (r a) c", c=256)
    hv_send = send_t[3].ap().rearrange("r (a c) -> (r a) c", c=1024)

    def recv_view(i, r):
        return recv_t[i].ap()[r * 256:(r + 1) * 256, :]

    ctx = Ctx(nc, es)
    S = lambda name, shape, dt: es.enter_context(nc.sbuf_tensor("s_" + name, list(shape), dt))
    psf = es.enter_context(nc.psum_tensor("psf", [128, 6, 512], F32))
    pst = [es.enter_context(nc.psum_tensor("pst0", [128, 1024], BF16)), es.enter_context(nc.psum_tensor("pst1", [128, 1024], BF16))]

    ident = S("ident", [128, 128], BF16)
    rotT = S("rotT", [128, 128], F32)
    onesD = S("onesD", [128, 128], F32)
    onesH = S("onesH", [128, 128], F32)
    sel = S("sel", [128, 8], F32)
    gains = S("gains", [128, L, 4 * KC], F32)
    qkg = S("qkg", [128, L, 2], F32)
    convp = S("convp", [128, L, 32], F32)
    epsT = S("epsT", [128, 1], F32)

    ph = Phase(ctx)
    t0 = ph.dma("sp", ident[:], ident_d, "c0")
    ph.dma("sp", rotT[:], rotT_d, "c0")
    ph.dma("sp", sel[:], sel_d, "c0")
    ph.dma("sp", gains[:], gains_d.rearrange("l p k -> p l k"), "c0")
    ph.dma("sp", qkg[:], qkg_d.rearrange("l p k -> p l k"), "c0")
    ph.dma("sp", convp[:], convp_d.rearrange("l p k -> p l k"), "c0")
    ph.op("pool", lambda e: e.memset(onesD[:], 1.0 / D))
    ph.op("pool", lambda e: e.memset(onesH[:], 1.0 / 128))
    ph.op("pool", lambda e: e.memset(epsT[:], EPS))
    ph.run()

    def rstd_ops(ph, out_ap, in_ap, waits):
        a = ph.op("act", lambda e: e.activation(out=out_ap, in_=in_ap, func=AF.Sqrt, bias=epsT[:, 0:1], scale=1.0), waits=waits, sig=True)
        r = ph.op("dve", lambda e: e.reciprocal(out=out_ap, in_=out_ap), waits=[a], sig=True)
        return a, r

    def phase_norm(X, g_ap):
        with ExitStack() as es2:
            T = lambda name, shape, dt: es2.enter_context(nc.sbuf_tensor(uname(name), list(shape), dt))
            xt = T("n_xt", [128, 2, KC, 512], F32)
            sq = T("n_sq", [128, 2, 4, 512], F32)
            rs = T("n_rs", [128, 2, 512], F32)
            hb = T("n_hb", [128, 2, KC, 512], BF16)
            ph = Phase(ctx)
            xfree = [None, None]
            sqfree = [None, None]
            hbfree = [None, None]
            rsfree = [None, None]
            psfree = [None, None]
            for tg in range(4):
                s = tg % 2
                ts = slice(tg * 512, (tg + 1) * 512)
                ld = []
                for q in range(4):
                    ld.append(ph.dma("sp", xt[:, s, 4 * q:4 * q + 4, :], X[4 * q:4 * q + 4, :, ts].rearrange("k p t -> p k t"),
                                     f"nx{s}{q}", waits=[xfree[s]]))
                mmtok = None
                for q in range(4):
                    qs = (tg * 4 + q) % 2
                    a = ph.op("act", lambda e, o=sq[:, qs], i=xt[:, s, 4 * q:4 * q + 4, :]: e.activation(out=o, in_=i, func=AF.Square),
                              waits=[ld[q], sqfree[qs]], sig=True)
                    for k in range(4):
                        kc = 4 * q + k
                        mmtok = ph.op("pe", lambda e, o=psf[:, s, :], r=sq[:, qs, k, :], st=(kc == 0), sp=(kc == KC - 1):
                                      e.matmul(o, lhsT=onesD[:], rhs=r, start=st, stop=sp),
                                      waits=[a, psfree[s]] if k == 0 else (), sig=(k == 3))
                    sqfree[qs] = mmtok
                ra, r = rstd_ops(ph, rs[:, s, :], psf[:, s, :], [mmtok, rsfree[s]])
                psfree[s] = ra
                last = {}
                for kc in range(KC):
                    eng = "dve"
                    last[eng] = ph.op(eng, lambda e, o=hb[:, s, kc, :], i=xt[:, s, kc, :], g=g_ap[:, kc:kc + 1], rr=rs[:, s, :]:
                                      e.scalar_tensor_tensor(out=o, in0=i, scalar=g, in1=rr, op0=ALU.mult, op1=ALU.mult),
                                      waits=[r, hbfree[s], ld[3]] if kc < 2 else (), sig=(kc >= KC - 1))
                xfree[s] = None
                st = ph.dma("sp", hT_d[:, :, ts].rearrange("k p t -> p k t"), hb[:, s], f"nh{s}", waits=[last["dve"]])
                hbfree[s] = st
                rsfree[s] = st
                xfree[s] = st
            ph.run()

    def phase_proj(l):
        with ExitStack() as es2:
            T = lambda name, shape, dt: es2.enter_context(nc.sbuf_tensor(uname(name), list(shape), dt))
            hT = T("p_hT", [128, KC, NTOK], BF16)
            rC = T("p_rC", [128, NTOK], F32)
            rS = T("p_rS", [128, NTOK], F32)
            wt = T("p_wt", [128, 2, KC, 512], BF16)
            stg = T("p_stg", [128, 3, 1024], BF16)
            stf = T("p_stf", [128, 3, 1024], F32)
            htmp = T("p_htmp", [128, 2, 1024], F32)
            sqb = T("p_sqb", [128, 2, 512], F32)
            rb = T("p_rb", [128, 2, 512], F32)
            qn = T("p_qn", [128, 2, 512], F32)
            t1 = T("p_t1", [128, 2, 512], F32)
            t2 = T("p_t2", [128, 2, 512], F32)
            ub = T("p_ub", [128, 2, 8], F32)
            vst = T("p_vst", [128, 3, 512], BF16)
            ph = Phase(ctx)
            hld = []
            for q in range(4):
                hld.append(ph.dma("sp", hT[:, 4 * q:4 * q + 4, :], hT_d[4 * q:4 * q + 4].rearrange("k p t -> p k t"), "ph"))
            ph.dma("sp", rC[:], ropeC_d, "ph")
            hld_all = ph.dma("sp", rS[:], ropeS_d, "ph")
            wring = Ring([wt[:, 0], wt[:, 1]])
            psring = Ring([psf[:, 0:2, :], psf[:, 2:4, :]])
            stgR = Ring([stg[:, i, :] for i in range(3)])
            stfR = Ring([stf[:, i, :] for i in range(3)])
            auxA = [None]
            auxB = [None]
            cnt = [0]
            pmul = [None, None]
            padd = [None, None]
            W = w_in[l]

            def wpiece(c0, ncols, dst0=0):
                return (lambda wap, d=dst0, n=ncols: wap[:, :, d:d + n],
                        W[:, c0:c0 + ncols].rearrange("(k p) n -> p k n", p=128))

            def mm_units(ncc, evac_of):
                units = []
                for cc in range(ncc):
                    for tb in range(2):
                        mm = []
                        for b in range(2):
                            ts = slice(tb * 1024 + b * 512, tb * 1024 + (b + 1) * 512)
                            mm.append((b, [(lambda wap, k=k, cc=cc: wap[:, k, cc * 128:(cc + 1) * 128], hT[:, k, ts]) for k in range(KC)]))
                        units.append(dict(mm=mm, evac=evac_of(cc, tb)))
                return units

            def ev_simple(dst_of, scale, eng):
                def evac_of(cc, tb):
                    def gen(pap, tok, setfree):
                        k, sap, sfree = stgR.get()
                        if eng == "act":
                            t = ph.op("act", lambda e: e.activation(out=sap, in_=pap.rearrange("p b t -> p (b t)"), func=AF.Copy, scale=scale),
                                      waits=[tok, sfree, hld_all], sig=True)
                        else:
                            t = ph.op("dve", lambda e: e.tensor_copy(out=sap, in_=pap.rearrange("p b t -> p (b t)")),
                                      waits=[tok, sfree, hld_all], sig=True)
                        setfree(t)
                        stgR.free[k] = ph.dma("sp", dst_of(cc)[:, tb * 1024:(tb + 1) * 1024], sap, f"ps{k}", waits=[t])
                        yield
                    return gen
                return evac_of

            def ev_rope(dst_of, gcol):
                def evac_of(cc, tb):
                    def gen(pap, tok, setfree):
                        k, sap, sfree = stgR.get()
                        lastd = None
                        for b in range(2):
                            i = cnt[0] % 2
                            cnt[0] += 1
                            ts = slice(tb * 1024 + b * 512, tb * 1024 + (b + 1) * 512)
                            a = ph.op("act", lambda e, o=sqb[:, i, :], p=pap[:, b, :]: e.activation(out=o, in_=p, func=AF.Square),
                                      waits=[tok, hld_all], sig=True)
                            m = ph.op("pe", lambda e, r=sqb[:, i, :]: e.matmul(psf[:, 4, :], lhsT=onesH[:], rhs=r, start=True, stop=True),
                                      waits=[a, auxA[0]], sig=True)
                            ra_, r_ = rstd_ops(ph, rb[:, i, :], psf[:, 4, :], [m])
                            auxA[0] = ra_
                            q_ = ph.op("dve", lambda e, o=qn[:, i, :], p=pap[:, b, :], rr=rb[:, i, :]:
                                       e.scalar_tensor_tensor(out=o, in0=p, scalar=qkg[:, l, gcol:gcol + 1], in1=rr, op0=ALU.mult, op1=ALU.mult),
                                       waits=[pmul[i]], sig=True)
                            lastd = q_
                            pq = ph.op("pe", lambda e, r=qn[:, i, :]: e.matmul(psf[:, 5, :], lhsT=rotT[:], rhs=r, start=True, stop=True),
                                       waits=[q_, auxB[0]], sig=True)
                            pmul[i] = ph.op("pool", lambda e, o=t1[:, i, :], a_=qn[:, i, :], c_=rC[:, ts]: e.tensor_tensor(out=o, in0=a_, in1=c_, op=ALU.mult),
                                            waits=[q_], sig=True)
                            g_ = ph.op("dve", lambda e, o=t2[:, i, :], s_=rS[:, ts]: e.tensor_tensor(out=o, in0=psf[:, 5, :], in1=s_, op=ALU.mult),
                                       waits=[pq, padd[i]], sig=True)
                            auxB[0] = g_
                            lastp = ph.op("pool", lambda e, o=sap[:, b * 512:(b + 1) * 512], a_=t1[:, i, :], b_=t2[:, i, :]:
                                          e.tensor_tensor(out=o, in0=a_, in1=b_, op=ALU.add), waits=[g_, sfree], sig=True)
                            padd[i] = lastp
                        setfree(lastd)
                        stgR.free[k] = ph.dma("sp", dst_of(cc)[:, tb * 1024:(tb + 1) * 1024], sap, f"ps{k}", waits=[lastp])
                        yield
                    return gen
                return evac_of

            hslot = {}

            def ev_conv(ci):
                def evac_of(cc, tb):
                    def gen(pap, tok, setfree):
                        pflat = pap.rearrange("p b t -> p (b t)")
                        if cc == 0:
                            t = ph.op("act", lambda e: e.activation(out=htmp[:, tb, :], in_=pflat, func=AF.Copy),
                                      waits=[tok, hslot.get(tb)], sig=True)
                            hslot[("h", tb)] = t
                            setfree(t)
                        elif cc == 1:
                            k, sap, sfree = stfR.get()
                            t = ph.op("dve", lambda e: e.tensor_tensor(out=sap, in0=pflat, in1=htmp[:, tb, :], op=ALU.mult),
                                      waits=[tok, sfree, hslot[("h", tb)]], sig=True)
                            hslot[tb] = t
                            setfree(t)
                            col = 0 if tb == 0 else 1023
                            t2_ = ph.op("dve", lambda e: e.tensor_copy(out=ub[:, tb, ci:ci + 1], in_=sap[:, col:col + 1]), sig=True)
                            stfR.free[k] = ph.dma("sp", uT_d[ci][:, tb * 1024:(tb + 1) * 1024], sap, f"pf{k}", waits=[t2_])
                        else:
                            k, sap, sfree = stfR.get()
                            t = ph.op("act", lambda e: e.activation(out=sap, in_=pflat, func=AF.Copy), waits=[tok, sfree], sig=True)
                            setfree(t)
                            stfR.free[k] = ph.dma("sp", bgT_d[ci][:, tb * 1024:(tb + 1) * 1024], sap, f"pf{k}", waits=[t])
                        yield
                    return gen
                return evac_of

            tiles = []
            for j in range(2):
                tiles.append(dict(pieces=[wpiece(j * 512, 512)],
                                  units=mm_units(4, ev_simple(lambda cc, j=j: qTna_d[4 * j + cc], SCALE, "act"))))
            for j in range(2):
                tiles.append(dict(pieces=[wpiece(1024 + j * 512, 512)],
                                  units=mm_units(4, ev_simple(lambda cc, j=j: kTna_d[4 * j + cc], 1.0, "dve"))))
            for j in range(2):
                tiles.append(dict(pieces=[wpiece(3072 + j * 512, 512)],
                                  units=mm_units(4, ev_rope(lambda cc, j=j: qTg_d[4 * j + cc], 0))))
            tiles.append(dict(pieces=[wpiece(4096, 256)],
                              units=mm_units(2, ev_rope(lambda cc: kTg_send[cc * 128:(cc + 1) * 128, :], 1))))
            for ci in range(8):
                tiles.append(dict(pieces=[wpiece(4608 + ci * 128, 128, 0), wpiece(6656 + ci * 128, 128, 128), wpiece(5632 + ci * 128, 128, 256)],
                                  units=mm_units(3, ev_conv(ci))))
            gemm(ph, tiles, wring, "pw", psring, pe_waits=[hld_all])

            vring = Ring([vst[:, i, :] for i in range(3)])
            vps = Ring([psf[:, 0, :], psf[:, 1, :], psf[:, 2, :], psf[:, 3, :]])
            vps.free = [psring.free[0], psring.free[0], psring.free[1], psring.free[1]]
            for (c0, ncols, dst) in [(2048, 512, vna_d[:, 0:512]), (2560, 512, vna_d[:, 512:1024]), (4352, 256, vg_send)]:
                ws, wap, wfree = wring.get()
                wtk = ph.dma("pool", wap[:, :, 0:ncols], W[:, c0:c0 + ncols].rearrange("(k p) n -> p k n", p=128), f"pw{ws}", waits=[wfree])
                tok = None
                for t in range(16):
                    pk, pap, pfree = vps.get()
                    for k in range(KC):
                        tok = ph.op("pe", lambda e, o=pap[:, 0:ncols], l_=hT[:, k, t * 128:(t + 1) * 128], r=wap[:, k, 0:ncols], a=(k == 0), b=(k == KC - 1):
                                    e.matmul(o, lhsT=l_, rhs=r, start=a, stop=b), waits=[wtk, pfree] if k == 0 else (), sig=(k == KC - 1))
                    sk, sap, sfree = vring.get()
                    if t % 2 == 0:
                        ev = ph.op("act", lambda e, o=sap[:, 0:ncols], i=pap[:, 0:ncols]: e.activation(out=o, in_=i, func=AF.Copy), waits=[tok, sfree], sig=True)
                    else:
                        ev = ph.op("dve", lambda e, o=sap[:, 0:ncols], i=pap[:, 0:ncols]: e.tensor_copy(out=o, in_=i), waits=[tok, sfree], sig=True)
                    vps.free[pk] = ev
                    vring.free[sk] = ph.dma("sp", dst[t * 128:(t + 1) * 128, :], sap[:, 0:ncols], f"pv{sk}", waits=[ev])
                wring.free[ws] = tok
            ph.dma("sp", ub_d.ap().rearrange("(w p) c -> p w c", p=128), ub[:], "pub",
                   waits=[Tok(ctx.esem["dve"], "m_dve", ctx.ecnt["dve"])])
            ph.run()

    def phase_exchange():
        ph = Phase(ctx)
        a = ph.dma("sp", hk_send[0:1024, :].rearrange("(h d) t -> h d t", d=128), kTna_d[:, :, 0:256], "xh")
        a = ph.dma("sp", hk_send[1024:2048, :].rearrange("(h d) t -> h d t", d=128), kTna_d[:, :, NTOK - 256:NTOK], "xh")
        a = ph.dma("sp", hv_send[0:256, :], vna_d[0:256, :], "xh")
        a = ph.dma("sp", hv_send[256:512, :], vna_d[NTOK - 256:NTOK, :], "xh")
        c1 = a
        for i in range(4):
            c1 = ph.coll(send_t[i].ap(), recv_t[i].ap(), waits=[c1])
        ph.coll(ub_d.ap(), uball_d.ap(), waits=[c1])
        ph.run()

    def phase_conv(l):
        with ExitStack() as es2:
            T = lambda name, shape, dt: es2.enter_context(nc.sbuf_tensor(uname(name), list(shape), dt))
            ubs = T("c_ub", [128, 4, 2, 8], F32)
            prv = T("c_prv", [128, 8], F32)
            nxt = T("c_nxt", [128, 8], F32)
            uh = T("c_uh", [128, 2, NTOK + 2], F32)
            bg = T("c_bg", [128, 2, NTOK], F32)
            y = T("c_y", [128, 2, NTOK], F32)
            yb = T("c_yb", [128, 2, NTOK], BF16)
            ph = Phase(ctx)
            ld = ph.dma("sp", ubs[:], uball_d.ap().rearrange("(r w p) c -> p r w c", r=4, w=2), "cu")
            tk = None
            for r in range(4):
                if r == 0:
                    ph.op("dve", lambda e: e.tensor_scalar(prv[:], ubs[:, 0, 1, :], sel[:, 0:1], None, ALU.mult), waits=[ld])
                    tk = ph.op("dve", lambda e: e.tensor_scalar(nxt[:], ubs[:, 0, 0, :], sel[:, 4:5], None, ALU.mult), sig=True)
                else:
                    ph.op("dve", lambda e, r=r: e.scalar_tensor_tensor(out=prv[:], in0=ubs[:, r, 1, :], scalar=sel[:, r:r + 1], in1=prv[:], op0=ALU.mult, op1=ALU.add), waits=[tk])
                    tk = ph.op("dve", lambda e, r=r: e.scalar_tensor_tensor(out=nxt[:], in0=ubs[:, r, 0, :], scalar=sel[:, 4 + r:5 + r], in1=nxt[:], op0=ALU.mult, op1=ALU.add), sig=True)
            free = [None, None]
            for ci in range(8):
                s = ci % 2
                eng = "dve"
                l1 = ph.dma("sp", uh[:, s, 1:NTOK + 1], uT_d[ci], f"cl{s}", waits=[free[s]])
                l2 = ph.dma("sp", bg[:, s, :], bgT_d[ci], f"cl{s}", waits=[free[s]])
                cp = lambda k: convp[:, l, ci * 4 + k:ci * 4 + k + 1]
                a = ph.op("dve", lambda e: e.tensor_copy(out=uh[:, s, 0:1], in_=prv[:, ci:ci + 1]), waits=[tk, l2])
                a = ph.op("dve", lambda e: e.tensor_copy(out=uh[:, s, NTOK + 1:NTOK + 2], in_=nxt[:, ci:ci + 1]), sig=True)
                ph.op(eng, lambda e: e.tensor_scalar(y[:, s, :], uh[:, s, 1:NTOK + 1], cp(1), cp(3), ALU.mult, ALU.add), waits=[a, l2])
                b = ph.op(eng, lambda e: e.scalar_tensor_tensor(out=y[:, s, :], in0=uh[:, s, 0:NTOK], scalar=cp(0), in1=y[:, s, :], op0=ALU.mult, op1=ALU.add), sig=True)
                b = ph.op(eng, lambda e: e.scalar_tensor_tensor(out=y[:, s, :], in0=uh[:, s, 2:NTOK + 2], scalar=cp(2), in1=y[:, s, :], op0=ALU.mult, op1=ALU.add), waits=[b], sig=True)
                b = ph.op(eng, lambda e: e.tensor_tensor(out=yb[:, s, :], in0=y[:, s, :], in1=bg[:, s, :], op=ALU.mult), waits=[b], sig=True)
                free[s] = ph.dma("sp", brT_d[2, ci], yb[:, s, :], f"cs{s}", waits=[b])
            ph.run()

    def halo_select(ph, eng, dst, src_of, selbase, ld):
        tk = ph.op(eng, lambda e: e.tensor_scalar(dst, src_of(0), sel[:, selbase:selbase + 1], None, ALU.mult),
                   waits=[ld, Tok(ctx.esem["pool"], "m_pool", ctx.ecnt["pool"])], sig=True)
        for r in range(1, 4):
            tk = ph.op(eng, lambda e, r=r: e.scalar_tensor_tensor(out=dst, in0=src_of(r), scalar=sel[:, selbase + r:selbase + r + 1], in1=dst,
                                                                   op0=ALU.mult, op1=ALU.add), waits=[tk], sig=True)
        return tk

    def attn_finish(ph, o_ps, pe_tok, ob_ap, ob_free, tp_ap, tp_free, dst_ap, dst_free, evac_eng):
        ph.op("dve", lambda e: e.reciprocal(out=ob_ap["r"], in_=o_ps[:, 128:129]), waits=[pe_tok, ob_free])
        n = ph.op("dve", lambda e: e.tensor_scalar(ob_ap["o"], o_ps[:, 0:128], ob_ap["r"], None, ALU.mult), sig=True)
        t = ph.op("pe", lambda e: e.transpose(out=tp_ap, in_=ob_ap["o"], identity=ident[:]), waits=[n, tp_free], sig=True)
        if evac_eng == "act":
            c = ph.op("act", lambda e: e.activation(out=dst_ap, in_=tp_ap, func=AF.Copy), waits=[t, dst_free], sig=True)
        else:
            c = ph.op("dve", lambda e: e.tensor_copy(out=dst_ap, in_=tp_ap), waits=[t, dst_free], sig=True)
        return n, t, c

    def phase_na(l):
        with ExitStack() as es2:
            T = lambda name, shape, dt: es2.enter_context(nc.sbuf_tensor(uname(name), list(shape), dt))
            qT = T("a_qT", [128, 8, NTOK], BF16)
            Kb = T("a_K", [128, 8, 20 * 128], BF16)
            Vb = T("a_V", [128, 20, 8, 129], BF16)
            tmpk = T("a_tk", [128, 4, 8, 256], BF16)
            tmpv = T("a_tv", [128, 4, 2, 1024], BF16)
            bias = T("a_bias", [128, 3, 768], BF16)
            PT = T("a_PT", [128, 2, 768], BF16)
            ob = T("a_ob", [128, 2, 128], BF16)
            rr = T("a_rr", [128, 2, 1], F32)
            ost = T("a_ost", [128, 2, 8, 128], BF16)
            ph = Phase(ctx)
            ms = ph.op("pool", lambda e: e.memset(Vb[:], 1.0), sig=True)
            lq = ph.dma("sp", qT[:], qTna_d.rearrange("h d t -> d h t"), "aq")
            lk = ph.dma("sp", Kb[:, :, 256:256 + NTOK], kTna_d.rearrange("h d t -> d h t"), "aq")
            lv = None
            for h8 in range(8):
                lv = ph.dma("sp", Vb[:, 2:18, h8, 0:128],
                            vna_d[:, 128 * h8:128 * h8 + 128].rearrange("(c p) d -> p c d", p=128), "aq", waits=[ms])
            for (w, selb, kdst, vdst) in [(1, 0, Kb[:, :, 0:256], Vb[:, 0:2, :, 0:128]), (0, 4, Kb[:, :, 2304:2560], Vb[:, 18:20, :, 0:128])]:
                ldk = ldv = None
                for r in range(4):
                    hk_r = recv_view(2, r).rearrange("r (a c) -> (r a) c", c=256)
                    hv_r = recv_view(3, r).rearrange("r (a c) -> (r a) c", c=1024)
                    ldk = ph.dma("sp", tmpk[:, r], hk_r[w * 1024:(w + 1) * 1024, :].rearrange("(h d) t -> d h t", d=128), f"ah{w}",
                                 waits=[Tok(ctx.esem["dve"], "m_dve", ctx.ecnt["dve"]), Tok(ctx.esem["pool"], "m_pool", ctx.ecnt["pool"])])
                    ldv = ph.dma("sp", tmpv[:, r], hv_r[w * 256:(w + 1) * 256, :].rearrange("(c p) n -> p c n", p=128), f"ah{w}",
                                 waits=[Tok(ctx.esem["dve"], "m_dve", ctx.ecnt["dve"]), Tok(ctx.esem["pool"], "m_pool", ctx.ecnt["pool"])])
                halo_select(ph, "dve", kdst, lambda r: tmpk[:, r], selb, ldv)
                halo_select(ph, "dve", vdst, lambda r: tmpv[:, r].rearrange("p c (h d) -> p c h d", d=128), selb, ldv)
            ready = [lv, Tok(ctx.esem["dve"], "m_dve", ctx.ecnt["dve"]), Tok(ctx.esem["pool"], "m_pool", ctx.ecnt["pool"])]

            units = [(lp, h) for lp in range(16) for h in range(8)]
            N = len(units)
            slot_of = lambda lp: 0 if lp == 0 else 1 if lp == 1 else 3 if lp == 14 else 4 if lp == 15 else 2
            bfree = [None] * 3
            btok = [None] * N
            sfree = [None, None]
            ptfree = [None, None]
            ofree = [None, None]
            obfree = [None, None]
            tpfree = [None, None]
            ostfree = [None, None]
            qk_tok = [None] * N
            ex_tok = [None] * N
            pv_tok = [None] * N

            def load_bias(u):
                lp, h = units[u]
                btok[u] = ph.dma("sp", bias[:, u % 3, :], nab_d[l, slot_of(lp), h], f"ab{u % 3}", waits=[bfree[u % 3]])

            load_bias(0)
            load_bias(1)
            nrm_tok = [None] * N
            for step in range(N + 3):
                if step + 2 < N:
                    load_bias(step + 2)
                if step < N:
                    u = step
                    lp, h = units[u]
                    s = u % 2
                    wlo = max(lp - 1, 0)
                    tok = None
                    for j in range(6):
                        o = psf[:, 2 * s + j // 4, (j % 4) * 128:(j % 4 + 1) * 128]
                        ph.op("pe", lambda e, o=o, k=Kb[:, h, (wlo + j) * 128:(wlo + j + 1) * 128], q=qT[:, h, lp * 128:(lp + 1) * 128]:
                              e.matmul(o, lhsT=k, rhs=q, start=True, stop=False), waits=ready + [sfree[s], btok[u]] if j == 0 else ())
                        tok = ph.op("pe", lambda e, o=o, b_=bias[:, u % 3, j * 128:(j + 1) * 128]: e.matmul(o, lhsT=ident[:], rhs=b_, start=False, stop=True),
                                    sig=(j == 5))
                    qk_tok[u] = tok
                    bfree[u % 3] = tok
                    ph.op("act", lambda e, s=s: e.activation(out=PT[:, s, 0:512], in_=psf[:, 2 * s, :], func=AF.Exp), waits=[tok, ptfree[s]])
                    ex_tok[u] = ph.op("act", lambda e, s=s: e.activation(out=PT[:, s, 512:768], in_=psf[:, 2 * s + 1, 0:256], func=AF.Exp), sig=True)
                    sfree[s] = ex_tok[u]
                if 0 <= step - 1 < N:
                    u = step - 1
                    lp, h = units[u]
                    s = u % 2
                    wlo = max(lp - 1, 0)
                    tok = None
                    for j in range(6):
                        tok = ph.op("pe", lambda e, s=s, j=j, v=Vb[:, wlo + j, h, :]: e.matmul(psf[:, 4 + s, 0:129], lhsT=PT[:, s, j * 128:(j + 1) * 128], rhs=v,
                                                                                             start=(j == 0), stop=(j == 5)),
                                    waits=[ex_tok[u], ofree[s]] if j == 0 else (), sig=(j == 5))
                    pv_tok[u] = tok
                    ptfree[s] = tok
                    ph.op("dve", lambda e, s=s: e.reciprocal(out=rr[:, s, :], in_=psf[:, 4 + s, 128:129]), waits=[tok, obfree[s]])
                    nrm_tok[u] = ph.op("dve", lambda e, s=s: e.tensor_scalar(ob[:, s, :], psf[:, 4 + s, 0:128], rr[:, s, :], None, ALU.mult), sig=True)
                    ofree[s] = nrm_tok[u]
                if 0 <= step - 2 < N:
                    u = step - 2
                    lp, h = units[u]
                    s = u % 2
                    tp_ap = pst[s][:, 0:128]
                    t = ph.op("pe", lambda e, s=s, tp_ap=tp_ap: e.transpose(out=tp_ap, in_=ob[:, s, :], identity=ident[:]), waits=[nrm_tok[u], tpfree[s]], sig=True)
                    obfree[s] = t
                    dst_ap = ost[:, lp % 2, h, :]
                    dfree = ostfree[lp % 2] if h <= 1 else None
                    if u % 2 == 0:
                        c = ph.op("dve", lambda e, d=dst_ap, tp_ap=tp_ap: e.tensor_copy(out=d, in_=tp_ap), waits=[t, dfree], sig=True)
                    else:
                        c = ph.op("act", lambda e, d=dst_ap, tp_ap=tp_ap: e.activation(out=d, in_=tp_ap, func=AF.Copy), waits=[t, dfree], sig=True)
                    tpfree[s] = c
                    if h == 7:
                        cprev = Tok(ctx.esem["dve"], "m_dve", ctx.ecnt["dve"])
                        cact = Tok(ctx.esem["act"], "m_act", ctx.ecnt["act"])
                        ostfree[lp % 2] = ph.dma("sp", brT_d[0].rearrange("h d t -> d h t")[:, :, lp * 128:(lp + 1) * 128], ost[:, lp % 2], f"ao{lp % 2}",
                                                 waits=[c, cprev, cact])
            ph.run()

    def phase_gqa():
        with ExitStack() as es2:
            T = lambda name, shape, dt: es2.enter_context(nc.sbuf_tensor(uname(name), list(shape), dt))
            qT = T("g_qT", [128, 8, NTOK], BF16)
            KT = T("g_KT", [128, 4, 2, NTOK], BF16)
            Vb = T("g_V", [128, 64, 2, 129], BF16)
            PT = T("g_PT", [128, 3, 512], BF16)
            ob = T("g_ob", [128, 2, 128], BF16)
            rr = T("g_rr", [128, 2, 1], F32)
            ost = T("g_ost", [128, 2, 4, 128], BF16)
            ph = Phase(ctx)
            ms = ph.op("pool", lambda e: e.memset(Vb[:], 1.0), sig=True)
            lds = [ph.dma("sp", qT[:], qTg_d.rearrange("h d t -> d h t"), "gq")]
            for r in range(4):
                lds.append(ph.dma("sp", KT[:, r], recv_view(0, r).rearrange("(k d) t -> d k t", d=128), "gq"))
                vg_r = recv_view(1, r).rearrange("r (a c) -> (r a) c", c=256)
                for k2 in range(2):
                    lds.append(ph.dma("sp", Vb[:, 16 * r:16 * r + 16, k2, 0:128], vg_r[:, k2 * 128:(k2 + 1) * 128].rearrange("(c p) d -> p c d", p=128), "gq", waits=[ms]))
            ready = [lds[-1]]
            blocks = [(g, t, kc) for g in range(2) for t in range(16) for kc in range(64)]
            N = len(blocks)
            sfree = [None, None]
            ptfree = [None] * 3
            ofree = [None] * 4
            obfree = [None, None]
            tpfree = [None, None]
            ostfree = [None, None]
            qk_tok = [None] * N
            ex_tok = [None] * N
            fin = 0
            for step in range(N + 1):
                if step < N:
                    g, t, kc = blocks[step]
                    s = step % 2
                    qk_tok[step] = ph.op("pe", lambda e, s=s, k=KT[:, kc // 16, g, (kc % 16) * 128:(kc % 16 + 1) * 128], q=qT[:, 4 * g:4 * g + 4, t * 128:(t + 1) * 128]:
                                         e.matmul(psf[:, s, :], lhsT=k, rhs=q, start=True, stop=True), waits=ready + [sfree[s]], sig=True)
                    p = step % 3
                    ex_tok[step] = ph.op("act", lambda e, s=s, p=p: e.activation(out=PT[:, p, :], in_=psf[:, s, :], func=AF.Exp, scale=SCALE),
                                         waits=[qk_tok[step], ptfree[p]], sig=True)
                    sfree[s] = ex_tok[step]
                if step >= 1:
                    b = step - 1
                    g, t, kc = blocks[b]
                    p = b % 3
                    tok = None
                    for hh in range(4):
                        tok = ph.op("pe", lambda e, hh=hh, p=p, v=Vb[:, kc, g, :]: e.matmul(psf[:, 2 + hh, 0:129], lhsT=PT[:, p, hh * 128:(hh + 1) * 128], rhs=v,
                                                                                          start=(kc == 0), stop=(kc == 63)),
                                    waits=[ex_tok[b]] + ([ofree[hh]] if kc == 0 else []) if hh == 0 or kc == 0 else (), sig=(hh == 3))
                    ptfree[p] = tok
                    if kc == 63:
                        gi = g * 16 + t
                        for hh in range(4):
                            s2 = fin % 2
                            fin += 1
                            obd = dict(o=ob[:, s2, :], r=rr[:, s2, :])
                            n, tt, c = attn_finish(ph, psf[:, 2 + hh, :], tok, obd, obfree[s2], pst[s2][:, 0:128], tpfree[s2],
                                                   ost[:, gi % 2, hh, :], ostfree[gi % 2] if hh == 0 else None, "dve")
                            ofree[hh] = n
                            obfree[s2] = tt
                            tpfree[s2] = c
                        ostfree[gi % 2] = ph.dma("sp", brT_d[1].rearrange("h d t -> d h t")[:, 4 * g:4 * g + 4, t * 128:(t + 1) * 128], ost[:, gi % 2], f"go{gi % 2}",
                                                 waits=[c])
            ph.run()

    def phase_merge(l):
        for half in range(2):
            with ExitStack() as es2:
                T = lambda name, shape, dt: es2.enter_context(nc.sbuf_tensor(uname(name), list(shape), dt))
                hT = T("m_hT", [128, KC, 1024], BF16)
                bT = T("m_bT", [128, 3, 8, 1024], BF16)
                wt = T("m_wt", [128, 2, 24, 256], BF16)
                sg = T("m_sg", [128, 2, 1024], F32)
                tmp = T("m_tmp", [128, 2, 1024], F32)
                macc = T("m_acc", [128, 2, 2, 1024], F32)
                mb = T("m_mb", [128, 2, 1024], BF16)
                ph = Phase(ctx)
                hs = slice(half * 1024, (half + 1) * 1024)
                ld = None
                for q in range(4):
                    ld = ph.dma("sp", hT[:, 4 * q:4 * q + 4, :], hT_d[4 * q:4 * q + 4, :, hs].rearrange("k p t -> p k t"), "mh")
                for b in range(3):
                    ld = ph.dma("sp", bT[:, b], brT_d[b, :, :, hs].rearrange("k p t -> p k t"), "mh")
                wring = Ring([wt[:, 0], wt[:, 1]])
                psring = Ring([psf[:, 0:2, :], psf[:, 2:4, :], psf[:, 4:6, :]])
                sgR = Ring([sg[:, 0, :], sg[:, 1, :]])
                tmpR = Ring([tmp[:, 0, :], tmp[:, 1, :]])
                mbR = Ring([mb[:, 0, :], mb[:, 1, :]])
                state = {}
                accfree = {}
                tiles = []
                for ct in range(8):
                    for b in range(3):
                        gc0 = 7680 + b * 2048 + ct * 256
                        pieces = [(lambda wap: wap[:, 0:16, :], w_in[l][:, gc0:gc0 + 256].rearrange("(k p) n -> p k n", p=128)),
                                  (lambda wap: wap[:, 16:24, :], w_br[l, b][:, ct * 256:(ct + 1) * 256].rearrange("(k p) n -> p k n", p=128))]
                        units = []
                        for cc in range(2):
                            def evG(pap, tok, setfree, ct=ct, b=b, cc=cc):
                                k, sap, sfree_ = sgR.get()
                                t = ph.op("act", lambda e: e.activation(out=sap, in_=pap.rearrange("p b t -> p (b t)"), func=AF.Sigmoid),
                                          waits=[tok, sfree_, ld], sig=True)
                                setfree(t)
                                state[(ct, b, cc)] = (k, sap, t)
                                yield

                            def evY(pap, tok, setfree, ct=ct, b=b, cc=cc):
                                k, sap, st = state[(ct, b, cc)]
                                pflat = pap.rearrange("p b t -> p (b t)")
                                acc = macc[:, ct % 2, cc, :]
                                if b == 0:
                                    t = ph.op("dve", lambda e: e.tensor_tensor(out=acc, in0=pflat, in1=sap, op=ALU.mult),
                                              waits=[tok, st, accfree.get((ct % 2, cc))], sig=True)
                                    setfree(t)
                                    sgR.free[k] = t
                                else:
                                    tk_, tap, tfree = tmpR.get()
                                    t = ph.op("dve", lambda e: e.tensor_tensor(out=tap, in0=pflat, in1=sap, op=ALU.mult),
                                              waits=[tok, st, tfree], sig=True)
                                    setfree(t)
                                    sgR.free[k] = t
                                    if b == 1:
                                        t2_ = ph.op("pool", lambda e: e.tensor_tensor(out=acc, in0=acc, in1=tap, op=ALU.add), waits=[t], sig=True)
                                        tmpR.free[tk_] = t2_
                                    else:
                                        mk, map_, mfree = mbR.get()
                                        t2_ = ph.op("pool", lambda e: e.tensor_tensor(out=map_, in0=acc, in1=tap, op=ALU.add), waits=[t, mfree], sig=True)
                                        tmpR.free[tk_] = t2_
                                        accfree[(ct % 2, cc)] = t2_
                                        mbR.free[mk] = ph.dma("sp", mT_d[ct * 2 + cc][:, hs], map_, f"mm{mk}", waits=[t2_])
                                yield

                            mmG = [(bk, [(lambda wap, k=k, cc=cc: wap[:, k, cc * 128:(cc + 1) * 128], hT[:, k, bk * 512:(bk + 1) * 512]) for k in range(16)])
                                   for bk in range(2)]
                            mmY = [(bk, [(lambda wap, k=k, cc=cc: wap[:, 16 + k, cc * 128:(cc + 1) * 128], bT[:, b, k, bk * 512:(bk + 1) * 512]) for k in range(8)])
                                   for bk in range(2)]
                            units.append(dict(mm=mmG, evac=evG))
                            units.append(dict(mm=mmY, evac=evY))
                        tiles.append(dict(pieces=pieces, units=units))
                gemm(ph, tiles, wring, "mw", psring, pe_waits=[ld])
                ph.run()

    def phase_proj_res(l, KCn, act_d, Wd, TB, gcol, Xsrc, Xdst, wcols):
        nb = TB // 512
        for blk in range(NTOK // TB):
            with ExitStack() as es2:
                T = lambda name, shape, dt: es2.enter_context(nc.sbuf_tensor(uname(name), list(shape), dt))
                aT = T("r_aT", [128, KCn, TB], BF16)
                wt = T("r_wt", [128, 2, KCn, wcols], BF16)
                z = T("r_z", [128, KC, TB], F32)
                sq = T("r_sq", [128, 2, TB], F32)
                rs = T("r_rs", [128, TB], F32)
                xt = T("r_xt", [128, 2, TB], F32)
                t1 = T("r_t1", [128, 2, TB], F32)
                xo = T("r_xo", [128, 2, TB], F32)
                ph = Phase(ctx)
                bs = slice(blk * TB, (blk + 1) * TB)
                ld = None
                step = 4 if KCn % 4 == 0 else 1
                for q in range(0, KCn, step):
                    ld = ph.dma("sp", aT[:, q:q + step, :], act_d[q:q + step, :, bs].rearrange("k p t -> p k t"), "ra")
                wring = Ring([wt[:, 0], wt[:, 1]])
                psring = Ring([psf[:, 0:nb, :], psf[:, 2:2 + nb, :]])
                sqR = Ring([sq[:, 0, :], sq[:, 1, :]])
                ssfree = [None]
                sstok = [None]
                tiles = []
                cpt = wcols // 128
                for ct in range(D // wcols):
                    units = []
                    for cc in range(cpt):
                        c = ct * cpt + cc

                        def ev(pap, tok, setfree, c=c):
                            pflat = pap.rearrange("p b t -> p (b t)")
                            k, sap, sfree_ = sqR.get()
                            a2 = ph.op("dve", lambda e: e.tensor_copy(out=z[:, c, :], in_=pflat), waits=[tok], sig=True)
                            a1 = ph.op("act", lambda e: e.activation(out=sap, in_=z[:, c, :], func=AF.Square), waits=[a2, sfree_, ld], sig=True)
                            setfree(a2)
                            psring.free[(psring.i - 1) % 2] = a2
                            yield
                            tk = None
                            for bk in range(nb):
                                tk = ph.op("pe", lambda e, bk=bk: e.matmul(psf[:, 4 + bk, :], lhsT=onesD[:], rhs=sap[:, bk * 512:(bk + 1) * 512],
                                                                         start=(c == 0), stop=(c == KC - 1)), waits=[a1, a2], sig=(bk == nb - 1))
                            sqR.free[k] = tk
                            sstok[0] = tk
                            yield

                        mm = [(bk, [(lambda wap, k=k, cc=cc: wap[:, k, cc * 128:(cc + 1) * 128], aT[:, k, bk * 512:(bk + 1) * 512]) for k in range(KCn)])
                              for bk in range(nb)]
                        units.append(dict(mm=mm, evac=ev))
                    tiles.append(dict(pieces=[(lambda wap: wap, Wd[:, ct * wcols:(ct + 1) * wcols].rearrange("(k p) n -> p k n", p=128))], units=units))
                gemm(ph, tiles, wring, "rw", psring, pe_waits=[ld])
                ra, r = rstd_ops(ph, rs[:], psf[:, 4:4 + nb, :].rearrange("p b t -> p (b t)"), [sstok[0]])
                xfree = [None, None]
                ofree = [None, None]
                for c in range(KC):
                    s = c % 2
                    lx = ph.dma("sp", xt[:, s, :], Xsrc[c][:, bs], f"rx{s}", waits=[xfree[s]])
                    a = ph.op("dve", lambda e, c=c, s=s: e.scalar_tensor_tensor(out=t1[:, s, :], in0=z[:, c, :], scalar=gains[:, l, gcol * KC + c:gcol * KC + c + 1],
                                                                              in1=rs[:], op0=ALU.mult, op1=ALU.mult), waits=[r, xfree[s]], sig=True)
                    b = ph.op("pool", lambda e, s=s: e.tensor_tensor(out=xo[:, s, :], in0=t1[:, s, :], in1=xt[:, s, :], op=ALU.add), waits=[a, lx, ofree[s]], sig=True)
                    xfree[s] = b
                    ofree[s] = ph.dma("sp", Xdst[c][:, bs], xo[:, s, :], f"ro{s}", waits=[b])
                ph.run()

    def phase_ffn_up(l):
        with ExitStack() as es2:
            T = lambda name, shape, dt: es2.enter_context(nc.sbuf_tensor(uname(name), list(shape), dt))
            hT = T("f_hT", [128, KC, NTOK], BF16)
            wt = T("f_wt", [128, 2, 32, 256], BF16)
            sl = T("f_sl", [128, 2, 1024], F32)
            ab = T("f_ab", [128, 3, 1024], BF16)
            ph = Phase(ctx)
            ld = None
            for q in range(4):
                ld = ph.dma("sp", hT[:, 4 * q:4 * q + 4, :], hT_d[4 * q:4 * q + 4].rearrange("k p t -> p k t"), "fh")
            wring = Ring([wt[:, 0], wt[:, 1]])
            psring = Ring([psf[:, 0:2, :], psf[:, 2:4, :], psf[:, 4:6, :]])
            slR = Ring([sl[:, 0, :], sl[:, 1, :]])
            abR = Ring([ab[:, i, :] for i in range(3)])
            state = {}
            tiles = []
            for jt in range(FF // 256):
                pieces = [(lambda wap: wap[:, 0:16, :], w_gate[l][:, jt * 256:(jt + 1) * 256].rearrange("(k p) n -> p k n", p=128)),
                          (lambda wap: wap[:, 16:32, :], w_up[l][:, jt * 256:(jt + 1) * 256].rearrange("(k p) n -> p k n", p=128))]
                units = []
                for tb in range(2):
                    for cc in range(2):
                        j = jt * 2 + cc

                        def evG(pap, tok, setfree, key=(jt, tb, cc)):
                            k, sap, sfree_ = slR.get()
                            t = ph.op("act", lambda e: e.activation(out=sap, in_=pap.rearrange("p b t -> p (b t)"), func=AF.Silu), waits=[tok, sfree_, ld], sig=True)
                            setfree(t)
                            state[key] = (k, sap, t)
                            yield

                        def evU(pap, tok, setfree, key=(jt, tb, cc), j=j, tb=tb):
                            k, sap, st = state[key]
                            ak, aap, afree = abR.get()
                            t = ph.op("dve", lambda e: e.tensor_tensor(out=aap, in0=pap.rearrange("p b t -> p (b t)"), in1=sap, op=ALU.mult),
                                      waits=[tok, st, afree], sig=True)
                            setfree(t)
                            slR.free[k] = t
                            abR.free[ak] = ph.dma("sp", aT_d[j][:, tb * 1024:(tb + 1) * 1024], aap, f"fa{ak}", waits=[t])
                            yield

                        mmG = [(bk, [(lambda wap, k=k, cc=cc: wap[:, k, cc * 128:(cc + 1) * 128], hT[:, k, tb * 1024 + bk * 512:tb * 1024 + (bk + 1) * 512])
                                     for k in range(16)]) for bk in range(2)]
                        mmU = [(bk, [(lambda wap, k=k, cc=cc: wap[:, 16 + k, cc * 128:(cc + 1) * 128], hT[:, k, tb * 1024 + bk * 512:tb * 1024 + (bk + 1) * 512])
                                     for k in range(16)]) for bk in range(2)]
                        units.append(dict(mm=mmG, evac=evG))
                        units.append(dict(mm=mmU, evac=evU))
                tiles.append(dict(pieces=pieces, units=units))
            gemm(ph, tiles, wring, "fw", psring, pe_waits=[ld])
            ph.run()

    for l in range(L):
        X = xT_in if l == 0 else xres
        Xout = yT if l == L - 1 else xres
        plist = [
            lambda: phase_norm(X, gains[:, l, 0:KC]),
            lambda: phase_proj(l),
            lambda: phase_exchange(),
            lambda: phase_conv(l),
            lambda: phase_na(l),
            lambda: phase_gqa(),
            lambda: phase_merge(l),
            lambda: phase_proj_res(l, KC, mT_d, w_out[l], 1024, 1, X, xmid, 256),
            lambda: phase_norm(xmid, gains[:, l, 2 * KC:3 * KC]),
            lambda: phase_ffn_up(l),
            lambda: phase_proj_res(l, FC, aT_d, w_down[l], 512, 3, xmid, Xout, 128),
        ]
        for pi, pf in enumerate(plist):
            if (stop_after is None or pi < stop_after) and pi not in skip:
                pf()
    es.close()
    return nc


def _fmajor_vec(v):
    return np.ascontiguousarray(v.reshape(-1, 128).T)


def _na_bias_tables(rpb, qtr):
    out = np.full((5, 8, 128, 768), NEG, dtype=np.float32)
    kk = np.arange(128)
    qq = np.arange(128)
    for si, lp in enumerate([0, 1, 2, 14, 15]):
        p = qtr * 16 + lp
        wlo = max(lp - 1, 0)
        g0 = qtr * 16 - 2 + wlo
        qrow = 2 * p + qq // 64
        qcol = qq % 64
        rstart = np.clip(qrow - 4, 0, 120)
        cstart = np.clip(qcol - 8, 0, 48)
        for j in range(6):
            gc = g0 + j
            if gc < 0 or gc > 63:
                continue
            krow = 2 * gc + kk // 64
            kcol = kk % 64
            dr = krow[:, None] - qrow[None, :]
            dc = kcol[:, None] - qcol[None, :]
            valid = ((krow[:, None] >= rstart[None, :]) & (krow[:, None] < rstart[None, :] + 8)
                     & (kcol[:, None] >= cstart[None, :]) & (kcol[:, None] < cstart[None, :] + 16))
            ri = np.clip(dr + 7, 0, 14)
            ci = np.clip(dc + 15, 0, 30)
            vals = rpb[:, ri, ci]
            out[si, :, :, j * 128:(j + 1) * 128] = np.where(valid[None], vals, NEG)
    return out.astype(NPBF)


def _rope_tables(qtr):
    t = qtr * NTOK + np.arange(NTOK)
    prow = (t // 64).astype(np.float32)
    pcol = (t % 64).astype(np.float32)
    inv = (1.0 / (10000.0 ** (np.arange(32, dtype=np.float32) / 32))).astype(np.float32)
    C = np.zeros((128, NTOK), np.float32)
    Sn = np.zeros((128, NTOK), np.float32)
    for d in range(128):
        pos = prow if d < 64 else pcol
        ang = pos * inv[d % 32]
        C[d] = np.cos(ang)
        Sn[d] = np.sin(ang)
    return C, Sn


def _rot_lhsT():
    Pm = np.zeros((128, 128), np.float32)
    for i in range(128):
        if (i % 64) < 32:
            Pm[i, i + 32] = -1.0
        else:
            Pm[i, i - 32] = 1.0
    return np.ascontiguousarray(Pm.T)


_CACHE = {}


def _prep_common(inputs, layers):
    L = len(layers)
    sl = lambda k: np.ascontiguousarray(inputs[k][layers])
    com = {
        "w_in": sl("w_in"),
        "w_br": np.ascontiguousarray(np.stack([inputs["w_br_na"][layers], inputs["w_br_gqa"][layers], inputs["w_br_conv"][layers]], axis=1)),
        "w_out": sl("w_out"), "w_gate": sl("w_ffn_gate"), "w_up": sl("w_ffn_up"), "w_down": sl("w_ffn_down"),
        "ident": np.eye(128, dtype=np.float32).astype(NPBF),
        "rotT": _rot_lhsT(),
    }
    gains = np.zeros((L, 128, 64), np.float32)
    qkg = np.zeros((L, 128, 2), np.float32)
    convp = np.zeros((L, 128, 32), np.float32)
    for i, l in enumerate(layers):
        for gi, k in enumerate(["pre_mix_g", "post_mix_g", "pre_ffn_g", "post_ffn_g"]):
            gains[i, :, gi * 16:(gi + 1) * 16] = _fmajor_vec(inputs[k][l])
        qkg[i, :, 0] = inputs["q_norm_g"][l]
        qkg[i, :, 1] = inputs["k_norm_g"][l]
        cw = inputs["conv_w"][l]
        cb = inputs["conv_b"][l]
        for ci in range(8):
            for k in range(3):
                convp[i, :, ci * 4 + k] = cw[k, ci * 128:(ci + 1) * 128]
            convp[i, :, ci * 4 + 3] = cb[ci * 128:(ci + 1) * 128]
    com["gains"], com["qkg"], com["convp"] = gains, qkg, convp
    return com


def _prep_core(inputs, layers, c):
    qtr = c % 4
    C, Sn = _rope_tables(qtr)
    sel = np.zeros((128, 8), np.float32)
    if qtr > 0:
        sel[:, qtr - 1] = 1.0
    if qtr < 3:
        sel[:, 4 + qtr + 1] = 1.0
    nab = np.stack([_na_bias_tables(np.asarray(inputs["na_rpb"][l]), qtr) for l in layers], axis=0)
    return {"ropeC": C, "ropeS": Sn, "sel": sel, "nab": nab}


def _x_to_fmajor(x, c):
    b, qtr = c // 4, c % 4
    xs = x[b, qtr * NTOK:(qtr + 1) * NTOK, :]
    return np.ascontiguousarray(xs.T.reshape(KC, 128, NTOK))


def _run(xTs, inputs, layers):
    L = len(layers)
    if L not in _CACHE:
        _CACHE[L] = build(L)
    nc = _CACHE[L]
    com = _prep_common(inputs, layers)
    in_maps = []
    for c in range(NCORE):
        m = dict(com)
        m.update(_prep_core(inputs, layers, c))
        m["xT"] = xTs[c]
        in_maps.append(m)
    res = run_bass_kernel_spmd(nc, in_maps, core_ids=list(range(NCORE)))
    return [r["yT"] for r in res.results]


LAYERS_PER_LAUNCH = 4


def kernel(**inputs):
    inputs = {k: np.asarray(v) for k, v in inputs.items()}
    x = inputs["x"]
    xTs = [_x_to_fmajor(x, c) for c in range(NCORE)]
    for l0 in range(0, DEPTH, LAYERS_PER_LAUNCH):
        xTs = _run(xTs, inputs, list(range(l0, l0 + LAYERS_PER_LAUNCH)))
    out = np.zeros((2, 4 * NTOK, D), np.float32)
    for c in range(NCORE):
        b, qtr = c // 4, c % 4
        out[b, qtr * NTOK:(qtr + 1) * NTOK, :] = xTs[c].reshape(D, NTOK).T
    return out
```

```python
import numpy as np
import ml_dtypes
from contextlib import ExitStack
import concourse.bass as bass
import concourse.mybir as mybir
from concourse.bass_utils import run_bass_kernel_spmd

F32 = mybir.dt.float32
BF16 = mybir.dt.bfloat16
AF = mybir.ActivationFunctionType
ALU = mybir.AluOpType
NPBF = ml_dtypes.bfloat16

NCORE = 8
NTOK = 2048
D = 2048
KC = 16
FF = 5632
FC = 44
INC = 13824
SCALE = 128.0 ** -0.5
EPS = 1e-6
NEG = -30000.0
ENGS = ["pe", "act", "dve", "pool", "sp"]
DEPTH = 4


class Tok:
    __slots__ = ("sem", "key", "val")

    def __init__(self, sem, key, val):
        self.sem, self.key, self.val = sem, key, val


class _Rec:
    def __getattr__(self, name):
        return lambda *a, **k: (name, a, k)


_REC = _Rec()


class Ctx:
    def __init__(self, nc, es):
        self.nc, self.es = nc, es
        self.esem = {e: es.enter_context(nc.semaphore("m_" + e)) for e in ["pe", "act", "dve", "pool"]}
        self.ecnt = {e: 0 for e in self.esem}
        self.dsem = {}
        self.bar = es.enter_context(nc.semaphore("bar"))
        self.barcnt = 0
        self.seen = {e: {} for e in ENGS}
        self.scr = es.enter_context(nc.sbuf_tensor("scr", [128, 8], F32))
        self.first = True

    def dma_sem(self, name):
        if name not in self.dsem:
            self.dsem[name] = [self.es.enter_context(self.nc.semaphore("d_" + name)), 0]
        return self.dsem[name]


class Phase:
    def __init__(self, ctx):
        self.c = ctx
        self.ops = {e: [] for e in ENGS}
        self.dtoks = {e: {} for e in ENGS}

    def op(self, eng, fn, waits=(), sig=False, chain=True):
        tok = None
        waits = list(waits)
        if eng != "pe":
            sig = True
            if chain and self.c.ecnt[eng] > 0:
                waits.append(Tok(self.c.esem[eng], "m_" + eng, self.c.ecnt[eng]))
        if sig:
            self.c.ecnt[eng] += 1
            tok = Tok(self.c.esem[eng], "m_" + eng, self.c.ecnt[eng])
        name, a, k = fn(_REC)
        self.ops[eng].append((lambda e, name=name, a=a, k=k: getattr(e, name)(*a, **k),
                              tuple(w for w in waits if w is not None), tok, 1))
        return tok

    def dma(self, eng, out, in_, sem, waits=()):
        s = self.c.dma_sem(sem)
        s[1] += 16
        tok = Tok(s[0], "d_" + sem, s[1])
        self.ops[eng].append((lambda e, o=out, i=in_: e.dma_start(out=o, in_=i),
                              tuple(w for w in waits if w is not None), tok, 16))
        self.dtoks[eng][tok.key] = tok
        return tok

    def coll(self, ins, outs, waits=()):
        s = self.c.dma_sem("cc")
        s[1] += 1
        tok = Tok(s[0], "d_cc", s[1])
        fn = lambda e, i=ins, o=outs: e.collective_compute(
            "AllGather", ALU.bypass, replica_groups=[[0, 1, 2, 3], [4, 5, 6, 7]], ins=[i], outs=[o])
        self.ops["pool"].append((fn, tuple(w for w in waits if w is not None), tok, 1))
        self.dtoks["pool"][tok.key] = tok
        return tok

    def run(self):
        c = self.c
        nc = c.nc
        c.barcnt += 4
        barv = c.barcnt

        def emit(eng, e):
            seen = c.seen[eng]

            def wait(t):
                if seen.get(t.key, 0) < t.val:
                    e.wait_ge(t.sem, t.val)
                    seen[t.key] = t.val

            for fn, waits, tok, inc in self.ops[eng]:
                for w in waits:
                    wait(w)
                ins = fn(e)
                if tok is not None:
                    ins.then_inc(tok.sem, inc)
            for t in self.dtoks[eng].values():
                wait(t)
            if eng == "sp":
                e.sem_inc(c.bar, 1)
            elif eng == "act":
                e.memzero(c.scr[:, 0:1]).then_inc(c.bar, 1)
            elif eng == "dve":
                e.memset(c.scr[:, 1:2], 0.0).then_inc(c.bar, 1)
            elif eng == "pool":
                e.memset(c.scr[:, 2:3], 0.0).then_inc(c.bar, 1)
            e.wait_ge(c.bar, barv)

        with nc.Block() as block:
            @block.tensor
            def _(e):
                emit("pe", e)

            @block.scalar
            def _(e):
                emit("act", e)

            @block.vector
            def _(e):
                emit("dve", e)

            @block.gpsimd
            def _(e):
                emit("pool", e)

            @block.sync
            def _(e):
                emit("sp", e)


class Ring:
    def __init__(self, aps):
        self.aps = list(aps)
        self.free = [None] * len(self.aps)
        self.i = 0

    def get(self):
        k = self.i % len(self.aps)
        self.i += 1
        return k, self.aps[k], self.free[k]


def gemm(ph, tiles, wring, wsem, psring, pe_waits=()):
    active = []

    def advance():
        for g in list(active):
            try:
                next(g)
            except StopIteration:
                active.remove(g)

    tiles = list(tiles)

    def issue(tile):
        ws, wap, wfree = wring.get()
        wt = None
        for dst_fn, src in tile["pieces"]:
            wt = ph.dma("pool", dst_fn(wap), src, f"{wsem}{ws}", waits=[wfree])
        return ws, wap, wt

    pending = issue(tiles[0]) if tiles else None
    for ti, tile in enumerate(tiles):
        ws, wap, wt = pending
        if ti + 1 < len(tiles):
            pending = issue(tiles[ti + 1])
        units = tile["units"]
        last_tok = None
        for ui, unit in enumerate(units):
            pk, pap, pfree = psring.get()
            mms = unit["mm"]
            tok = None
            nmm = sum(len(kl) for _, kl in mms)
            cnt = 0
            for bank, klist in mms:
                for ki, (lhs_fn, rhs) in enumerate(klist):
                    cnt += 1
                    lastmm = cnt == nmm
                    tok = ph.op("pe", lambda e, o=pap[:, bank, 0:rhs.shape[-1] if len(rhs.shape) == 2 else 512], l=lhs_fn(wap), r=rhs,
                                a=(ki == 0), b=(ki == len(klist) - 1): e.matmul(o, lhsT=l, rhs=r, start=a, stop=b),
                                waits=[wt, pfree] + list(pe_waits) if cnt == 1 else (), sig=lastmm)
            last_tok = tok

            def setfree(t, k=pk):
                psring.free[k] = t

            advance()
            g = unit["evac"](pap, tok, setfree)
            active.append(g)
            try:
                next(g)
            except StopIteration:
                active.remove(g)
        wring.free[ws] = last_tok
    while active:
        advance()


def build(L, dbg=False, stop_after=None, skip=()):
    nc = bass.Bass("TRN2", target_bir_lowering=False)
    es = ExitStack()
    uid = [0]

    def uname(n):
        uid[0] += 1
        return f"{n}_{uid[0]}"

    def din(name, shape, dt=F32):
        return nc.dram_tensor(name, list(shape), dt, kind="ExternalInput").ap()

    def dscr(name, shape, dt):
        if dbg and name in ("brT_d", "mT_d", "xmid", "qTg_d", "qTna_d", "kTna_d", "vna_d", "hT_d", "uT_d", "bgT_d"):
            return nc.dram_tensor(name, list(shape), dt, kind="ExternalOutput").ap()
        return nc.dram_tensor(name, list(shape), dt).ap()

    xT_in = din("xT", [KC, 128, NTOK])
    w_in = din("w_in", [L, D, INC])
    w_br = din("w_br", [L, 3, 1024, D])
    w_out = din("w_out", [L, D, D])
    small_ffn = stop_after is not None and stop_after <= 9
    w_gate = din("w_gate", [L, 128, 128] if small_ffn else [L, D, FF])
    w_up = din("w_up", [L, 128, 128] if small_ffn else [L, D, FF])
    w_down = din("w_down", [L, 128, 128] if small_ffn else [L, FF, D])
    gains_d = din("gains", [L, 128, 4 * KC])
    qkg_d = din("qkg", [L, 128, 2])
    convp_d = din("convp", [L, 128, 32])
    nab_d = din("nab", [L, 5, 8, 128, 768], BF16)
    ropeC_d = din("ropeC", [128, NTOK])
    ropeS_d = din("ropeS", [128, NTOK])
    ident_d = din("ident", [128, 128], BF16)
    rotT_d = din("rotT", [128, 128])
    sel_d = din("sel", [128, 8])
    yT = nc.dram_tensor("yT", [KC, 128, NTOK], F32, kind="ExternalOutput").ap()

    xmid = dscr("xmid", [KC, 128, NTOK], F32)
    xres = dscr("xres", [KC, 128, NTOK], F32)
    hT_d = dscr("hT_d", [KC, 128, NTOK], BF16)
    qTna_d = dscr("qTna_d", [8, 128, NTOK], BF16)
    kTna_d = dscr("kTna_d", [8, 128, NTOK], BF16)
    vna_d = dscr("vna_d", [NTOK, 1024], BF16)
    qTg_d = dscr("qTg_d", [8, 128, NTOK], BF16)
    send_t = [nc.dram_tensor(f"send{i}_d", [256, 2048], BF16) for i in range(4)]
    recv_t = [nc.dram_tensor(f"recv{i}_d", [1024, 2048], BF16) for i in range(4)]
    ub_d = nc.dram_tensor("ub_d", [256, 8], F32)
    uball_d = nc.dram_tensor("uball_d", [1024, 8], F32)
    uT_d = dscr("uT_d", [8, 128, NTOK], F32)
    bgT_d = dscr("bgT_d", [8, 128, NTOK], F32)
    brT_d = dscr("brT_d", [3, 8, 128, NTOK], BF16)
    mT_d = dscr("mT_d", [KC, 128, NTOK], BF16)
    aT_d = dscr("aT_d", [FC, 128, NTOK], BF16)
    kTg_send = send_t[0].ap()
    vg_send = send_t[1].ap().rearrange("r (a c) -> (r a) c", c=256)
    hk_send = send_t[2].ap().rearrange("r (a c) -> (r a) c", c=256)
    hv_send = send_t[3].ap().rearrange("r (a c) -> (r a) c", c=1024)

    def recv_view(i, r):
        return recv_t[i].ap()[r * 256:(r + 1) * 256, :]

    ctx = Ctx(nc, es)
    S = lambda name, shape, dt: es.enter_context(nc.sbuf_tensor("s_" + name, list(shape), dt))
    psf = es.enter_context(nc.psum_tensor("psf", [128, 6, 512], F32))
    pst = [es.enter_context(nc.psum_tensor("pst0", [128, 1024], BF16)), es.enter_context(nc.psum_tensor("pst1", [128, 1024], BF16))]

    ident = S("ident", [128, 128], BF16)
    rotT = S("rotT", [128, 128], F32)
    onesD = S("onesD", [128, 128], F32)
    onesH = S("onesH", [128, 128], F32)
    sel = S("sel", [128, 8], F32)
    gains = S("gains", [128, L, 4 * KC], F32)
    qkg = S("qkg", [128, L, 2], F32)
    convp = S("convp", [128, L, 32], F32)
    epsT = S("epsT", [128, 1], F32)

    ph = Phase(ctx)
    t0 = ph.dma("sp", ident[:], ident_d, "c0")
    ph.dma("sp", rotT[:], rotT_d, "c0")
    ph.dma("sp", sel[:], sel_d, "c0")
    ph.dma("sp", gains[:], gains_d.rearrange("l p k -> p l k"), "c0")
    ph.dma("sp", qkg[:], qkg_d.rearrange("l p k -> p l k"), "c0")
    ph.dma("sp", convp[:], convp_d.rearrange("l p k -> p l k"), "c0")
    ph.op("pool", lambda e: e.memset(onesD[:], 1.0 / D))
    ph.op("pool", lambda e: e.memset(onesH[:], 1.0 / 128))
    ph.op("pool", lambda e: e.memset(epsT[:], EPS))
    ph.run()

    def rstd_ops(ph, out_ap, in_ap, waits):
        a = ph.op("act", lambda e: e.activation(out=out_ap, in_=in_ap, func=AF.Sqrt, bias=epsT[:, 0:1], scale=1.0), waits=waits, sig=True)
        r = ph.op("dve", lambda e: e.reciprocal(out=out_ap, in_=out_ap), waits=[a], sig=True)
        return a, r

    def phase_norm(X, g_ap):
        with ExitStack() as es2:
            T = lambda name, shape, dt: es2.enter_context(nc.sbuf_tensor(uname(name), list(shape), dt))
            xt = T("n_xt", [128, 2, KC, 512], F32)
            sq = T("n_sq", [128, 2, 4, 512], F32)
            rs = T("n_rs", [128, 2, 512], F32)
            hb = T("n_hb", [128, 2, KC, 512], BF16)
            ph = Phase(ctx)
            xfree = [None, None]
            sqfree = [None, None]
            hbfree = [None, None]
            rsfree = [None, None]
            psfree = [None, None]
            for tg in range(4):
                s = tg % 2
                ts = slice(tg * 512, (tg + 1) * 512)
                ld = []
                for q in range(4):
                    ld.append(ph.dma("sp", xt[:, s, 4 * q:4 * q + 4, :], X[4 * q:4 * q + 4, :, ts].rearrange("k p t -> p k t"),
                                     f"nx{s}{q}", waits=[xfree[s]]))
                mmtok = None
                for q in range(4):
                    qs = (tg * 4 + q) % 2
                    a = ph.op("act", lambda e, o=sq[:, qs], i=xt[:, s, 4 * q:4 * q + 4, :]: e.activation(out=o, in_=i, func=AF.Square),
                              waits=[ld[q], sqfree[qs]], sig=True)
                    for k in range(4):
                        kc = 4 * q + k
                        mmtok = ph.op("pe", lambda e, o=psf[:, s, :], r=sq[:, qs, k, :], st=(kc == 0), sp=(kc == KC - 1):
                                      e.matmul(o, lhsT=onesD[:], rhs=r, start=st, stop=sp),
                                      waits=[a, psfree[s]] if k == 0 else (), sig=(k == 3))
                    sqfree[qs] = mmtok
                ra, r = rstd_ops(ph, rs[:, s, :], psf[:, s, :], [mmtok, rsfree[s]])
                psfree[s] = ra
                last = {}
                for kc in range(KC):
                    eng = "dve"
                    last[eng] = ph.op(eng, lambda e, o=hb[:, s, kc, :], i=xt[:, s, kc, :], g=g_ap[:, kc:kc + 1], rr=rs[:, s, :]:
                                      e.scalar_tensor_tensor(out=o, in0=i, scalar=g, in1=rr, op0=ALU.mult, op1=ALU.mult),
                                      waits=[r, hbfree[s], ld[3]] if kc < 2 else (), sig=(kc >= KC - 1))
                xfree[s] = None
                st = ph.dma("sp", hT_d[:, :, ts].rearrange("k p t -> p k t"), hb[:, s], f"nh{s}", waits=[last["dve"]])
                hbfree[s] = st
                rsfree[s] = st
                xfree[s] = st
            ph.run()

    def phase_proj(l):
        with ExitStack() as es2:
            T = lambda name, shape, dt: es2.enter_context(nc.sbuf_tensor(uname(name), list(shape), dt))
            hT = T("p_hT", [128, KC, NTOK], BF16)
            rC = T("p_rC", [128, NTOK], F32)
            rS = T("p_rS", [128, NTOK], F32)
            wt = T("p_wt", [128, 2, KC, 512], BF16)
            stg = T("p_stg", [128, 3, 1024], BF16)
            stf = T("p_stf", [128, 3, 1024], F32)
            htmp = T("p_htmp", [128, 2, 1024], F32)
            sqb = T("p_sqb", [128, 2, 512], F32)
            rb = T("p_rb", [128, 2, 512], F32)
            qn = T("p_qn", [128, 2, 512], F32)
            t1 = T("p_t1", [128, 2, 512], F32)
            t2 = T("p_t2", [128, 2, 512], F32)
            ub = T("p_ub", [128, 2, 8], F32)
            vst = T("p_vst", [128, 3, 512], BF16)
            ph = Phase(ctx)
            hld = []
            for q in range(4):
                hld.append(ph.dma("sp", hT[:, 4 * q:4 * q + 4, :], hT_d[4 * q:4 * q + 4].rearrange("k p t -> p k t"), "ph"))
            ph.dma("sp", rC[:], ropeC_d, "ph")
            hld_all = ph.dma("sp", rS[:], ropeS_d, "ph")
            wring = Ring([wt[:, 0], wt[:, 1]])
            psring = Ring([psf[:, 0:2, :], psf[:, 2:4, :]])
            stgR = Ring([stg[:, i, :] for i in range(3)])
            stfR = Ring([stf[:, i, :] for i in range(3)])
            auxA = [None]
            auxB = [None]
            cnt = [0]
            pmul = [None, None]
            padd = [None, None]
            W = w_in[l]

            def wpiece(c0, ncols, dst0=0):
                return (lambda wap, d=dst0, n=ncols: wap[:, :, d:d + n],
                        W[:, c0:c0 + ncols].rearrange("(k p) n -> p k n", p=128))

            def mm_units(ncc, evac_of):
                units = []
                for cc in range(ncc):
                    for tb in range(2):
                        mm = []
                        for b in range(2):
                            ts = slice(tb * 1024 + b * 512, tb * 1024 + (b + 1) * 512)
                            mm.append((b, [(lambda wap, k=k, cc=cc: wap[:, k, cc * 128:(cc + 1) * 128], hT[:, k, ts]) for k in range(KC)]))
                        units.append(dict(mm=mm, evac=evac_of(cc, tb)))
                return units

            def ev_simple(dst_of, scale, eng):
                def evac_of(cc, tb):
                    def gen(pap, tok, setfree):
                        k, sap, sfree = stgR.get()
                        if eng == "act":
                            t = ph.op("act", lambda e: e.activation(out=sap, in_=pap.rearrange("p b t -> p (b t)"), func=AF.Copy, scale=scale),
                                      waits=[tok, sfree, hld_all], sig=True)
                        else:
                            t = ph.op("dve", lambda e: e.tensor_copy(out=sap, in_=pap.rearrange("p b t -> p (b t)")),
                                      waits=[tok, sfree, hld_all], sig=True)
                        setfree(t)
                        stgR.free[k] = ph.dma("sp", dst_of(cc)[:, tb * 1024:(tb + 1) * 1024], sap, f"ps{k}", waits=[t])
                        yield
                    return gen
                return evac_of

            def ev_rope(dst_of, gcol):
                def evac_of(cc, tb):
                    def gen(pap, tok, setfree):
                        k, sap, sfree = stgR.get()
                        lastd = None
                        for b in range(2):
                            i = cnt[0] % 2
                            cnt[0] += 1
                            ts = slice(tb * 1024 + b * 512, tb * 1024 + (b + 1) * 512)
                            a = ph.op("act", lambda e, o=sqb[:, i, :], p=pap[:, b, :]: e.activation(out=o, in_=p, func=AF.Square),
                                      waits=[tok, hld_all], sig=True)
                            m = ph.op("pe", lambda e, r=sqb[:, i, :]: e.matmul(psf[:, 4, :], lhsT=onesH[:], rhs=r, start=True, stop=True),
                                      waits=[a, auxA[0]], sig=True)
                            ra_, r_ = rstd_ops(ph, rb[:, i, :], psf[:, 4, :], [m])
                            auxA[0] = ra_
                            q_ = ph.op("dve", lambda e, o=qn[:, i, :], p=pap[:, b, :], rr=rb[:, i, :]:
                                       e.scalar_tensor_tensor(out=o, in0=p, scalar=qkg[:, l, gcol:gcol + 1], in1=rr, op0=ALU.mult, op1=ALU.mult),
                                       waits=[pmul[i]], sig=True)
                            lastd = q_
                            pq = ph.op("pe", lambda e, r=qn[:, i, :]: e.matmul(psf[:, 5, :], lhsT=rotT[:], rhs=r, start=True, stop=True),
                                       waits=[q_, auxB[0]], sig=True)
                            pmul[i] = ph.op("pool", lambda e, o=t1[:, i, :], a_=qn[:, i, :], c_=rC[:, ts]: e.tensor_tensor(out=o, in0=a_, in1=c_, op=ALU.mult),
                                            waits=[q_], sig=True)
                            g_ = ph.op("dve", lambda e, o=t2[:, i, :], s_=rS[:, ts]: e.tensor_tensor(out=o, in0=psf[:, 5, :], in1=s_, op=ALU.mult),
                                       waits=[pq, padd[i]], sig=True)
                            auxB[0] = g_
                            lastp = ph.op("pool", lambda e, o=sap[:, b * 512:(b + 1) * 512], a_=t1[:, i, :], b_=t2[:, i, :]:
                                          e.tensor_tensor(out=o, in0=a_, in1=b_, op=ALU.add), waits=[g_, sfree], sig=True)
                            padd[i] = lastp
                        setfree(lastd)
                        stgR.free[k] = ph.dma("sp", dst_of(cc)[:, tb * 1024:(tb + 1) * 1024], sap, f"ps{k}", waits=[lastp])
                        yield
                    return gen
                return evac_of

            hslot = {}

            def ev_conv(ci):
                def evac_of(cc, tb):
                    def gen(pap, tok, setfree):
                        pflat = pap.rearrange("p b t -> p (b t)")
                        if cc == 0:
                            t = ph.op("act", lambda e: e.activation(out=htmp[:, tb, :], in_=pflat, func=AF.Copy),
                                      waits=[tok, hslot.get(tb)], sig=True)
                            hslot[("h", tb)] = t
                            setfree(t)
                        elif cc == 1:
                            k, sap, sfree = stfR.get()
                            t = ph.op("dve", lambda e: e.tensor_tensor(out=sap, in0=pflat, in1=htmp[:, tb, :], op=ALU.mult),
                                      waits=[tok, sfree, hslot[("h", tb)]], sig=True)
                            hslot[tb] = t
                            setfree(t)
                            col = 0 if tb == 0 else 1023
                            t2_ = ph.op("dve", lambda e: e.tensor_copy(out=ub[:, tb, ci:ci + 1], in_=sap[:, col:col + 1]), sig=True)
                            stfR.free[k] = ph.dma("sp", uT_d[ci][:, tb * 1024:(tb + 1) * 1024], sap, f"pf{k}", waits=[t2_])
                        else:
                            k, sap, sfree = stfR.get()
                            t = ph.op("act", lambda e: e.activation(out=sap, in_=pflat, func=AF.Copy), waits=[tok, sfree], sig=True)
                            setfree(t)
                            stfR.free[k] = ph.dma("sp", bgT_d[ci][:, tb * 1024:(tb + 1) * 1024], sap, f"pf{k}", waits=[t])
                        yield
                    return gen
                return evac_of

            tiles = []
            for j in range(2):
                tiles.append(dict(pieces=[wpiece(j * 512, 512)],
                                  units=mm_units(4, ev_simple(lambda cc, j=j: qTna_d[4 * j + cc], SCALE, "act"))))
            for j in range(2):
                tiles.append(dict(pieces=[wpiece(1024 + j * 512, 512)],
                                  units=mm_units(4, ev_simple(lambda cc, j=j: kTna_d[4 * j + cc], 1.0, "dve"))))
            for j in range(2):
                tiles.append(dict(pieces=[wpiece(3072 + j * 512, 512)],
                                  units=mm_units(4, ev_rope(lambda cc, j=j: qTg_d[4 * j + cc], 0))))
            tiles.append(dict(pieces=[wpiece(4096, 256)],
                              units=mm_units(2, ev_rope(lambda cc: kTg_send[cc * 128:(cc + 1) * 128, :], 1))))
            for ci in range(8):
                tiles.append(dict(pieces=[wpiece(4608 + ci * 128, 128, 0), wpiece(6656 + ci * 128, 128, 128), wpiece(5632 + ci * 128, 128, 256)],
                                  units=mm_units(3, ev_conv(ci))))
            gemm(ph, tiles, wring, "pw", psring, pe_waits=[hld_all])

            vring = Ring([vst[:, i, :] for i in range(3)])
            vps = Ring([psf[:, 0, :], psf[:, 1, :], psf[:, 2, :], psf[:, 3, :]])
            vps.free = [psring.free[0], psring.free[0], psring.free[1], psring.free[1]]
            for (c0, ncols, dst) in [(2048, 512, vna_d[:, 0:512]), (2560, 512, vna_d[:, 512:1024]), (4352, 256, vg_send)]:
                ws, wap, wfree = wring.get()
                wtk = ph.dma("pool", wap[:, :, 0:ncols], W[:, c0:c0 + ncols].rearrange("(k p) n -> p k n", p=128), f"pw{ws}", waits=[wfree])
                tok = None
                for t in range(16):
                    pk, pap, pfree = vps.get()
                    for k in range(KC):
                        tok = ph.op("pe", lambda e, o=pap[:, 0:ncols], l_=hT[:, k, t * 128:(t + 1) * 128], r=wap[:, k, 0:ncols], a=(k == 0), b=(k == KC - 1):
                                    e.matmul(o, lhsT=l_, rhs=r, start=a, stop=b), waits=[wtk, pfree] if k == 0 else (), sig=(k == KC - 1))
                    sk, sap, sfree = vring.get()
                    if t % 2 == 0:
                        ev = ph.op("act", lambda e, o=sap[:, 0:ncols], i=pap[:, 0:ncols]: e.activation(out=o, in_=i, func=AF.Copy), waits=[tok, sfree], sig=True)
                    else:
                        ev = ph.op("dve", lambda e, o=sap[:, 0:ncols], i=pap[:, 0:ncols]: e.tensor_copy(out=o, in_=i), waits=[tok, sfree], sig=True)
                    vps.free[pk] = ev
                    vring.free[sk] = ph.dma("sp", dst[t * 128:(t + 1) * 128, :], sap[:, 0:ncols], f"pv{sk}", waits=[ev])
                wring.free[ws] = tok
            ph.dma("sp", ub_d.ap().rearrange("(w p) c -> p w c", p=128), ub[:], "pub",
                   waits=[Tok(ctx.esem["dve"], "m_dve", ctx.ecnt["dve"])])
            ph.run()

    def phase_exchange():
        ph = Phase(ctx)
        a = ph.dma("sp", hk_send[0:1024, :].rearrange("(h d) t -> h d t", d=128), kTna_d[:, :, 0:256], "xh")
        a = ph.dma("sp", hk_send[1024:2048, :].rearrange("(h d) t -> h d t", d=128), kTna_d[:, :, NTOK - 256:NTOK], "xh")
        a = ph.dma("sp", hv_send[0:256, :], vna_d[0:256, :], "xh")
        a = ph.dma("sp", hv_send[256:512, :], vna_d[NTOK - 256:NTOK, :], "xh")
        c1 = a
        for i in range(4):
            c1 = ph.coll(send_t[i].ap(), recv_t[i].ap(), waits=[c1])
        ph.coll(ub_d.ap(), uball_d.ap(), waits=[c1])
        ph.run()

    def phase_conv(l):
        with ExitStack() as es2:
            T = lambda name, shape, dt: es2.enter_context(nc.sbuf_tensor(uname(name), list(shape), dt))
            ubs = T("c_ub", [128, 4, 2, 8], F32)
            prv = T("c_prv", [128, 8], F32)
            nxt = T("c_nxt", [128, 8], F32)
            uh = T("c_uh", [128, 2, NTOK + 2], F32)
            bg = T("c_bg", [128, 2, NTOK], F32)
            y = T("c_y", [128, 2, NTOK], F32)
            yb = T("c_yb", [128, 2, NTOK], BF16)
            ph = Phase(ctx)
            ld = ph.dma("sp", ubs[:], uball_d.ap().rearrange("(r w p) c -> p r w c", r=4, w=2), "cu")
            tk = None
            for r in range(4):
                if r == 0:
                    ph.op("dve", lambda e: e.tensor_scalar(prv[:], ubs[:, 0, 1, :], sel[:, 0:1], None, ALU.mult), waits=[ld])
                    tk = ph.op("dve", lambda e: e.tensor_scalar(nxt[:], ubs[:, 0, 0, :], sel[:, 4:5], None, ALU.mult), sig=True)
                else:
                    ph.op("dve", lambda e, r=r: e.scalar_tensor_tensor(out=prv[:], in0=ubs[:, r, 1, :], scalar=sel[:, r:r + 1], in1=prv[:], op0=ALU.mult, op1=ALU.add), waits=[tk])
                    tk = ph.op("dve", lambda e, r=r: e.scalar_tensor_tensor(out=nxt[:], in0=ubs[:, r, 0, :], scalar=sel[:, 4 + r:5 + r], in1=nxt[:], op0=ALU.mult, op1=ALU.add), sig=True)
            free = [None, None]
            for ci in range(8):
                s = ci % 2
                eng = "dve"
                l1 = ph.dma("sp", uh[:, s, 1:NTOK + 1], uT_d[ci], f"cl{s}", waits=[free[s]])
                l2 = ph.dma("sp", bg[:, s, :], bgT_d[ci], f"cl{s}", waits=[free[s]])
                cp = lambda k: convp[:, l, ci * 4 + k:ci * 4 + k + 1]
                a = ph.op("dve", lambda e: e.tensor_copy(out=uh[:, s, 0:1], in_=prv[:, ci:ci + 1]), waits=[tk, l2])
                a = ph.op("dve", lambda e: e.tensor_copy(out=uh[:, s, NTOK + 1:NTOK + 2], in_=nxt[:, ci:ci + 1]), sig=True)
                ph.op(eng, lambda e: e.tensor_scalar(y[:, s, :], uh[:, s, 1:NTOK + 1], cp(1), cp(3), ALU.mult, ALU.add), waits=[a, l2])
                b = ph.op(eng, lambda e: e.scalar_tensor_tensor(out=y[:, s, :], in0=uh[:, s, 0:NTOK], scalar=cp(0), in1=y[:, s, :], op0=ALU.mult, op1=ALU.add), sig=True)
                b = ph.op(eng, lambda e: e.scalar_tensor_tensor(out=y[:, s, :], in0=uh[:, s, 2:NTOK + 2], scalar=cp(2), in1=y[:, s, :], op0=ALU.mult, op1=ALU.add), waits=[b], sig=True)
                b = ph.op(eng, lambda e: e.tensor_tensor(out=yb[:, s, :], in0=y[:, s, :], in1=bg[:, s, :], op=ALU.mult), waits=[b], sig=True)
                free[s] = ph.dma("sp", brT_d[2, ci], yb[:, s, :], f"cs{s}", waits=[b])
            ph.run()

    def halo_select(ph, eng, dst, src_of, selbase, ld):
        tk = ph.op(eng, lambda e: e.tensor_scalar(dst, src_of(0), sel[:, selbase:selbase + 1], None, ALU.mult),
                   waits=[ld, Tok(ctx.esem["pool"], "m_pool", ctx.ecnt["pool"])], sig=True)
        for r in range(1, 4):
            tk = ph.op(eng, lambda e, r=r: e.scalar_tensor_tensor(out=dst, in0=src_of(r), scalar=sel[:, selbase + r:selbase + r + 1], in1=dst,
                                                                   op0=ALU.mult, op1=ALU.add), waits=[tk], sig=True)
        return tk

    def attn_finish(ph, o_ps, pe_tok, ob_ap, ob_free, tp_ap, tp_free, dst_ap, dst_free, evac_eng):
        ph.op("dve", lambda e: e.reciprocal(out=ob_ap["r"], in_=o_ps[:, 128:129]), waits=[pe_tok, ob_free])
        n = ph.op("dve", lambda e: e.tensor_scalar(ob_ap["o"], o_ps[:, 0:128], ob_ap["r"], None, ALU.mult), sig=True)
        t = ph.op("pe", lambda e: e.transpose(out=tp_ap, in_=ob_ap["o"], identity=ident[:]), waits=[n, tp_free], sig=True)
        if evac_eng == "act":
            c = ph.op("act", lambda e: e.activation(out=dst_ap, in_=tp_ap, func=AF.Copy), waits=[t, dst_free], sig=True)
        else:
            c = ph.op("dve", lambda e: e.tensor_copy(out=dst_ap, in_=tp_ap), waits=[t, dst_free], sig=True)
        return n, t, c

    def phase_na(l):
        with ExitStack() as es2:
            T = lambda name, shape, dt: es2.enter_context(nc.sbuf_tensor(uname(name), list(shape), dt))
            qT = T("a_qT", [128, 8, NTOK], BF16)
            Kb = T("a_K", [128, 8, 20 * 128], BF16)
            Vb = T("a_V", [128, 20, 8, 129], BF16)
            tmpk = T("a_tk", [128, 4, 8, 256], BF16)
            tmpv = T("a_tv", [128, 4, 2, 1024], BF16)
            bias = T("a_bias", [128, 3, 768], BF16)
            PT = T("a_PT", [128, 2, 768], BF16)
            ob = T("a_ob", [128, 2, 128], BF16)
            rr = T("a_rr", [128, 2, 1], F32)
            ost = T("a_ost", [128, 2, 8, 128], BF16)
            ph = Phase(ctx)
            ms = ph.op("pool", lambda e: e.memset(Vb[:], 1.0), sig=True)
            lq = ph.dma("sp", qT[:], qTna_d.rearrange("h d t -> d h t"), "aq")
            lk = ph.dma("sp", Kb[:, :, 256:256 + NTOK], kTna_d.rearrange("h d t -> d h t"), "aq")
            lv = None
            for h8 in range(8):
                lv = ph.dma("sp", Vb[:, 2:18, h8, 0:128],
                            vna_d[:, 128 * h8:128 * h8 + 128].rearrange("(c p) d -> p c d", p=128), "aq", waits=[ms])
            for (w, selb, kdst, vdst) in [(1, 0, Kb[:, :, 0:256], Vb[:, 0:2, :, 0:128]), (0, 4, Kb[:, :, 2304:2560], Vb[:, 18:20, :, 0:128])]:
                ldk = ldv = None
                for r in range(4):
                    hk_r = recv_view(2, r).rearrange("r (a c) -> (r a) c", c=256)
                    hv_r = recv_view(3, r).rearrange("r (a c) -> (r a) c", c=1024)
                    ldk = ph.dma("sp", tmpk[:, r], hk_r[w * 1024:(w + 1) * 1024, :].rearrange("(h d) t -> d h t", d=128), f"ah{w}",
                                 waits=[Tok(ctx.esem["dve"], "m_dve", ctx.ecnt["dve"]), Tok(ctx.esem["pool"], "m_pool", ctx.ecnt["pool"])])
                    ldv = ph.dma("sp", tmpv[:, r], hv_r[w * 256:(w + 1) * 256, :].rearrange("(c p) n -> p c n", p=128), f"ah{w}",
                                 waits=[Tok(ctx.esem["dve"], "m_dve", ctx.ecnt["dve"]), Tok(ctx.esem["pool"], "m_pool", ctx.ecnt["pool"])])
                halo_select(ph, "dve", kdst, lambda r: tmpk[:, r], selb, ldv)
                halo_select(ph, "dve", vdst, lambda r: tmpv[:, r].rearrange("p c (h d) -> p c h d", d=128), selb, ldv)
            ready = [lv, Tok(ctx.esem["dve"], "m_dve", ctx.ecnt["dve"]), Tok(ctx.esem["pool"], "m_pool", ctx.ecnt["pool"])]

            units = [(lp, h) for lp in range(16) for h in range(8)]
            N = len(units)
            slot_of = lambda lp: 0 if lp == 0 else 1 if lp == 1 else 3 if lp == 14 else 4 if lp == 15 else 2
            bfree = [None] * 3
            btok = [None] * N
            sfree = [None, None]
            ptfree = [None, None]
            ofree = [None, None]
            obfree = [None, None]
            tpfree = [None, None]
            ostfree = [None, None]
            qk_tok = [None] * N
            ex_tok = [None] * N
            pv_tok = [None] * N

            def load_bias(u):
                lp, h = units[u]
                btok[u] = ph.dma("sp", bias[:, u % 3, :], nab_d[l, slot_of(lp), h], f"ab{u % 3}", waits=[bfree[u % 3]])

            load_bias(0)
            load_bias(1)
            nrm_tok = [None] * N
            for step in range(N + 3):
                if step + 2 < N:
                    load_bias(step + 2)
                if step < N:
                    u = step
                    lp, h = units[u]
                    s = u % 2
                    wlo = max(lp - 1, 0)
                    tok = None
                    for j in range(6):
                        o = psf[:, 2 * s + j // 4, (j % 4) * 128:(j % 4 + 1) * 128]
                        ph.op("pe", lambda e, o=o, k=Kb[:, h, (wlo + j) * 128:(wlo + j + 1) * 128], q=qT[:, h, lp * 128:(lp + 1) * 128]:
                              e.matmul(o, lhsT=k, rhs=q, start=True, stop=False), waits=ready + [sfree[s], btok[u]] if j == 0 else ())
                        tok = ph.op("pe", lambda e, o=o, b_=bias[:, u % 3, j * 128:(j + 1) * 128]: e.matmul(o, lhsT=ident[:], rhs=b_, start=False, stop=True),
                                    sig=(j == 5))
                    qk_tok[u] = tok
                    bfree[u % 3] = tok
                    ph.op("act", lambda e, s=s: e.activation(out=PT[:, s, 0:512], in_=psf[:, 2 * s, :], func=AF.Exp), waits=[tok, ptfree[s]])
                    ex_tok[u] = ph.op("act", lambda e, s=s: e.activation(out=PT[:, s, 512:768], in_=psf[:, 2 * s + 1, 0:256], func=AF.Exp), sig=True)
                    sfree[s] = ex_tok[u]
                if 0 <= step - 1 < N:
                    u = step - 1
                    lp, h = units[u]
                    s = u % 2
                    wlo = max(lp - 1, 0)
                    tok = None
                    for j in range(6):
                        tok = ph.op("pe", lambda e, s=s, j=j, v=Vb[:, wlo + j, h, :]: e.matmul(psf[:, 4 + s, 0:129], lhsT=PT[:, s, j * 128:(j + 1) * 128], rhs=v,
                                                                                             start=(j == 0), stop=(j == 5)),
                                    waits=[ex_tok[u], ofree[s]] if j == 0 else (), sig=(j == 5))
                    pv_tok[u] = tok
                    ptfree[s] = tok
                    ph.op("dve", lambda e, s=s: e.reciprocal(out=rr[:, s, :], in_=psf[:, 4 + s, 128:129]), waits=[tok, obfree[s]])
                    nrm_tok[u] = ph.op("dve", lambda e, s=s: e.tensor_scalar(ob[:, s, :], psf[:, 4 + s, 0:128], rr[:, s, :], None, ALU.mult), sig=True)
                    ofree[s] = nrm_tok[u]
                if 0 <= step - 2 < N:
                    u = step - 2
                    lp, h = units[u]
                    s = u % 2
                    tp_ap = pst[s][:, 0:128]
                    t = ph.op("pe", lambda e, s=s, tp_ap=tp_ap: e.transpose(out=tp_ap, in_=ob[:, s, :], identity=ident[:]), waits=[nrm_tok[u], tpfree[s]], sig=True)
                    obfree[s] = t
                    dst_ap = ost[:, lp % 2, h, :]
                    dfree = ostfree[lp % 2] if h <= 1 else None
                    if u % 2 == 0:
                        c = ph.op("dve", lambda e, d=dst_ap, tp_ap=tp_ap: e.tensor_copy(out=d, in_=tp_ap), waits=[t, dfree], sig=True)
                    else:
                        c = ph.op("act", lambda e, d=dst_ap, tp_ap=tp_ap: e.activation(out=d, in_=tp_ap, func=AF.Copy), waits=[t, dfree], sig=True)
                    tpfree[s] = c
                    if h == 7:
                        cprev = Tok(ctx.esem["dve"], "m_dve", ctx.ecnt["dve"])
                        cact = Tok(ctx.esem["act"], "m_act", ctx.ecnt["act"])
                        ostfree[lp % 2] = ph.dma("sp", brT_d[0].rearrange("h d t -> d h t")[:, :, lp * 128:(lp + 1) * 128], ost[:, lp % 2], f"ao{lp % 2}",
                                                 waits=[c, cprev, cact])
            ph.run()

    def phase_gqa():
        with ExitStack() as es2:
            T = lambda name, shape, dt: es2.enter_context(nc.sbuf_tensor(uname(name), list(shape), dt))
            qT = T("g_qT", [128, 8, NTOK], BF16)
            KT = T("g_KT", [128, 4, 2, NTOK], BF16)
            Vb = T("g_V", [128, 64, 2, 129], BF16)
            PT = T("g_PT", [128, 3, 512], BF16)
            ob = T("g_ob", [128, 2, 128], BF16)
            rr = T("g_rr", [128, 2, 1], F32)
            ost = T("g_ost", [128, 2, 4, 128], BF16)
            ph = Phase(ctx)
            ms = ph.op("pool", lambda e: e.memset(Vb[:], 1.0), sig=True)
            lds = [ph.dma("sp", qT[:], qTg_d.rearrange("h d t -> d h t"), "gq")]
            for r in range(4):
                lds.append(ph.dma("sp", KT[:, r], recv_view(0, r).rearrange("(k d) t -> d k t", d=128), "gq"))
                vg_r = recv_view(1, r).rearrange("r (a c) -> (r a) c", c=256)
                for k2 in range(2):
                    lds.append(ph.dma("sp", Vb[:, 16 * r:16 * r + 16, k2, 0:128], vg_r[:, k2 * 128:(k2 + 1) * 128].rearrange("(c p) d -> p c d", p=128), "gq", waits=[ms]))
            ready = [lds[-1]]
            blocks = [(g, t, kc) for g in range(2) for t in range(16) for kc in range(64)]
            N = len(blocks)
            sfree = [None, None]
            ptfree = [None] * 3
            ofree = [None] * 4
            obfree = [None, None]
            tpfree = [None, None]
            ostfree = [None, None]
            qk_tok = [None] * N
            ex_tok = [None] * N
            fin = 0
            for step in range(N + 1):
                if step < N:
                    g, t, kc = blocks[step]
                    s = step % 2
                    qk_tok[step] = ph.op("pe", lambda e, s=s, k=KT[:, kc // 16, g, (kc % 16) * 128:(kc % 16 + 1) * 128], q=qT[:, 4 * g:4 * g + 4, t * 128:(t + 1) * 128]:
                                         e.matmul(psf[:, s, :], lhsT=k, rhs=q, start=True, stop=True), waits=ready + [sfree[s]], sig=True)
                    p = step % 3
                    ex_tok[step] = ph.op("act", lambda e, s=s, p=p: e.activation(out=PT[:, p, :], in_=psf[:, s, :], func=AF.Exp, scale=SCALE),
                                         waits=[qk_tok[step], ptfree[p]], sig=True, chain=False)
                    sfree[s] = ex_tok[step]
                if step >= 1:
                    b = step - 1
                    g, t, kc = blocks[b]
                    p = b % 3
                    tok = None
                    for hh in range(4):
                        tok = ph.op("pe", lambda e, hh=hh, p=p, v=Vb[:, kc, g, :]: e.matmul(psf[:, 2 + hh, 0:129], lhsT=PT[:, p, hh * 128:(hh + 1) * 128], rhs=v,
                                                                                          start=(kc == 0), stop=(kc == 63)),
                                    waits=[ex_tok[b]] + ([ofree[hh]] if kc == 0 else []) if hh == 0 or kc == 0 else (), sig=(hh == 3))
                    ptfree[p] = tok
                    if kc == 63:
                        gi = g * 16 + t
                        for hh in range(4):
                            s2 = fin % 2
                            fin += 1
                            obd = dict(o=ob[:, s2, :], r=rr[:, s2, :])
                            n, tt, c = attn_finish(ph, psf[:, 2 + hh, :], tok, obd, obfree[s2], pst[s2][:, 0:128], tpfree[s2],
                                                   ost[:, gi % 2, hh, :], ostfree[gi % 2] if hh == 0 else None, "dve")
                            ofree[hh] = n
                            obfree[s2] = tt
                            tpfree[s2] = c
                        ostfree[gi % 2] = ph.dma("sp", brT_d[1].rearrange("h d t -> d h t")[:, 4 * g:4 * g + 4, t * 128:(t + 1) * 128], ost[:, gi % 2], f"go{gi % 2}",
                                                 waits=[c])
            ph.run()

    def phase_merge(l):
        for half in range(2):
            with ExitStack() as es2:
                T = lambda name, shape, dt: es2.enter_context(nc.sbuf_tensor(uname(name), list(shape), dt))
                hT = T("m_hT", [128, KC, 1024], BF16)
                bT = T("m_bT", [128, 3, 8, 1024], BF16)
                wt = T("m_wt", [128, 2, 24, 256], BF16)
                sg = T("m_sg", [128, 2, 1024], F32)
                tmp = T("m_tmp", [128, 2, 1024], F32)
                macc = T("m_acc", [128, 2, 2, 1024], F32)
                mb = T("m_mb", [128, 2, 1024], BF16)
                ph = Phase(ctx)
                hs = slice(half * 1024, (half + 1) * 1024)
                ld = None
                for q in range(4):
                    ld = ph.dma("sp", hT[:, 4 * q:4 * q + 4, :], hT_d[4 * q:4 * q + 4, :, hs].rearrange("k p t -> p k t"), "mh")
                for b in range(3):
                    ld = ph.dma("sp", bT[:, b], brT_d[b, :, :, hs].rearrange("k p t -> p k t"), "mh")
                wring = Ring([wt[:, 0], wt[:, 1]])
                psring = Ring([psf[:, 0:2, :], psf[:, 2:4, :], psf[:, 4:6, :]])
                sgR = Ring([sg[:, 0, :], sg[:, 1, :]])
                tmpR = Ring([tmp[:, 0, :], tmp[:, 1, :]])
                mbR = Ring([mb[:, 0, :], mb[:, 1, :]])
                state = {}
                accfree = {}
                tiles = []
                for ct in range(8):
                    for b in range(3):
                        gc0 = 7680 + b * 2048 + ct * 256
                        pieces = [(lambda wap: wap[:, 0:16, :], w_in[l][:, gc0:gc0 + 256].rearrange("(k p) n -> p k n", p=128)),
                                  (lambda wap: wap[:, 16:24, :], w_br[l, b][:, ct * 256:(ct + 1) * 256].rearrange("(k p) n -> p k n", p=128))]
                        units = []
                        for cc in range(2):
                            def evG(pap, tok, setfree, ct=ct, b=b, cc=cc):
                                k, sap, sfree_ = sgR.get()
                                t = ph.op("act", lambda e: e.activation(out=sap, in_=pap.rearrange("p b t -> p (b t)"), func=AF.Sigmoid),
                                          waits=[tok, sfree_, ld], sig=True)
                                setfree(t)
                                state[(ct, b, cc)] = (k, sap, t)
                                yield

                            def evY(pap, tok, setfree, ct=ct, b=b, cc=cc):
                                k, sap, st = state[(ct, b, cc)]
                                pflat = pap.rearrange("p b t -> p (b t)")
                                acc = macc[:, ct % 2, cc, :]
                                if b == 0:
                                    t = ph.op("dve", lambda e: e.tensor_tensor(out=acc, in0=pflat, in1=sap, op=ALU.mult),
                                              waits=[tok, st, accfree.get((ct % 2, cc))], sig=True)
                                    setfree(t)
                                    sgR.free[k] = t
                                else:
                                    tk_, tap, tfree = tmpR.get()
                                    t = ph.op("dve", lambda e: e.tensor_tensor(out=tap, in0=pflat, in1=sap, op=ALU.mult),
                                              waits=[tok, st, tfree], sig=True)
                                    setfree(t)
                                    sgR.free[k] = t
                                    if b == 1:
                                        t2_ = ph.op("pool", lambda e: e.tensor_tensor(out=acc, in0=acc, in1=tap, op=ALU.add), waits=[t], sig=True)
                                        tmpR.free[tk_] = t2_
                                    else:
                                        mk, map_, mfree = mbR.get()
                                        t2_ = ph.op("pool", lambda e: e.tensor_tensor(out=map_, in0=acc, in1=tap, op=ALU.add), waits=[t, mfree], sig=True)
                                        tmpR.free[tk_] = t2_
                                        accfree[(ct % 2, cc)] = t2_
                                        mbR.free[mk] = ph.dma("sp", mT_d[ct * 2 + cc][:, hs], map_, f"mm{mk}", waits=[t2_])
                                yield

                            mmG = [(bk, [(lambda wap, k=k, cc=cc: wap[:, k, cc * 128:(cc + 1) * 128], hT[:, k, bk * 512:(bk + 1) * 512]) for k in range(16)])
                                   for bk in range(2)]
                            mmY = [(bk, [(lambda wap, k=k, cc=cc: wap[:, 16 + k, cc * 128:(cc + 1) * 128], bT[:, b, k, bk * 512:(bk + 1) * 512]) for k in range(8)])
                                   for bk in range(2)]
                            units.append(dict(mm=mmG, evac=evG))
                            units.append(dict(mm=mmY, evac=evY))
                        tiles.append(dict(pieces=pieces, units=units))
                gemm(ph, tiles, wring, "mw", psring, pe_waits=[ld])
                ph.run()

    def phase_proj_res(l, KCn, act_d, Wd, TB, gcol, Xsrc, Xdst, wcols):
        nb = TB // 512
        for blk in range(NTOK // TB):
            with ExitStack() as es2:
                T = lambda name, shape, dt: es2.enter_context(nc.sbuf_tensor(uname(name), list(shape), dt))
                ksplit = (KCn == FC)
                NSQ = 4 if ksplit else 2
                aT = T("r_aT", [128, KCn, TB], BF16)
                if ksplit:
                    wtk = T("r_wtk", [128, 3, 11, 512], BF16)
                    wt = T("r_wt", [128, 2, 1, 128], BF16)
                else:
                    wt = T("r_wt", [128, 2, KCn, wcols], BF16)
                z = T("r_z", [128, KC, TB], F32)
                sq = T("r_sq", [128, NSQ, TB], F32)
                rs = T("r_rs", [128, TB], F32)
                xt = T("r_xt", [128, 2, TB], F32)
                t1 = T("r_t1", [128, 2, TB], F32)
                xo = T("r_xo", [128, 2, TB], F32)
                ph = Phase(ctx)
                bs = slice(blk * TB, (blk + 1) * TB)
                ld = None
                step = 4 if KCn % 4 == 0 else 1
                for q in range(0, KCn, step):
                    ld = ph.dma("sp", aT[:, q:q + step, :], act_d[q:q + step, :, bs].rearrange("k p t -> p k t"), "ra")
                wring = Ring([wt[:, 0], wt[:, 1]])
                psring = Ring([psf[:, 0:nb, :], psf[:, 2:2 + nb, :]])
                sqR = Ring([sq[:, i, :] for i in range(NSQ)])
                ssfree = [None]
                sstok = [None]
                tiles = []
                cpt = wcols // 128
                if ksplit:
                    KPG, NKG = 11, 4
                    wslot_free = [None] * 3
                    bankfree = [None] * 4
                    pend = []
                    wtiles = [(cg, kg) for cg in range(4) for kg in range(NKG)]
                    wtoks = {}

                    def issue_w(i):
                        cg, kg = wtiles[i]
                        sl_ = i % 3
                        wtoks[i] = ph.dma("pool", wtk[:, sl_], Wd[kg * KPG * 128:(kg + 1) * KPG * 128, cg * 512:(cg + 1) * 512].rearrange("(k p) n -> p k n", p=128),
                                          f"rk{sl_}", waits=[wslot_free[sl_]])

                    def flush():
                        for (c_, k_, sap_, a1_) in pend:
                            tk_ = ph.op("pe", lambda e: e.matmul(psf[:, 4, :], lhsT=onesD[:], rhs=sap_, start=(c_ == 0), stop=(c_ == KC - 1)), waits=[a1_], sig=True)
                            sqR.free[k_] = tk_
                            sstok[0] = tk_
                        pend.clear()

                    issue_w(0)
                    issue_w(1)
                    for ti, (cg, kg) in enumerate(wtiles):
                        if ti + 2 < len(wtiles):
                            issue_w(ti + 2)
                        sl_ = ti % 3
                        last = None
                        for cc in range(4):
                            for k in range(KPG):
                                first = (kg == 0 and k == 0)
                                lastmm = (kg == NKG - 1 and k == KPG - 1)
                                w_ = []
                                if k == 0 and cc == 0:
                                    w_ += [wtoks[ti], ld]
                                if first:
                                    w_.append(bankfree[cc])
                                last = ph.op("pe", lambda e: e.matmul(psf[:, cc, :], lhsT=wtk[:, sl_, k, cc * 128:(cc + 1) * 128], rhs=aT[:, kg * KPG + k, :],
                                                                      start=first, stop=lastmm), waits=w_, sig=(k == KPG - 1))
                            if kg == NKG - 1:
                                c = cg * 4 + cc
                                a2 = ph.op("dve", lambda e: e.tensor_copy(out=z[:, c, :], in_=psf[:, cc, :]), waits=[last], sig=True)
                                bankfree[cc] = a2
                                k_, sap, sfree_ = sqR.get()
                                a1 = ph.op("act", lambda e: e.activation(out=sap, in_=z[:, c, :], func=AF.Square), waits=[a2, sfree_], sig=True)
                                pend.append((c, k_, sap, a1))
                        wslot_free[sl_] = last
                        if kg == 0 and pend:
                            flush()
                    flush()
                for ct in range(0 if ksplit else D // wcols):
                    units = []
                    for cc in range(cpt):
                        c = ct * cpt + cc

                        def ev(pap, tok, setfree, c=c):
                            pflat = pap.rearrange("p b t -> p (b t)")
                            k, sap, sfree_ = sqR.get()
                            a2 = ph.op("dve", lambda e: e.tensor_copy(out=z[:, c, :], in_=pflat), waits=[tok], sig=True)
                            a1 = ph.op("act", lambda e: e.activation(out=sap, in_=z[:, c, :], func=AF.Square), waits=[a2, sfree_, ld], sig=True)
                            setfree(a2)
                            psring.free[(psring.i - 1) % 2] = a2
                            yield
                            tk = None
                            for bk in range(nb):
                                tk = ph.op("pe", lambda e, bk=bk: e.matmul(psf[:, 4 + bk, :], lhsT=onesD[:], rhs=sap[:, bk * 512:(bk + 1) * 512],
                                                                         start=(c == 0), stop=(c == KC - 1)), waits=[a1, a2], sig=(bk == nb - 1))
                            sqR.free[k] = tk
                            sstok[0] = tk
                            yield

                        mm = [(bk, [(lambda wap, k=k, cc=cc: wap[:, k, cc * 128:(cc + 1) * 128], aT[:, k, bk * 512:(bk + 1) * 512]) for k in range(KCn)])
                              for bk in range(nb)]
                        units.append(dict(mm=mm, evac=ev))
                    tiles.append(dict(pieces=[(lambda wap: wap, Wd[:, ct * wcols:(ct + 1) * wcols].rearrange("(k p) n -> p k n", p=128))], units=units))
                if tiles:
                    gemm(ph, tiles, wring, "rw", psring, pe_waits=[ld])
                ra, r = rstd_ops(ph, rs[:], psf[:, 4:4 + nb, :].rearrange("p b t -> p (b t)"), [sstok[0]])
                xfree = [None, None]
                ofree = [None, None]
                for c in range(KC):
                    s = c % 2
                    lx = ph.dma("sp", xt[:, s, :], Xsrc[c][:, bs], f"rx{s}", waits=[xfree[s]])
                    a = ph.op("dve", lambda e, c=c, s=s: e.scalar_tensor_tensor(out=t1[:, s, :], in0=z[:, c, :], scalar=gains[:, l, gcol * KC + c:gcol * KC + c + 1],
                                                                              in1=rs[:], op0=ALU.mult, op1=ALU.mult), waits=[r, xfree[s]], sig=True)
                    b = ph.op("pool", lambda e, s=s: e.tensor_tensor(out=xo[:, s, :], in0=t1[:, s, :], in1=xt[:, s, :], op=ALU.add), waits=[a, lx, ofree[s]], sig=True)
                    xfree[s] = b
                    ofree[s] = ph.dma("sp", Xdst[c][:, bs], xo[:, s, :], f"ro{s}", waits=[b])
                ph.run()

    def phase_ffn_up(l):
        with ExitStack() as es2:
            T = lambda name, shape, dt: es2.enter_context(nc.sbuf_tensor(uname(name), list(shape), dt))
            hT = T("f_hT", [128, KC, NTOK], BF16)
            wt = T("f_wt", [128, 2, 32, 256], BF16)
            sl = T("f_sl", [128, 2, 1024], F32)
            ab = T("f_ab", [128, 3, 1024], BF16)
            ph = Phase(ctx)
            ld = None
            for q in range(4):
                ld = ph.dma("sp", hT[:, 4 * q:4 * q + 4, :], hT_d[4 * q:4 * q + 4].rearrange("k p t -> p k t"), "fh")
            wring = Ring([wt[:, 0], wt[:, 1]])
            psring = Ring([psf[:, 0:2, :], psf[:, 2:4, :], psf[:, 4:6, :]])
            slR = Ring([sl[:, 0, :], sl[:, 1, :]])
            abR = Ring([ab[:, i, :] for i in range(3)])
            state = {}
            tiles = []
            for jt in range(FF // 256):
                pieces = [(lambda wap: wap[:, 0:16, :], w_gate[l][:, jt * 256:(jt + 1) * 256].rearrange("(k p) n -> p k n", p=128)),
                          (lambda wap: wap[:, 16:32, :], w_up[l][:, jt * 256:(jt + 1) * 256].rearrange("(k p) n -> p k n", p=128))]
                units = []
                for tb in range(2):
                    for cc in range(2):
                        j = jt * 2 + cc

                        def evG(pap, tok, setfree, key=(jt, tb, cc)):
                            k, sap, sfree_ = slR.get()
                            t = ph.op("act", lambda e: e.activation(out=sap, in_=pap.rearrange("p b t -> p (b t)"), func=AF.Silu), waits=[tok, sfree_, ld], sig=True)
                            setfree(t)
                            state[key] = (k, sap, t)
                            yield

                        def evU(pap, tok, setfree, key=(jt, tb, cc), j=j, tb=tb):
                            k, sap, st = state[key]
                            ak, aap, afree = abR.get()
                            t = ph.op("dve", lambda e: e.tensor_tensor(out=aap, in0=pap.rearrange("p b t -> p (b t)"), in1=sap, op=ALU.mult),
                                      waits=[tok, st, afree], sig=True)
                            setfree(t)
                            slR.free[k] = t
                            abR.free[ak] = ph.dma("sp", aT_d[j][:, tb * 1024:(tb + 1) * 1024], aap, f"fa{ak}", waits=[t])
                            yield

                        mmG = [(bk, [(lambda wap, k=k, cc=cc: wap[:, k, cc * 128:(cc + 1) * 128], hT[:, k, tb * 1024 + bk * 512:tb * 1024 + (bk + 1) * 512])
                                     for k in range(16)]) for bk in range(2)]
                        mmU = [(bk, [(lambda wap, k=k, cc=cc: wap[:, 16 + k, cc * 128:(cc + 1) * 128], hT[:, k, tb * 1024 + bk * 512:tb * 1024 + (bk + 1) * 512])
                                     for k in range(16)]) for bk in range(2)]
                        units.append(dict(mm=mmG, evac=evG))
                        units.append(dict(mm=mmU, evac=evU))
                tiles.append(dict(pieces=pieces, units=units))
            gemm(ph, tiles, wring, "fw", psring, pe_waits=[ld])
            ph.run()

    for l in range(L):
        X = xT_in if l == 0 else xres
        Xout = yT if l == L - 1 else xres
        plist = [
            lambda: phase_norm(X, gains[:, l, 0:KC]),
            lambda: phase_proj(l),
            lambda: phase_exchange(),
            lambda: phase_conv(l),
            lambda: phase_na(l),
            lambda: phase_gqa(),
            lambda: phase_merge(l),
            lambda: phase_proj_res(l, KC, mT_d, w_out[l], 1024, 1, X, xmid, 512),
            lambda: phase_norm(xmid, gains[:, l, 2 * KC:3 * KC]),
            lambda: phase_ffn_up(l),
            lambda: phase_proj_res(l, FC, aT_d, w_down[l], 512, 3, xmid, Xout, 128),
        ]
        for pi, pf in enumerate(plist):
            if (stop_after is None or pi < stop_after) and pi not in skip:
                pf()
    es.close()
    return nc


def _fmajor_vec(v):
    return np.ascontiguousarray(v.reshape(-1, 128).T)


def _na_bias_tables(rpb, qtr):
    out = np.full((5, 8, 128, 768), NEG, dtype=np.float32)
    kk = np.arange(128)
    qq = np.arange(128)
    for si, lp in enumerate([0, 1, 2, 14, 15]):
        p = qtr * 16 + lp
        wlo = max(lp - 1, 0)
        g0 = qtr * 16 - 2 + wlo
        qrow = 2 * p + qq // 64
        qcol = qq % 64
        rstart = np.clip(qrow - 4, 0, 120)
        cstart = np.clip(qcol - 8, 0, 48)
        for j in range(6):
            gc = g0 + j
            if gc < 0 or gc > 63:
                continue
            krow = 2 * gc + kk // 64
            kcol = kk % 64
            dr = krow[:, None] - qrow[None, :]
            dc = kcol[:, None] - qcol[None, :]
            valid = ((krow[:, None] >= rstart[None, :]) & (krow[:, None] < rstart[None, :] + 8)
                     & (kcol[:, None] >= cstart[None, :]) & (kcol[:, None] < cstart[None, :] + 16))
            ri = np.clip(dr + 7, 0, 14)
            ci = np.clip(dc + 15, 0, 30)
            vals = rpb[:, ri, ci]
            out[si, :, :, j * 128:(j + 1) * 128] = np.where(valid[None], vals, NEG)
    return out.astype(NPBF)


def _rope_tables(qtr):
    t = qtr * NTOK + np.arange(NTOK)
    prow = (t // 64).astype(np.float32)
    pcol = (t % 64).astype(np.float32)
    inv = (1.0 / (10000.0 ** (np.arange(32, dtype=np.float32) / 32))).astype(np.float32)
    C = np.zeros((128, NTOK), np.float32)
    Sn = np.zeros((128, NTOK), np.float32)
    for d in range(128):
        pos = prow if d < 64 else pcol
        ang = pos * inv[d % 32]
        C[d] = np.cos(ang)
        Sn[d] = np.sin(ang)
    return C, Sn


def _rot_lhsT():
    Pm = np.zeros((128, 128), np.float32)
    for i in range(128):
        if (i % 64) < 32:
            Pm[i, i + 32] = -1.0
        else:
            Pm[i, i - 32] = 1.0
    return np.ascontiguousarray(Pm.T)


_CACHE = {}


def _prep_common(inputs, layers):
    L = len(layers)
    sl = lambda k: np.ascontiguousarray(inputs[k][layers])
    com = {
        "w_in": sl("w_in"),
        "w_br": np.ascontiguousarray(np.stack([inputs["w_br_na"][layers], inputs["w_br_gqa"][layers], inputs["w_br_conv"][layers]], axis=1)),
        "w_out": sl("w_out"), "w_gate": sl("w_ffn_gate"), "w_up": sl("w_ffn_up"), "w_down": sl("w_ffn_down"),
        "ident": np.eye(128, dtype=np.float32).astype(NPBF),
        "rotT": _rot_lhsT(),
    }
    gains = np.zeros((L, 128, 64), np.float32)
    qkg = np.zeros((L, 128, 2), np.float32)
    convp = np.zeros((L, 128, 32), np.float32)
    for i, l in enumerate(layers):
        for gi, k in enumerate(["pre_mix_g", "post_mix_g", "pre_ffn_g", "post_ffn_g"]):
            gains[i, :, gi * 16:(gi + 1) * 16] = _fmajor_vec(inputs[k][l])
        qkg[i, :, 0] = inputs["q_norm_g"][l]
        qkg[i, :, 1] = inputs["k_norm_g"][l]
        cw = inputs["conv_w"][l]
        cb = inputs["conv_b"][l]
        for ci in range(8):
            for k in range(3):
                convp[i, :, ci * 4 + k] = cw[k, ci * 128:(ci + 1) * 128]
            convp[i, :, ci * 4 + 3] = cb[ci * 128:(ci + 1) * 128]
    com["gains"], com["qkg"], com["convp"] = gains, qkg, convp
    return com


def _prep_core(inputs, layers, c):
    qtr = c % 4
    C, Sn = _rope_tables(qtr)
    sel = np.zeros((128, 8), np.float32)
    if qtr > 0:
        sel[:, qtr - 1] = 1.0
    if qtr < 3:
        sel[:, 4 + qtr + 1] = 1.0
    nab = np.stack([_na_bias_tables(np.asarray(inputs["na_rpb"][l]), qtr) for l in layers], axis=0)
    return {"ropeC": C, "ropeS": Sn, "sel": sel, "nab": nab}


def _x_to_fmajor(x, c):
    b, qtr = c // 4, c % 4
    xs = x[b, qtr * NTOK:(qtr + 1) * NTOK, :]
    return np.ascontiguousarray(xs.T.reshape(KC, 128, NTOK))


def _run(xTs, inputs, layers):
    L = len(layers)
    if L not in _CACHE:
        _CACHE[L] = build(L)
    nc = _CACHE[L]
    com = _prep_common(inputs, layers)
    in_maps = []
    for c in range(NCORE):
        m = dict(com)
        m.update(_prep_core(inputs, layers, c))
        m["xT"] = xTs[c]
        in_maps.append(m)
    res = run_bass_kernel_spmd(nc, in_maps, core_ids=list(range(NCORE)))
    return [r["yT"] for r in res.results]


LAYERS_PER_LAUNCH = 4


def kernel(**inputs):
    inputs = {k: np.asarray(v) for k, v in inputs.items()}
    x = inputs["x"]
    xTs = [_x_to_fmajor(x, c) for c in range(NCORE)]
    for l0 in range(0, DEPTH, LAYERS_PER_LAUNCH):
        xTs = _run(xTs, inputs, list(range(l0, l0 + LAYERS_PER_LAUNCH)))
    out = np.zeros((2, 4 * NTOK, D), np.float32)
    for c in range(NCORE):
        b, qtr = c // 4, c % 4
        out[b, qtr * NTOK:(qtr + 1) * NTOK, :] = xTs[c].reshape(D, NTOK).T
    return out
```

```python
import numpy as np
import ml_dtypes
from contextlib import ExitStack
import concourse.bass as bass
import concourse.mybir as mybir
from concourse.bass_utils import run_bass_kernel_spmd

F32 = mybir.dt.float32
BF16 = mybir.dt.bfloat16
AF = mybir.ActivationFunctionType
ALU = mybir.AluOpType
NPBF = ml_dtypes.bfloat16

NCORE = 8
NTOK = 2048
D = 2048
KC = 16
FF = 5632
FC = 44
INC = 13824
SCALE = 128.0 ** -0.5
EPS = 1e-6
NEG = -30000.0
ENGS = ["pe", "act", "dve", "pool", "sp"]
DEPTH = 4


class Tok:
    __slots__ = ("sem", "key", "val")

    def __init__(self, sem, key, val):
        self.sem, self.key, self.val = sem, key, val


class _Rec:
    def __getattr__(self, name):
        return lambda *a, **k: (name, a, k)


_REC = _Rec()


class Ctx:
    def __init__(self, nc, es):
        self.nc, self.es = nc, es
        self.esem = {e: es.enter_context(nc.semaphore("m_" + e)) for e in ["pe", "act", "dve", "pool"]}
        self.ecnt = {e: 0 for e in self.esem}
        self.dsem = {}
        self.bar = es.enter_context(nc.semaphore("bar"))
        self.barcnt = 0
        self.seen = {e: {} for e in ENGS}
        self.scr = es.enter_context(nc.sbuf_tensor("scr", [128, 8], F32))
        self.first = True

    def dma_sem(self, name):
        if name not in self.dsem:
            self.dsem[name] = [self.es.enter_context(self.nc.semaphore("d_" + name)), 0]
        return self.dsem[name]


class Phase:
    def __init__(self, ctx):
        self.c = ctx
        self.ops = {e: [] for e in ENGS}
        self.dtoks = {e: {} for e in ENGS}

    def op(self, eng, fn, waits=(), sig=False, chain=True):
        tok = None
        waits = list(waits)
        if eng != "pe":
            sig = True
            if chain and self.c.ecnt[eng] > 0:
                waits.append(Tok(self.c.esem[eng], "m_" + eng, self.c.ecnt[eng]))
        if sig:
            self.c.ecnt[eng] += 1
            tok = Tok(self.c.esem[eng], "m_" + eng, self.c.ecnt[eng])
        name, a, k = fn(_REC)
        self.ops[eng].append((lambda e, name=name, a=a, k=k: getattr(e, name)(*a, **k),
                              tuple(w for w in waits if w is not None), tok, 1))
        return tok

    def dma(self, eng, out, in_, sem, waits=()):
        s = self.c.dma_sem(sem)
        s[1] += 16
        tok = Tok(s[0], "d_" + sem, s[1])
        self.ops[eng].append((lambda e, o=out, i=in_: e.dma_start(out=o, in_=i),
                              tuple(w for w in waits if w is not None), tok, 16))
        self.dtoks[eng][tok.key] = tok
        return tok

    def coll(self, ins, outs, waits=()):
        s = self.c.dma_sem("cc")
        s[1] += 1
        tok = Tok(s[0], "d_cc", s[1])
        fn = lambda e, i=ins, o=outs: e.collective_compute(
            "AllGather", ALU.bypass, replica_groups=[[0, 1, 2, 3], [4, 5, 6, 7]], ins=[i], outs=[o])
        self.ops["pool"].append((fn, tuple(w for w in waits if w is not None), tok, 1))
        self.dtoks["pool"][tok.key] = tok
        return tok

    def run(self):
        c = self.c
        nc = c.nc
        c.barcnt += 4
        barv = c.barcnt

        def emit(eng, e):
            seen = c.seen[eng]

            def wait(t):
                if seen.get(t.key, 0) < t.val:
                    e.wait_ge(t.sem, t.val)
                    seen[t.key] = t.val

            for fn, waits, tok, inc in self.ops[eng]:
                for w in waits:
                    wait(w)
                ins = fn(e)
                if tok is not None:
                    ins.then_inc(tok.sem, inc)
            for t in self.dtoks[eng].values():
                wait(t)
            if eng == "sp":
                e.sem_inc(c.bar, 1)
            elif eng == "act":
                e.memzero(c.scr[:, 0:1]).then_inc(c.bar, 1)
            elif eng == "dve":
                e.memset(c.scr[:, 1:2], 0.0).then_inc(c.bar, 1)
            elif eng == "pool":
                e.memset(c.scr[:, 2:3], 0.0).then_inc(c.bar, 1)
            e.wait_ge(c.bar, barv)

        with nc.Block() as block:
            @block.tensor
            def _(e):
                emit("pe", e)

            @block.scalar
            def _(e):
                emit("act", e)

            @block.vector
            def _(e):
                emit("dve", e)

            @block.gpsimd
            def _(e):
                emit("pool", e)

            @block.sync
            def _(e):
                emit("sp", e)


class Ring:
    def __init__(self, aps):
        self.aps = list(aps)
        self.free = [None] * len(self.aps)
        self.i = 0

    def get(self):
        k = self.i % len(self.aps)
        self.i += 1
        return k, self.aps[k], self.free[k]


def gemm(ph, tiles, wring, wsem, psring, pe_waits=()):
    active = []

    def advance():
        for g in list(active):
            try:
                next(g)
            except StopIteration:
                active.remove(g)

    tiles = list(tiles)

    def issue(tile):
        ws, wap, wfree = wring.get()
        wt = None
        for dst_fn, src in tile["pieces"]:
            wt = ph.dma("pool", dst_fn(wap), src, f"{wsem}{ws}", waits=[wfree])
        return ws, wap, wt

    pending = issue(tiles[0]) if tiles else None
    for ti, tile in enumerate(tiles):
        ws, wap, wt = pending
        if ti + 1 < len(tiles):
            pending = issue(tiles[ti + 1])
        units = tile["units"]
        last_tok = None
        for ui, unit in enumerate(units):
            pk, pap, pfree = psring.get()
            mms = unit["mm"]
            tok = None
            nmm = sum(len(kl) for _, kl in mms)
            cnt = 0
            for bank, klist in mms:
                for ki, (lhs_fn, rhs) in enumerate(klist):
                    cnt += 1
                    lastmm = cnt == nmm
                    tok = ph.op("pe", lambda e, o=pap[:, bank, 0:rhs.shape[-1] if len(rhs.shape) == 2 else 512], l=lhs_fn(wap), r=rhs,
                                a=(ki == 0), b=(ki == len(klist) - 1): e.matmul(o, lhsT=l, rhs=r, start=a, stop=b),
                                waits=[wt, pfree] + list(pe_waits) if cnt == 1 else (), sig=lastmm)
            last_tok = tok

            def setfree(t, k=pk):
                psring.free[k] = t

            advance()
            g = unit["evac"](pap, tok, setfree)
            active.append(g)
            try:
                next(g)
            except StopIteration:
                active.remove(g)
        wring.free[ws] = last_tok
    while active:
        advance()


def build(L, dbg=False, stop_after=None, skip=()):
    nc = bass.Bass("TRN2", target_bir_lowering=False)
    es = ExitStack()
    uid = [0]

    def uname(n):
        uid[0] += 1
        return f"{n}_{uid[0]}"

    def din(name, shape, dt=F32):
        return nc.dram_tensor(name, list(shape), dt, kind="ExternalInput").ap()

    def dscr(name, shape, dt):
        if dbg and name in ("brT_d", "mT_d", "xmid", "qTg_d", "qTna_d", "kTna_d", "vna_d", "hT_d", "uT_d", "bgT_d"):
            return nc.dram_tensor(name, list(shape), dt, kind="ExternalOutput").ap()
        return nc.dram_tensor(name, list(shape), dt).ap()

    xT_in = din("xT", [KC, 128, NTOK])
    w_in = din("w_in", [L, D, INC])
    w_br = din("w_br", [L, 3, 1024, D])
    w_out = din("w_out", [L, D, D])
    small_ffn = stop_after is not None and stop_after <= 9
    w_gate = din("w_gate", [L, 128, 128] if small_ffn else [L, D, FF])
    w_up = din("w_up", [L, 128, 128] if small_ffn else [L, D, FF])
    w_down = din("w_down", [L, 128, 128] if small_ffn else [L, FF, D])
    gains_d = din("gains", [L, 128, 4 * KC])
    qkg_d = din("qkg", [L, 128, 2])
    convp_d = din("convp", [L, 128, 32])
    nab_d = din("nab", [L, 5, 8, 128, 768], BF16)
    ropeC_d = din("ropeC", [128, NTOK])
    ropeS_d = din("ropeS", [128, NTOK])
    ident_d = din("ident", [128, 128], BF16)
    rotT_d = din("rotT", [128, 128])
    sel_d = din("sel", [128, 8])
    yT = nc.dram_tensor("yT", [KC, 128, NTOK], F32, kind="ExternalOutput").ap()

    xmid = dscr("xmid", [KC, 128, NTOK], F32)
    xres = dscr("xres", [KC, 128, NTOK], F32)
    hT_d = dscr("hT_d", [KC, 128, NTOK], BF16)
    qTna_d = dscr("qTna_d", [8, 128, NTOK], BF16)
    kTna_d = dscr("kTna_d", [8, 128, NTOK], BF16)
    vna_d = dscr("vna_d", [NTOK, 1024], BF16)
    qTg_d = dscr("qTg_d", [8, 128, NTOK], BF16)
    send_t = [nc.dram_tensor(f"send{i}_d", [256, 2048], BF16) for i in range(4)]
    recv_t = [nc.dram_tensor(f"recv{i}_d", [1024, 2048], BF16) for i in range(4)]
    ub_d = nc.dram_tensor("ub_d", [256, 8], F32)
    uball_d = nc.dram_tensor("uball_d", [1024, 8], F32)
    uT_d = dscr("uT_d", [8, 128, NTOK], F32)
    bgT_d = dscr("bgT_d", [8, 128, NTOK], F32)
    brT_d = dscr("brT_d", [3, 8, 128, NTOK], BF16)
    mT_d = dscr("mT_d", [KC, 128, NTOK], BF16)
    aT_d = dscr("aT_d", [FC, 128, NTOK], BF16)
    kTg_send = send_t[0].ap()
    vg_send = send_t[1].ap().rearrange("r (a c) -> (r a) c", c=256)
    hk_send = send_t[2].ap().rearrange("r (a c) -> (r a) c", c=256)
    hv_send = send_t[3].ap().rearrange("r (a c) -> (r a) c", c=1024)

    def recv_view(i, r):
        return recv_t[i].ap()[r * 256:(r + 1) * 256, :]

    ctx = Ctx(nc, es)
    S = lambda name, shape, dt: es.enter_context(nc.sbuf_tensor("s_" + name, list(shape), dt))
    psf = es.enter_context(nc.psum_tensor("psf", [128, 6, 512], F32))
    pst = [es.enter_context(nc.psum_tensor("pst0", [128, 1024], BF16)), es.enter_context(nc.psum_tensor("pst1", [128, 1024], BF16))]

    ident = S("ident", [128, 128], BF16)
    rotT = S("rotT", [128, 128], F32)
    onesD = S("onesD", [128, 128], F32)
    onesH = S("onesH", [128, 128], F32)
    sel = S("sel", [128, 8], F32)
    gains = S("gains", [128, L, 4 * KC], F32)
    qkg = S("qkg", [128, L, 2], F32)
    convp = S("convp", [128, L, 32], F32)
    epsT = S("epsT", [128, 1], F32)

    ph = Phase(ctx)
    t0 = ph.dma("sp", ident[:], ident_d, "c0")
    ph.dma("sp", rotT[:], rotT_d, "c0")
    ph.dma("sp", sel[:], sel_d, "c0")
    ph.dma("sp", gains[:], gains_d.rearrange("l p k -> p l k"), "c0")
    ph.dma("sp", qkg[:], qkg_d.rearrange("l p k -> p l k"), "c0")
    ph.dma("sp", convp[:], convp_d.rearrange("l p k -> p l k"), "c0")
    ph.op("pool", lambda e: e.memset(onesD[:], 1.0 / D))
    ph.op("pool", lambda e: e.memset(onesH[:], 1.0 / 128))
    ph.op("pool", lambda e: e.memset(epsT[:], EPS))
    ph.run()

    def rstd_ops(ph, out_ap, in_ap, waits):
        a = ph.op("act", lambda e: e.activation(out=out_ap, in_=in_ap, func=AF.Sqrt, bias=epsT[:, 0:1], scale=1.0), waits=waits, sig=True)
        r = ph.op("dve", lambda e: e.reciprocal(out=out_ap, in_=out_ap), waits=[a], sig=True)
        return a, r

    def phase_norm(X, g_ap):
        with ExitStack() as es2:
            T = lambda name, shape, dt: es2.enter_context(nc.sbuf_tensor(uname(name), list(shape), dt))
            xt = T("n_xt", [128, 2, KC, 512], F32)
            sq = T("n_sq", [128, 2, 4, 512], F32)
            rs = T("n_rs", [128, 2, 512], F32)
            hb = T("n_hb", [128, 2, KC, 512], BF16)
            ph = Phase(ctx)
            xfree = [None, None]
            sqfree = [None, None]
            hbfree = [None, None]
            rsfree = [None, None]
            psfree = [None, None]
            def issue_ld(tg_):
                s_ = tg_ % 2
                ts_ = slice(tg_ * 512, (tg_ + 1) * 512)
                return [ph.dma("sp", xt[:, s_, 4 * q:4 * q + 4, :], X[4 * q:4 * q + 4, :, ts_].rearrange("k p t -> p k t"),
                               f"nx{s_}{q}", waits=[xfree[s_]]) for q in range(4)]

            lds = {0: issue_ld(0), 1: issue_ld(1)}
            for tg in range(4):
                s = tg % 2
                ts = slice(tg * 512, (tg + 1) * 512)
                ld = lds[tg]
                mmtok = None
                for q in range(4):
                    qs = (tg * 4 + q) % 2
                    a = ph.op("act", lambda e, o=sq[:, qs], i=xt[:, s, 4 * q:4 * q + 4, :]: e.activation(out=o, in_=i, func=AF.Square),
                              waits=[ld[q], sqfree[qs]], sig=True)
                    for k in range(4):
                        kc = 4 * q + k
                        mmtok = ph.op("pe", lambda e, o=psf[:, s, :], r=sq[:, qs, k, :], st=(kc == 0), sp=(kc == KC - 1):
                                      e.matmul(o, lhsT=onesD[:], rhs=r, start=st, stop=sp),
                                      waits=[a, psfree[s]] if k == 0 else (), sig=(k == 3))
                    sqfree[qs] = mmtok
                ra, r = rstd_ops(ph, rs[:, s, :], psf[:, s, :], [mmtok, rsfree[s]])
                psfree[s] = ra
                last = {}
                for kc in range(KC):
                    eng = "dve"
                    last[eng] = ph.op(eng, lambda e, o=hb[:, s, kc, :], i=xt[:, s, kc, :], g=g_ap[:, kc:kc + 1], rr=rs[:, s, :]:
                                      e.scalar_tensor_tensor(out=o, in0=i, scalar=g, in1=rr, op0=ALU.mult, op1=ALU.mult),
                                      waits=[r, hbfree[s], ld[3]] if kc < 2 else (), sig=(kc >= KC - 1))
                xfree[s] = None
                st = ph.dma("sp", hT_d[:, :, ts].rearrange("k p t -> p k t"), hb[:, s], f"nh{s}", waits=[last["dve"]])
                hbfree[s] = st
                rsfree[s] = st
                xfree[s] = st
                if tg + 2 < 4:
                    lds[tg + 2] = issue_ld(tg + 2)
            ph.run()

    def phase_proj(l):
        with ExitStack() as es2:
            T = lambda name, shape, dt: es2.enter_context(nc.sbuf_tensor(uname(name), list(shape), dt))
            hT = T("p_hT", [128, KC, NTOK], BF16)
            rC = T("p_rC", [128, NTOK], F32)
            rS = T("p_rS", [128, NTOK], F32)
            wt = T("p_wt", [128, 2, KC, 512], BF16)
            stg = T("p_stg", [128, 3, 1024], BF16)
            stf = T("p_stf", [128, 3, 1024], F32)
            htmp = T("p_htmp", [128, 2, 1024], F32)
            sqb = T("p_sqb", [128, 2, 512], F32)
            rb = T("p_rb", [128, 2, 512], F32)
            qn = T("p_qn", [128, 2, 512], F32)
            t1 = T("p_t1", [128, 2, 512], F32)
            t2 = T("p_t2", [128, 2, 512], F32)
            ub = T("p_ub", [128, 2, 8], F32)
            vst = T("p_vst", [128, 3, 512], BF16)
            ph = Phase(ctx)
            hld = []
            for q in range(4):
                hld.append(ph.dma("sp", hT[:, 4 * q:4 * q + 4, :], hT_d[4 * q:4 * q + 4].rearrange("k p t -> p k t"), "ph"))
            ph.dma("sp", rC[:], ropeC_d, "ph")
            hld_all = ph.dma("sp", rS[:], ropeS_d, "ph")
            wring = Ring([wt[:, 0], wt[:, 1]])
            psring = Ring([psf[:, 0:2, :], psf[:, 2:4, :]])
            stgR = Ring([stg[:, i, :] for i in range(3)])
            stfR = Ring([stf[:, i, :] for i in range(3)])
            auxA = [None]
            auxB = [None]
            cnt = [0]
            pmul = [None, None]
            padd = [None, None]
            W = w_in[l]

            def wpiece(c0, ncols, dst0=0):
                return (lambda wap, d=dst0, n=ncols: wap[:, :, d:d + n],
                        W[:, c0:c0 + ncols].rearrange("(k p) n -> p k n", p=128))

            def mm_units(ncc, evac_of):
                units = []
                for cc in range(ncc):
                    for tb in range(2):
                        mm = []
                        for b in range(2):
                            ts = slice(tb * 1024 + b * 512, tb * 1024 + (b + 1) * 512)
                            mm.append((b, [(lambda wap, k=k, cc=cc: wap[:, k, cc * 128:(cc + 1) * 128], hT[:, k, ts]) for k in range(KC)]))
                        units.append(dict(mm=mm, evac=evac_of(cc, tb)))
                return units

            def ev_simple(dst_of, scale, eng):
                def evac_of(cc, tb):
                    def gen(pap, tok, setfree):
                        k, sap, sfree = stgR.get()
                        if eng == "act":
                            t = ph.op("act", lambda e: e.activation(out=sap, in_=pap.rearrange("p b t -> p (b t)"), func=AF.Copy, scale=scale),
                                      waits=[tok, sfree, hld_all], sig=True)
                        else:
                            t = ph.op("dve", lambda e: e.tensor_copy(out=sap, in_=pap.rearrange("p b t -> p (b t)")),
                                      waits=[tok, sfree, hld_all], sig=True)
                        setfree(t)
                        stgR.free[k] = ph.dma("sp", dst_of(cc)[:, tb * 1024:(tb + 1) * 1024], sap, f"ps{k}", waits=[t])
                        yield
                    return gen
                return evac_of

            def ev_rope(dst_of, gcol):
                def evac_of(cc, tb):
                    def gen(pap, tok, setfree):
                        k, sap, sfree = stgR.get()
                        lastd = None
                        for b in range(2):
                            i = cnt[0] % 2
                            cnt[0] += 1
                            ts = slice(tb * 1024 + b * 512, tb * 1024 + (b + 1) * 512)
                            a = ph.op("act", lambda e, o=sqb[:, i, :], p=pap[:, b, :]: e.activation(out=o, in_=p, func=AF.Square),
                                      waits=[tok, hld_all], sig=True)
                            m = ph.op("pe", lambda e, r=sqb[:, i, :]: e.matmul(psf[:, 4, :], lhsT=onesH[:], rhs=r, start=True, stop=True),
                                      waits=[a, auxA[0]], sig=True)
                            ra_, r_ = rstd_ops(ph, rb[:, i, :], psf[:, 4, :], [m])
                            auxA[0] = ra_
                            q_ = ph.op("dve", lambda e, o=qn[:, i, :], p=pap[:, b, :], rr=rb[:, i, :]:
                                       e.scalar_tensor_tensor(out=o, in0=p, scalar=qkg[:, l, gcol:gcol + 1], in1=rr, op0=ALU.mult, op1=ALU.mult),
                                       waits=[pmul[i]], sig=True)
                            lastd = q_
                            pq = ph.op("pe", lambda e, r=qn[:, i, :]: e.matmul(psf[:, 5, :], lhsT=rotT[:], rhs=r, start=True, stop=True),
                                       waits=[q_, auxB[0]], sig=True)
                            pmul[i] = ph.op("pool", lambda e, o=t1[:, i, :], a_=qn[:, i, :], c_=rC[:, ts]: e.tensor_tensor(out=o, in0=a_, in1=c_, op=ALU.mult),
                                            waits=[q_], sig=True)
                            g_ = ph.op("dve", lambda e, o=t2[:, i, :], s_=rS[:, ts]: e.tensor_tensor(out=o, in0=psf[:, 5, :], in1=s_, op=ALU.mult),
                                       waits=[pq, padd[i]], sig=True)
                            auxB[0] = g_
                            lastp = ph.op("pool", lambda e, o=sap[:, b * 512:(b + 1) * 512], a_=t1[:, i, :], b_=t2[:, i, :]:
                                          e.tensor_tensor(out=o, in0=a_, in1=b_, op=ALU.add), waits=[g_, sfree], sig=True)
                            padd[i] = lastp
                        setfree(lastd)
                        stgR.free[k] = ph.dma("sp", dst_of(cc)[:, tb * 1024:(tb + 1) * 1024], sap, f"ps{k}", waits=[lastp])
                        yield
                    return gen
                return evac_of

            hslot = {}

            def ev_conv(ci):
                def evac_of(cc, tb):
                    def gen(pap, tok, setfree):
                        pflat = pap.rearrange("p b t -> p (b t)")
                        if cc == 0:
                            t = ph.op("act", lambda e: e.activation(out=htmp[:, tb, :], in_=pflat, func=AF.Copy),
                                      waits=[tok, hslot.get(tb)], sig=True)
                            hslot[("h", tb)] = t
                            setfree(t)
                        elif cc == 1:
                            k, sap, sfree = stfR.get()
                            t = ph.op("dve", lambda e: e.tensor_tensor(out=sap, in0=pflat, in1=htmp[:, tb, :], op=ALU.mult),
                                      waits=[tok, sfree, hslot[("h", tb)]], sig=True)
                            hslot[tb] = t
                            setfree(t)
                            col = 0 if tb == 0 else 1023
                            t2_ = ph.op("dve", lambda e: e.tensor_copy(out=ub[:, tb, ci:ci + 1], in_=sap[:, col:col + 1]), sig=True)
                            stfR.free[k] = ph.dma("sp", uT_d[ci][:, tb * 1024:(tb + 1) * 1024], sap, f"pf{k}", waits=[t2_])
                        else:
                            k, sap, sfree = stfR.get()
                            t = ph.op("act", lambda e: e.activation(out=sap, in_=pflat, func=AF.Copy), waits=[tok, sfree], sig=True)
                            setfree(t)
                            stfR.free[k] = ph.dma("sp", bgT_d[ci][:, tb * 1024:(tb + 1) * 1024], sap, f"pf{k}", waits=[t])
                        yield
                    return gen
                return evac_of

            tiles = []
            for j in range(2):
                tiles.append(dict(pieces=[wpiece(j * 512, 512)],
                                  units=mm_units(4, ev_simple(lambda cc, j=j: qTna_d[4 * j + cc], SCALE, "act"))))
            for j in range(2):
                tiles.append(dict(pieces=[wpiece(1024 + j * 512, 512)],
                                  units=mm_units(4, ev_simple(lambda cc, j=j: kTna_d[4 * j + cc], 1.0, "dve"))))
            for j in range(2):
                tiles.append(dict(pieces=[wpiece(3072 + j * 512, 512)],
                                  units=mm_units(4, ev_rope(lambda cc, j=j: qTg_d[4 * j + cc], 0))))
            tiles.append(dict(pieces=[wpiece(4096, 256)],
                              units=mm_units(2, ev_rope(lambda cc: kTg_send[cc * 128:(cc + 1) * 128, :], 1))))
            for ci in range(8):
                tiles.append(dict(pieces=[wpiece(4608 + ci * 128, 128, 0), wpiece(6656 + ci * 128, 128, 128), wpiece(5632 + ci * 128, 128, 256)],
                                  units=mm_units(3, ev_conv(ci))))
            gemm(ph, tiles, wring, "pw", psring, pe_waits=[hld_all])

            vring = Ring([vst[:, i, :] for i in range(3)])
            vps = Ring([psf[:, 0, :], psf[:, 1, :], psf[:, 2, :], psf[:, 3, :]])
            vps.free = [psring.free[0], psring.free[0], psring.free[1], psring.free[1]]
            for (c0, ncols, dst) in [(2048, 512, vna_d[:, 0:512]), (2560, 512, vna_d[:, 512:1024]), (4352, 256, vg_send)]:
                ws, wap, wfree = wring.get()
                wtk = ph.dma("pool", wap[:, :, 0:ncols], W[:, c0:c0 + ncols].rearrange("(k p) n -> p k n", p=128), f"pw{ws}", waits=[wfree])
                tok = None
                for t in range(16):
                    pk, pap, pfree = vps.get()
                    for k in range(KC):
                        tok = ph.op("pe", lambda e, o=pap[:, 0:ncols], l_=hT[:, k, t * 128:(t + 1) * 128], r=wap[:, k, 0:ncols], a=(k == 0), b=(k == KC - 1):
                                    e.matmul(o, lhsT=l_, rhs=r, start=a, stop=b), waits=[wtk, pfree] if k == 0 else (), sig=(k == KC - 1))
                    sk, sap, sfree = vring.get()
                    if t % 2 == 0:
                        ev = ph.op("act", lambda e, o=sap[:, 0:ncols], i=pap[:, 0:ncols]: e.activation(out=o, in_=i, func=AF.Copy), waits=[tok, sfree], sig=True)
                    else:
                        ev = ph.op("dve", lambda e, o=sap[:, 0:ncols], i=pap[:, 0:ncols]: e.tensor_copy(out=o, in_=i), waits=[tok, sfree], sig=True)
                    vps.free[pk] = ev
                    vring.free[sk] = ph.dma("sp", dst[t * 128:(t + 1) * 128, :], sap[:, 0:ncols], f"pv{sk}", waits=[ev])
                wring.free[ws] = tok
            ph.dma("sp", ub_d.ap().rearrange("(w p) c -> p w c", p=128), ub[:], "pub",
                   waits=[Tok(ctx.esem["dve"], "m_dve", ctx.ecnt["dve"])])
            ph.run()

    def phase_exchange():
        ph = Phase(ctx)
        a = ph.dma("sp", hk_send[0:1024, :].rearrange("(h d) t -> h d t", d=128), kTna_d[:, :, 0:256], "xh")
        a = ph.dma("sp", hk_send[1024:2048, :].rearrange("(h d) t -> h d t", d=128), kTna_d[:, :, NTOK - 256:NTOK], "xh")
        a = ph.dma("sp", hv_send[0:256, :], vna_d[0:256, :], "xh")
        a = ph.dma("sp", hv_send[256:512, :], vna_d[NTOK - 256:NTOK, :], "xh")
        c1 = a
        for i in range(4):
            c1 = ph.coll(send_t[i].ap(), recv_t[i].ap(), waits=[c1])
        ph.coll(ub_d.ap(), uball_d.ap(), waits=[c1])
        ph.run()

    def phase_conv(l):
        with ExitStack() as es2:
            T = lambda name, shape, dt: es2.enter_context(nc.sbuf_tensor(uname(name), list(shape), dt))
            ubs = T("c_ub", [128, 4, 2, 8], F32)
            prv = T("c_prv", [128, 8], F32)
            nxt = T("c_nxt", [128, 8], F32)
            uh = T("c_uh", [128, 2, NTOK + 2], F32)
            bg = T("c_bg", [128, 2, NTOK], F32)
            y = T("c_y", [128, 2, NTOK], F32)
            yb = T("c_yb", [128, 2, NTOK], BF16)
            ph = Phase(ctx)
            ld = ph.dma("sp", ubs[:], uball_d.ap().rearrange("(r w p) c -> p r w c", r=4, w=2), "cu")
            tk = None
            for r in range(4):
                if r == 0:
                    ph.op("dve", lambda e: e.tensor_scalar(prv[:], ubs[:, 0, 1, :], sel[:, 0:1], None, ALU.mult), waits=[ld])
                    tk = ph.op("dve", lambda e: e.tensor_scalar(nxt[:], ubs[:, 0, 0, :], sel[:, 4:5], None, ALU.mult), sig=True)
                else:
                    ph.op("dve", lambda e, r=r: e.scalar_tensor_tensor(out=prv[:], in0=ubs[:, r, 1, :], scalar=sel[:, r:r + 1], in1=prv[:], op0=ALU.mult, op1=ALU.add), waits=[tk])
                    tk = ph.op("dve", lambda e, r=r: e.scalar_tensor_tensor(out=nxt[:], in0=ubs[:, r, 0, :], scalar=sel[:, 4 + r:5 + r], in1=nxt[:], op0=ALU.mult, op1=ALU.add), sig=True)
            free = [None, None]
            for ci in range(8):
                s = ci % 2
                eng = "dve"
                l1 = ph.dma("sp", uh[:, s, 1:NTOK + 1], uT_d[ci], f"cl{s}", waits=[free[s]])
                l2 = ph.dma("sp", bg[:, s, :], bgT_d[ci], f"cl{s}", waits=[free[s]])
                cp = lambda k: convp[:, l, ci * 4 + k:ci * 4 + k + 1]
                a = ph.op("dve", lambda e: e.tensor_copy(out=uh[:, s, 0:1], in_=prv[:, ci:ci + 1]), waits=[tk, l2])
                a = ph.op("dve", lambda e: e.tensor_copy(out=uh[:, s, NTOK + 1:NTOK + 2], in_=nxt[:, ci:ci + 1]), sig=True)
                ph.op(eng, lambda e: e.tensor_scalar(y[:, s, :], uh[:, s, 1:NTOK + 1], cp(1), cp(3), ALU.mult, ALU.add), waits=[a, l2])
                b = ph.op(eng, lambda e: e.scalar_tensor_tensor(out=y[:, s, :], in0=uh[:, s, 0:NTOK], scalar=cp(0), in1=y[:, s, :], op0=ALU.mult, op1=ALU.add), sig=True)
                b = ph.op(eng, lambda e: e.scalar_tensor_tensor(out=y[:, s, :], in0=uh[:, s, 2:NTOK + 2], scalar=cp(2), in1=y[:, s, :], op0=ALU.mult, op1=ALU.add), waits=[b], sig=True)
                b = ph.op(eng, lambda e: e.tensor_tensor(out=yb[:, s, :], in0=y[:, s, :], in1=bg[:, s, :], op=ALU.mult), waits=[b], sig=True)
                free[s] = ph.dma("sp", brT_d[2, ci], yb[:, s, :], f"cs{s}", waits=[b])
            ph.run()

    def halo_select(ph, eng, dst, src_of, selbase, ld):
        tk = ph.op(eng, lambda e: e.tensor_scalar(dst, src_of(0), sel[:, selbase:selbase + 1], None, ALU.mult),
                   waits=[ld, Tok(ctx.esem["pool"], "m_pool", ctx.ecnt["pool"])], sig=True)
        for r in range(1, 4):
            tk = ph.op(eng, lambda e, r=r: e.scalar_tensor_tensor(out=dst, in0=src_of(r), scalar=sel[:, selbase + r:selbase + r + 1], in1=dst,
                                                                   op0=ALU.mult, op1=ALU.add), waits=[tk], sig=True)
        return tk

    def attn_finish(ph, o_ps, pe_tok, ob_ap, ob_free, tp_ap, tp_free, dst_ap, dst_free, evac_eng):
        ph.op("dve", lambda e: e.reciprocal(out=ob_ap["r"], in_=o_ps[:, 128:129]), waits=[pe_tok, ob_free])
        n = ph.op("dve", lambda e: e.tensor_scalar(ob_ap["o"], o_ps[:, 0:128], ob_ap["r"], None, ALU.mult), sig=True)
        t = ph.op("pe", lambda e: e.transpose(out=tp_ap, in_=ob_ap["o"], identity=ident[:]), waits=[n, tp_free], sig=True)
        if evac_eng == "act":
            c = ph.op("act", lambda e: e.activation(out=dst_ap, in_=tp_ap, func=AF.Copy), waits=[t, dst_free], sig=True)
        else:
            c = ph.op("dve", lambda e: e.tensor_copy(out=dst_ap, in_=tp_ap), waits=[t, dst_free], sig=True)
        return n, t, c

    def phase_na(l):
        with ExitStack() as es2:
            T = lambda name, shape, dt: es2.enter_context(nc.sbuf_tensor(uname(name), list(shape), dt))
            qT = T("a_qT", [128, 8, NTOK], BF16)
            Kb = T("a_K", [128, 8, 20 * 128], BF16)
            Vb = T("a_V", [128, 20, 8, 129], BF16)
            tmpk = T("a_tk", [128, 4, 8, 256], BF16)
            tmpv = T("a_tv", [128, 4, 2, 1024], BF16)
            bias = T("a_bias", [128, 3, 768], BF16)
            PT = T("a_PT", [128, 2, 768], BF16)
            ob = T("a_ob", [128, 2, 128], BF16)
            rr = T("a_rr", [128, 2, 1], F32)
            ost = T("a_ost", [128, 2, 8, 128], BF16)
            ph = Phase(ctx)
            ms = ph.op("pool", lambda e: e.memset(Vb[:], 1.0), sig=True)
            lq = ph.dma("sp", qT[:], qTna_d.rearrange("h d t -> d h t"), "aq")
            lk = ph.dma("sp", Kb[:, :, 256:256 + NTOK], kTna_d.rearrange("h d t -> d h t"), "aq")
            lv = None
            for h8 in range(8):
                lv = ph.dma("sp", Vb[:, 2:18, h8, 0:128],
                            vna_d[:, 128 * h8:128 * h8 + 128].rearrange("(c p) d -> p c d", p=128), "aq", waits=[ms])
            for (w, selb, kdst, vdst) in [(1, 0, Kb[:, :, 0:256], Vb[:, 0:2, :, 0:128]), (0, 4, Kb[:, :, 2304:2560], Vb[:, 18:20, :, 0:128])]:
                ldk = ldv = None
                for r in range(4):
                    hk_r = recv_view(2, r).rearrange("r (a c) -> (r a) c", c=256)
                    hv_r = recv_view(3, r).rearrange("r (a c) -> (r a) c", c=1024)
                    ldk = ph.dma("sp", tmpk[:, r], hk_r[w * 1024:(w + 1) * 1024, :].rearrange("(h d) t -> d h t", d=128), f"ah{w}",
                                 waits=[Tok(ctx.esem["dve"], "m_dve", ctx.ecnt["dve"]), Tok(ctx.esem["pool"], "m_pool", ctx.ecnt["pool"])])
                    ldv = ph.dma("sp", tmpv[:, r], hv_r[w * 256:(w + 1) * 256, :].rearrange("(c p) n -> p c n", p=128), f"ah{w}",
                                 waits=[Tok(ctx.esem["dve"], "m_dve", ctx.ecnt["dve"]), Tok(ctx.esem["pool"], "m_pool", ctx.ecnt["pool"])])
                halo_select(ph, "dve", kdst, lambda r: tmpk[:, r], selb, ldv)
                halo_select(ph, "dve", vdst, lambda r: tmpv[:, r].rearrange("p c (h d) -> p c h d", d=128), selb, ldv)
            ready = [lv, Tok(ctx.esem["dve"], "m_dve", ctx.ecnt["dve"]), Tok(ctx.esem["pool"], "m_pool", ctx.ecnt["pool"])]

            units = [(lp, h) for lp in range(16) for h in range(8)]
            N = len(units)
            slot_of = lambda lp: 0 if lp == 0 else 1 if lp == 1 else 3 if lp == 14 else 4 if lp == 15 else 2
            bfree = [None] * 3
            btok = [None] * N
            sfree = [None, None]
            ptfree = [None, None]
            ofree = [None, None]
            obfree = [None, None]
            tpfree = [None, None]
            ostfree = [None, None]
            qk_tok = [None] * N
            ex_tok = [None] * N
            pv_tok = [None] * N

            def load_bias(u):
                lp, h = units[u]
                btok[u] = ph.dma("sp", bias[:, u % 3, :], nab_d[l, slot_of(lp), h], f"ab{u % 3}", waits=[bfree[u % 3]])

            load_bias(0)
            load_bias(1)
            nrm_tok = [None] * N
            for step in range(N + 3):
                if step + 2 < N:
                    load_bias(step + 2)
                if step < N:
                    u = step
                    lp, h = units[u]
                    s = u % 2
                    wlo = max(lp - 1, 0)
                    tok = None
                    for j in range(6):
                        o = psf[:, 2 * s + j // 4, (j % 4) * 128:(j % 4 + 1) * 128]
                        ph.op("pe", lambda e, o=o, k=Kb[:, h, (wlo + j) * 128:(wlo + j + 1) * 128], q=qT[:, h, lp * 128:(lp + 1) * 128]:
                              e.matmul(o, lhsT=k, rhs=q, start=True, stop=False), waits=ready + [sfree[s], btok[u]] if j == 0 else ())
                        tok = ph.op("pe", lambda e, o=o, b_=bias[:, u % 3, j * 128:(j + 1) * 128]: e.matmul(o, lhsT=ident[:], rhs=b_, start=False, stop=True),
                                    sig=(j == 5))
                    qk_tok[u] = tok
                    bfree[u % 3] = tok
                    ph.op("act", lambda e, s=s: e.activation(out=PT[:, s, 0:512], in_=psf[:, 2 * s, :], func=AF.Exp), waits=[tok, ptfree[s]])
                    ex_tok[u] = ph.op("act", lambda e, s=s: e.activation(out=PT[:, s, 512:768], in_=psf[:, 2 * s + 1, 0:256], func=AF.Exp), sig=True)
                    sfree[s] = ex_tok[u]
                if 0 <= step - 1 < N:
                    u = step - 1
                    lp, h = units[u]
                    s = u % 2
                    wlo = max(lp - 1, 0)
                    tok = None
                    for j in range(6):
                        tok = ph.op("pe", lambda e, s=s, j=j, v=Vb[:, wlo + j, h, :]: e.matmul(psf[:, 4 + s, 0:129], lhsT=PT[:, s, j * 128:(j + 1) * 128], rhs=v,
                                                                                             start=(j == 0), stop=(j == 5)),
                                    waits=[ex_tok[u], ofree[s]] if j == 0 else (), sig=(j == 5))
                    pv_tok[u] = tok
                    ptfree[s] = tok
                    ph.op("dve", lambda e, s=s: e.reciprocal(out=rr[:, s, :], in_=psf[:, 4 + s, 128:129]), waits=[tok, obfree[s]])
                    nrm_tok[u] = ph.op("dve", lambda e, s=s: e.tensor_scalar(ob[:, s, :], psf[:, 4 + s, 0:128], rr[:, s, :], None, ALU.mult), sig=True)
                    ofree[s] = nrm_tok[u]
                if 0 <= step - 2 < N:
                    u = step - 2
                    lp, h = units[u]
                    s = u % 2
                    tp_ap = pst[s][:, 0:128]
                    t = ph.op("pe", lambda e, s=s, tp_ap=tp_ap: e.transpose(out=tp_ap, in_=ob[:, s, :], identity=ident[:]), waits=[nrm_tok[u], tpfree[s]], sig=True)
                    obfree[s] = t
                    dst_ap = ost[:, lp % 2, h, :]
                    dfree = ostfree[lp % 2] if h <= 1 else None
                    if u % 2 == 0:
                        c = ph.op("dve", lambda e, d=dst_ap, tp_ap=tp_ap: e.tensor_copy(out=d, in_=tp_ap), waits=[t, dfree], sig=True)
                    else:
                        c = ph.op("act", lambda e, d=dst_ap, tp_ap=tp_ap: e.activation(out=d, in_=tp_ap, func=AF.Copy), waits=[t, dfree], sig=True)
                    tpfree[s] = c
                    if h == 7:
                        cprev = Tok(ctx.esem["dve"], "m_dve", ctx.ecnt["dve"])
                        cact = Tok(ctx.esem["act"], "m_act", ctx.ecnt["act"])
                        ostfree[lp % 2] = ph.dma("sp", brT_d[0].rearrange("h d t -> d h t")[:, :, lp * 128:(lp + 1) * 128], ost[:, lp % 2], f"ao{lp % 2}",
                                                 waits=[c, cprev, cact])
            ph.run()

    def phase_gqa():
        with ExitStack() as es2:
            T = lambda name, shape, dt: es2.enter_context(nc.sbuf_tensor(uname(name), list(shape), dt))
            qT = T("g_qT", [128, 8, NTOK], BF16)
            KT = T("g_KT", [128, 4, 2, NTOK], BF16)
            Vb = T("g_V", [128, 64, 2, 129], BF16)
            PT = T("g_PT", [128, 3, 512], BF16)
            ob = T("g_ob", [128, 2, 128], BF16)
            rr = T("g_rr", [128, 2, 1], F32)
            ost = T("g_ost", [128, 2, 4, 128], BF16)
            ph = Phase(ctx)
            ms = ph.op("pool", lambda e: e.memset(Vb[:], 1.0), sig=True)
            lds = [ph.dma("sp", qT[:], qTg_d.rearrange("h d t -> d h t"), "gq")]
            for r in range(4):
                lds.append(ph.dma("sp", KT[:, r], recv_view(0, r).rearrange("(k d) t -> d k t", d=128), "gq"))
                vg_r = recv_view(1, r).rearrange("r (a c) -> (r a) c", c=256)
                for k2 in range(2):
                    lds.append(ph.dma("sp", Vb[:, 16 * r:16 * r + 16, k2, 0:128], vg_r[:, k2 * 128:(k2 + 1) * 128].rearrange("(c p) d -> p c d", p=128), "gq", waits=[ms]))
            ready = [lds[-1]]
            blocks = [(g, t, kc) for g in range(2) for t in range(16) for kc in range(64)]
            N = len(blocks)
            sfree = [None, None]
            ptfree = [None] * 3
            ofree = [None] * 4
            obfree = [None, None]
            tpfree = [None, None]
            ostfree = [None, None]
            qk_tok = [None] * N
            ex_tok = [None] * N
            fin = 0
            for step in range(N + 1):
                if step < N:
                    g, t, kc = blocks[step]
                    s = step % 2
                    qk_tok[step] = ph.op("pe", lambda e, s=s, k=KT[:, kc // 16, g, (kc % 16) * 128:(kc % 16 + 1) * 128], q=qT[:, 4 * g:4 * g + 4, t * 128:(t + 1) * 128]:
                                         e.matmul(psf[:, s, :], lhsT=k, rhs=q, start=True, stop=True), waits=ready + [sfree[s]], sig=True)
                    p = step % 3
                    ex_tok[step] = ph.op("act", lambda e, s=s, p=p: e.activation(out=PT[:, p, :], in_=psf[:, s, :], func=AF.Exp, scale=SCALE),
                                         waits=[qk_tok[step], ptfree[p]], sig=True, chain=False)
                    sfree[s] = ex_tok[step]
                if step >= 1:
                    b = step - 1
                    g, t, kc = blocks[b]
                    p = b % 3
                    tok = None
                    for hh in range(4):
                        tok = ph.op("pe", lambda e, hh=hh, p=p, v=Vb[:, kc, g, :]: e.matmul(psf[:, 2 + hh, 0:129], lhsT=PT[:, p, hh * 128:(hh + 1) * 128], rhs=v,
                                                                                          start=(kc == 0), stop=(kc == 63)),
                                    waits=[ex_tok[b]] + ([ofree[hh]] if kc == 0 else []) if hh == 0 or kc == 0 else (), sig=(hh == 3))
                    ptfree[p] = tok
                    if kc == 63:
                        gi = g * 16 + t
                        for hh in range(4):
                            s2 = fin % 2
                            fin += 1
                            obd = dict(o=ob[:, s2, :], r=rr[:, s2, :])
                            n, tt, c = attn_finish(ph, psf[:, 2 + hh, :], tok, obd, obfree[s2], pst[s2][:, 0:128], tpfree[s2],
                                                   ost[:, gi % 2, hh, :], ostfree[gi % 2] if hh == 0 else None, "dve")
                            ofree[hh] = n
                            obfree[s2] = tt
                            tpfree[s2] = c
                        ostfree[gi % 2] = ph.dma("sp", brT_d[1].rearrange("h d t -> d h t")[:, 4 * g:4 * g + 4, t * 128:(t + 1) * 128], ost[:, gi % 2], f"go{gi % 2}",
                                                 waits=[c])
            ph.run()

    def phase_merge(l):
        for half in range(2):
            with ExitStack() as es2:
                T = lambda name, shape, dt: es2.enter_context(nc.sbuf_tensor(uname(name), list(shape), dt))
                hT = T("m_hT", [128, KC, 1024], BF16)
                bT = T("m_bT", [128, 3, 8, 1024], BF16)
                wt = T("m_wt", [128, 2, 24, 256], BF16)
                sg = T("m_sg", [128, 2, 1024], F32)
                tmp = T("m_tmp", [128, 2, 1024], F32)
                macc = T("m_acc", [128, 2, 2, 1024], F32)
                mb = T("m_mb", [128, 2, 1024], BF16)
                ph = Phase(ctx)
                hs = slice(half * 1024, (half + 1) * 1024)
                ld = None
                for q in range(4):
                    ld = ph.dma("sp", hT[:, 4 * q:4 * q + 4, :], hT_d[4 * q:4 * q + 4, :, hs].rearrange("k p t -> p k t"), "mh")
                for b in range(3):
                    ld = ph.dma("sp", bT[:, b], brT_d[b, :, :, hs].rearrange("k p t -> p k t"), "mh")
                wring = Ring([wt[:, 0], wt[:, 1]])
                psring = Ring([psf[:, 0:2, :], psf[:, 2:4, :], psf[:, 4:6, :]])
                sgR = Ring([sg[:, 0, :], sg[:, 1, :]])
                tmpR = Ring([tmp[:, 0, :], tmp[:, 1, :]])
                mbR = Ring([mb[:, 0, :], mb[:, 1, :]])
                state = {}
                accfree = {}
                tiles = []
                for ct in range(8):
                    for b in range(3):
                        gc0 = 7680 + b * 2048 + ct * 256
                        pieces = [(lambda wap: wap[:, 0:16, :], w_in[l][:, gc0:gc0 + 256].rearrange("(k p) n -> p k n", p=128)),
                                  (lambda wap: wap[:, 16:24, :], w_br[l, b][:, ct * 256:(ct + 1) * 256].rearrange("(k p) n -> p k n", p=128))]
                        units = []
                        for cc in range(2):
                            def evG(pap, tok, setfree, ct=ct, b=b, cc=cc):
                                k, sap, sfree_ = sgR.get()
                                t = ph.op("act", lambda e: e.activation(out=sap, in_=pap.rearrange("p b t -> p (b t)"), func=AF.Sigmoid),
                                          waits=[tok, sfree_, ld], sig=True)
                                setfree(t)
                                state[(ct, b, cc)] = (k, sap, t)
                                yield

                            def evY(pap, tok, setfree, ct=ct, b=b, cc=cc):
                                k, sap, st = state[(ct, b, cc)]
                                pflat = pap.rearrange("p b t -> p (b t)")
                                acc = macc[:, ct % 2, cc, :]
                                if b == 0:
                                    t = ph.op("dve", lambda e: e.tensor_tensor(out=acc, in0=pflat, in1=sap, op=ALU.mult),
                                              waits=[tok, st, accfree.get((ct % 2, cc))], sig=True)
                                    setfree(t)
                                    sgR.free[k] = t
                                else:
                                    tk_, tap, tfree = tmpR.get()
                                    t = ph.op("dve", lambda e: e.tensor_tensor(out=tap, in0=pflat, in1=sap, op=ALU.mult),
                                              waits=[tok, st, tfree], sig=True)
                                    setfree(t)
                                    sgR.free[k] = t
                                    if b == 1:
                                        t2_ = ph.op("pool", lambda e: e.tensor_tensor(out=acc, in0=acc, in1=tap, op=ALU.add), waits=[t], sig=True)
                                        tmpR.free[tk_] = t2_
                                    else:
                                        mk, map_, mfree = mbR.get()
                                        t2_ = ph.op("pool", lambda e: e.tensor_tensor(out=map_, in0=acc, in1=tap, op=ALU.add), waits=[t, mfree], sig=True)
                                        tmpR.free[tk_] = t2_
                                        accfree[(ct % 2, cc)] = t2_
                                        mbR.free[mk] = ph.dma("sp", mT_d[ct * 2 + cc][:, hs], map_, f"mm{mk}", waits=[t2_])
                                yield

                            mmG = [(bk, [(lambda wap, k=k, cc=cc: wap[:, k, cc * 128:(cc + 1) * 128], hT[:, k, bk * 512:(bk + 1) * 512]) for k in range(16)])
                                   for bk in range(2)]
                            mmY = [(bk, [(lambda wap, k=k, cc=cc: wap[:, 16 + k, cc * 128:(cc + 1) * 128], bT[:, b, k, bk * 512:(bk + 1) * 512]) for k in range(8)])
                                   for bk in range(2)]
                            units.append(dict(mm=mmG, evac=evG))
                            units.append(dict(mm=mmY, evac=evY))
                        tiles.append(dict(pieces=pieces, units=units))
                gemm(ph, tiles, wring, "mw", psring, pe_waits=[ld])
                ph.run()

    def phase_proj_res(l, KCn, act_d, Wd, TB, gcol, Xsrc, Xdst, wcols):
        nb = TB // 512
        for blk in range(NTOK // TB):
            with ExitStack() as es2:
                T = lambda name, shape, dt: es2.enter_context(nc.sbuf_tensor(uname(name), list(shape), dt))
                ksplit = (KCn == FC)
                NSQ = 4 if ksplit else 2
                aT = T("r_aT", [128, KCn, TB], BF16)
                if ksplit:
                    wtk = T("r_wtk", [128, 3, 11, 512], BF16)
                    wt = T("r_wt", [128, 2, 1, 128], BF16)
                else:
                    wt = T("r_wt", [128, 2, KCn, wcols], BF16)
                z = T("r_z", [128, KC, TB], F32)
                sq = T("r_sq", [128, NSQ, TB], F32)
                rs = T("r_rs", [128, TB], F32)
                NX = 4
                xt = T("r_xt", [128, NX, TB], F32)
                t1 = T("r_t1", [128, 2, TB], F32)
                xo = T("r_xo", [128, 2, TB], F32)
                ph = Phase(ctx)
                bs = slice(blk * TB, (blk + 1) * TB)
                ld = None
                step = 4 if KCn % 4 == 0 else 1
                for q in range(0, KCn, step):
                    ld = ph.dma("sp", aT[:, q:q + step, :], act_d[q:q + step, :, bs].rearrange("k p t -> p k t"), "ra")
                xfree = [None] * NX
                lxs = {}

                def load_x(c_):
                    s4_ = c_ % NX
                    lxs[c_] = ph.dma("sp", xt[:, s4_, :], Xsrc[c_][:, bs], f"rx{s4_}", waits=[xfree[s4_]])

                for c_ in range(NX):
                    load_x(c_)
                wring = Ring([wt[:, 0], wt[:, 1]])
                psring = Ring([psf[:, 0:nb, :], psf[:, 2:2 + nb, :]])
                sqR = Ring([sq[:, i, :] for i in range(NSQ)])
                ssfree = [None]
                sstok = [None]
                tiles = []
                cpt = wcols // 128
                if ksplit:
                    KPG, NKG = 11, 4
                    wslot_free = [None] * 3
                    bankfree = [None] * 4
                    pend = []
                    wtiles = [(cg, kg) for cg in range(4) for kg in range(NKG)]
                    wtoks = {}

                    def issue_w(i):
                        cg, kg = wtiles[i]
                        sl_ = i % 3
                        wtoks[i] = ph.dma("pool", wtk[:, sl_], Wd[kg * KPG * 128:(kg + 1) * KPG * 128, cg * 512:(cg + 1) * 512].rearrange("(k p) n -> p k n", p=128),
                                          f"rk{sl_}", waits=[wslot_free[sl_]])

                    def flush():
                        for (c_, k_, sap_, a1_) in pend:
                            tk_ = ph.op("pe", lambda e: e.matmul(psf[:, 4, :], lhsT=onesD[:], rhs=sap_, start=(c_ == 0), stop=(c_ == KC - 1)), waits=[a1_], sig=True)
                            sqR.free[k_] = tk_
                            sstok[0] = tk_
                        pend.clear()

                    issue_w(0)
                    issue_w(1)
                    for ti, (cg, kg) in enumerate(wtiles):
                        if ti + 2 < len(wtiles):
                            issue_w(ti + 2)
                        sl_ = ti % 3
                        last = None
                        for cc in range(4):
                            for k in range(KPG):
                                first = (kg == 0 and k == 0)
                                lastmm = (kg == NKG - 1 and k == KPG - 1)
                                w_ = []
                                if k == 0 and cc == 0:
                                    w_ += [wtoks[ti], ld]
                                if first:
                                    w_.append(bankfree[cc])
                                last = ph.op("pe", lambda e: e.matmul(psf[:, cc, :], lhsT=wtk[:, sl_, k, cc * 128:(cc + 1) * 128], rhs=aT[:, kg * KPG + k, :],
                                                                      start=first, stop=lastmm), waits=w_, sig=(k == KPG - 1))
                            if kg == NKG - 1:
                                c = cg * 4 + cc
                                a2 = ph.op("dve", lambda e: e.tensor_copy(out=z[:, c, :], in_=psf[:, cc, :]), waits=[last], sig=True)
                                bankfree[cc] = a2
                                k_, sap, sfree_ = sqR.get()
                                a1 = ph.op("act", lambda e: e.activation(out=sap, in_=z[:, c, :], func=AF.Square), waits=[a2, sfree_], sig=True)
                                pend.append((c, k_, sap, a1))
                        wslot_free[sl_] = last
                        if kg == 0 and pend:
                            flush()
                    flush()
                for ct in range(0 if ksplit else D // wcols):
                    units = []
                    for cc in range(cpt):
                        c = ct * cpt + cc

                        def ev(pap, tok, setfree, c=c):
                            pflat = pap.rearrange("p b t -> p (b t)")
                            k, sap, sfree_ = sqR.get()
                            a2 = ph.op("dve", lambda e: e.tensor_copy(out=z[:, c, :], in_=pflat), waits=[tok], sig=True)
                            a1 = ph.op("act", lambda e: e.activation(out=sap, in_=z[:, c, :], func=AF.Square), waits=[a2, sfree_, ld], sig=True)
                            setfree(a2)
                            psring.free[(psring.i - 1) % 2] = a2
                            yield
                            tk = None
                            for bk in range(nb):
                                tk = ph.op("pe", lambda e, bk=bk: e.matmul(psf[:, 4 + bk, :], lhsT=onesD[:], rhs=sap[:, bk * 512:(bk + 1) * 512],
                                                                         start=(c == 0), stop=(c == KC - 1)), waits=[a1, a2], sig=(bk == nb - 1))
                            sqR.free[k] = tk
                            sstok[0] = tk
                            yield

                        mm = [(bk, [(lambda wap, k=k, cc=cc: wap[:, k, cc * 128:(cc + 1) * 128], aT[:, k, bk * 512:(bk + 1) * 512]) for k in range(KCn)])
                              for bk in range(nb)]
                        units.append(dict(mm=mm, evac=ev))
                    tiles.append(dict(pieces=[(lambda wap: wap, Wd[:, ct * wcols:(ct + 1) * wcols].rearrange("(k p) n -> p k n", p=128))], units=units))
                if tiles:
                    gemm(ph, tiles, wring, "rw", psring, pe_waits=[ld])
                ra, r = rstd_ops(ph, rs[:], psf[:, 4:4 + nb, :].rearrange("p b t -> p (b t)"), [sstok[0]])
                ofree = [None, None]
                t1free = [None, None]
                for c in range(KC):
                    s = c % 2
                    s4 = c % NX
                    a = ph.op("dve", lambda e: e.scalar_tensor_tensor(out=t1[:, s, :], in0=z[:, c, :], scalar=gains[:, l, gcol * KC + c:gcol * KC + c + 1],
                                                                      in1=rs[:], op0=ALU.mult, op1=ALU.mult), waits=[r, t1free[s]], sig=True)
                    b = ph.op("pool" if c % 2 == 0 else "dve", lambda e: e.tensor_tensor(out=xo[:, s, :], in0=t1[:, s, :], in1=xt[:, s4, :], op=ALU.add),
                              waits=[a, lxs[c], ofree[s]], sig=True)
                    t1free[s] = b
                    xfree[s4] = b
                    ofree[s] = ph.dma("act", Xdst[c][:, bs], xo[:, s, :], f"ro{s}", waits=[b])
                    if c + NX < KC:
                        load_x(c + NX)
                ph.run()

    def phase_ffn_up(l):
        with ExitStack() as es2:
            T = lambda name, shape, dt: es2.enter_context(nc.sbuf_tensor(uname(name), list(shape), dt))
            hT = T("f_hT", [128, KC, NTOK], BF16)
            wt = T("f_wt", [128, 2, 32, 256], BF16)
            sl = T("f_sl", [128, 2, 1024], F32)
            ab = T("f_ab", [128, 3, 1024], BF16)
            ph = Phase(ctx)
            ld = None
            for q in range(4):
                ld = ph.dma("sp", hT[:, 4 * q:4 * q + 4, :], hT_d[4 * q:4 * q + 4].rearrange("k p t -> p k t"), "fh")
            wring = Ring([wt[:, 0], wt[:, 1]])
            psring = Ring([psf[:, 0:2, :], psf[:, 2:4, :], psf[:, 4:6, :]])
            slR = Ring([sl[:, 0, :], sl[:, 1, :]])
            abR = Ring([ab[:, i, :] for i in range(3)])
            state = {}
            tiles = []
            for jt in range(FF // 256):
                pieces = [(lambda wap: wap[:, 0:16, :], w_gate[l][:, jt * 256:(jt + 1) * 256].rearrange("(k p) n -> p k n", p=128)),
                          (lambda wap: wap[:, 16:32, :], w_up[l][:, jt * 256:(jt + 1) * 256].rearrange("(k p) n -> p k n", p=128))]
                units = []
                for tb in range(2):
                    for cc in range(2):
                        j = jt * 2 + cc

                        def evG(pap, tok, setfree, key=(jt, tb, cc)):
                            k, sap, sfree_ = slR.get()
                            t = ph.op("act", lambda e: e.activation(out=sap, in_=pap.rearrange("p b t -> p (b t)"), func=AF.Silu), waits=[tok, sfree_, ld], sig=True)
                            setfree(t)
                            state[key] = (k, sap, t)
                            yield

                        def evU(pap, tok, setfree, key=(jt, tb, cc), j=j, tb=tb):
                            k, sap, st = state[key]
                            ak, aap, afree = abR.get()
                            t = ph.op("dve", lambda e: e.tensor_tensor(out=aap, in0=pap.rearrange("p b t -> p (b t)"), in1=sap, op=ALU.mult),
                                      waits=[tok, st, afree], sig=True)
                            setfree(t)
                            slR.free[k] = t
                            abR.free[ak] = ph.dma("sp", aT_d[j][:, tb * 1024:(tb + 1) * 1024], aap, f"fa{ak}", waits=[t])
                            yield

                        mmG = [(bk, [(lambda wap, k=k, cc=cc: wap[:, k, cc * 128:(cc + 1) * 128], hT[:, k, tb * 1024 + bk * 512:tb * 1024 + (bk + 1) * 512])
                                     for k in range(16)]) for bk in range(2)]
                        mmU = [(bk, [(lambda wap, k=k, cc=cc: wap[:, 16 + k, cc * 128:(cc + 1) * 128], hT[:, k, tb * 1024 + bk * 512:tb * 1024 + (bk + 1) * 512])
                                     for k in range(16)]) for bk in range(2)]
                        units.append(dict(mm=mmG, evac=evG))
                        units.append(dict(mm=mmU, evac=evU))
                tiles.append(dict(pieces=pieces, units=units))
            gemm(ph, tiles, wring, "fw", psring, pe_waits=[ld])
            ph.run()

    for l in range(L):
        X = xT_in if l == 0 else xres
        Xout = yT if l == L - 1 else xres
        plist = [
            lambda: phase_norm(X, gains[:, l, 0:KC]),
            lambda: phase_proj(l),
            lambda: phase_exchange(),
            lambda: phase_conv(l),
            lambda: phase_na(l),
            lambda: phase_gqa(),
            lambda: phase_merge(l),
            lambda: phase_proj_res(l, KC, mT_d, w_out[l], 1024, 1, X, xmid, 512),
            lambda: phase_norm(xmid, gains[:, l, 2 * KC:3 * KC]),
            lambda: phase_ffn_up(l),
            lambda: phase_proj_res(l, FC, aT_d, w_down[l], 512, 3, xmid, Xout, 128),
        ]
        for pi, pf in enumerate(plist):
            if (stop_after is None or pi < stop_after) and pi not in skip:
                pf()
    es.close()
    return nc


def _fmajor_vec(v):
    return np.ascontiguousarray(v.reshape(-1, 128).T)


def _na_bias_tables(rpb, qtr):
    out = np.full((5, 8, 128, 768), NEG, dtype=np.float32)
    kk = np.arange(128)
    qq = np.arange(128)
    for si, lp in enumerate([0, 1, 2, 14, 15]):
        p = qtr * 16 + lp
        wlo = max(lp - 1, 0)
        g0 = qtr * 16 - 2 + wlo
        qrow = 2 * p + qq // 64
        qcol = qq % 64
        rstart = np.clip(qrow - 4, 0, 120)
        cstart = np.clip(qcol - 8, 0, 48)
        for j in range(6):
            gc = g0 + j
            if gc < 0 or gc > 63:
                continue
            krow = 2 * gc + kk // 64
            kcol = kk % 64
            dr = krow[:, None] - qrow[None, :]
            dc = kcol[:, None] - qcol[None, :]
            valid = ((krow[:, None] >= rstart[None, :]) & (krow[:, None] < rstart[None, :] + 8)
                     & (kcol[:, None] >= cstart[None, :]) & (kcol[:, None] < cstart[None, :] + 16))
            ri = np.clip(dr + 7, 0, 14)
            ci = np.clip(dc + 15, 0, 30)
            vals = rpb[:, ri, ci]
            out[si, :, :, j * 128:(j + 1) * 128] = np.where(valid[None], vals, NEG)
    return out.astype(NPBF)


def _rope_tables(qtr):
    t = qtr * NTOK + np.arange(NTOK)
    prow = (t // 64).astype(np.float32)
    pcol = (t % 64).astype(np.float32)
    inv = (1.0 / (10000.0 ** (np.arange(32, dtype=np.float32) / 32))).astype(np.float32)
    C = np.zeros((128, NTOK), np.float32)
    Sn = np.zeros((128, NTOK), np.float32)
    for d in range(128):
        pos = prow if d < 64 else pcol
        ang = pos * inv[d % 32]
        C[d] = np.cos(ang)
        Sn[d] = np.sin(ang)
    return C, Sn


def _rot_lhsT():
    Pm = np.zeros((128, 128), np.float32)
    for i in range(128):
        if (i % 64) < 32:
            Pm[i, i + 32] = -1.0
        else:
            Pm[i, i - 32] = 1.0
    return np.ascontiguousarray(Pm.T)


_CACHE = {}


def _prep_common(inputs, layers):
    L = len(layers)
    sl = lambda k: np.ascontiguousarray(inputs[k][layers])
    com = {
        "w_in": sl("w_in"),
        "w_br": np.ascontiguousarray(np.stack([inputs["w_br_na"][layers], inputs["w_br_gqa"][layers], inputs["w_br_conv"][layers]], axis=1)),
        "w_out": sl("w_out"), "w_gate": sl("w_ffn_gate"), "w_up": sl("w_ffn_up"), "w_down": sl("w_ffn_down"),
        "ident": np.eye(128, dtype=np.float32).astype(NPBF),
        "rotT": _rot_lhsT(),
    }
    gains = np.zeros((L, 128, 64), np.float32)
    qkg = np.zeros((L, 128, 2), np.float32)
    convp = np.zeros((L, 128, 32), np.float32)
    for i, l in enumerate(layers):
        for gi, k in enumerate(["pre_mix_g", "post_mix_g", "pre_ffn_g", "post_ffn_g"]):
            gains[i, :, gi * 16:(gi + 1) * 16] = _fmajor_vec(inputs[k][l])
        qkg[i, :, 0] = inputs["q_norm_g"][l]
        qkg[i, :, 1] = inputs["k_norm_g"][l]
        cw = inputs["conv_w"][l]
        cb = inputs["conv_b"][l]
        for ci in range(8):
            for k in range(3):
                convp[i, :, ci * 4 + k] = cw[k, ci * 128:(ci + 1) * 128]
            convp[i, :, ci * 4 + 3] = cb[ci * 128:(ci + 1) * 128]
    com["gains"], com["qkg"], com["convp"] = gains, qkg, convp
    return com


def _prep_core(inputs, layers, c):
    qtr = c % 4
    C, Sn = _rope_tables(qtr)
    sel = np.zeros((128, 8), np.float32)
    if qtr > 0:
        sel[:, qtr - 1] = 1.0
    if qtr < 3:
        sel[:, 4 + qtr + 1] = 1.0
    nab = np.stack([_na_bias_tables(np.asarray(inputs["na_rpb"][l]), qtr) for l in layers], axis=0)
    return {"ropeC": C, "ropeS": Sn, "sel": sel, "nab": nab}


def _x_to_fmajor(x, c):
    b, qtr = c // 4, c % 4
    xs = x[b, qtr * NTOK:(qtr + 1) * NTOK, :]
    return np.ascontiguousarray(xs.T.reshape(KC, 128, NTOK))


def _run(xTs, inputs, layers):
    L = len(layers)
    if L not in _CACHE:
        _CACHE[L] = build(L)
    nc = _CACHE[L]
    com = _prep_common(inputs, layers)
    in_maps = []
    for c in range(NCORE):
        m = dict(com)
        m.update(_prep_core(inputs, layers, c))
        m["xT"] = xTs[c]
        in_maps.append(m)
    res = run_bass_kernel_spmd(nc, in_maps, core_ids=list(range(NCORE)))
    return [r["yT"] for r in res.results]


LAYERS_PER_LAUNCH = 4


def kernel(**inputs):
    inputs = {k: np.asarray(v) for k, v in inputs.items()}
    x = inputs["x"]
    xTs = [_x_to_fmajor(x, c) for c in range(NCORE)]
    for l0 in range(0, DEPTH, LAYERS_PER_LAUNCH):
        xTs = _run(xTs, inputs, list(range(l0, l0 + LAYERS_PER_LAUNCH)))
    out = np.zeros((2, 4 * NTOK, D), np.float32)
    for c in range(NCORE):
        b, qtr = c // 4, c % 4
        out[b, qtr * NTOK:(qtr + 1) * NTOK, :] = xTs[c].reshape(D, NTOK).T
    return out
```

```python
import numpy as np
import ml_dtypes
from contextlib import ExitStack
import concourse.bass as bass
import concourse.mybir as mybir
from concourse.bass_utils import run_bass_kernel_spmd

F32 = mybir.dt.float32
BF16 = mybir.dt.bfloat16
AF = mybir.ActivationFunctionType
ALU = mybir.AluOpType
NPBF = ml_dtypes.bfloat16

NCORE = 8
NTOK = 2048
D = 2048
KC = 16
FF = 5632
FC = 44
INC = 13824
SCALE = 128.0 ** -0.5
EPS = 1e-6
NEG = -30000.0
ENGS = ["pe", "act", "dve", "pool", "sp"]
DEPTH = 4


class Tok:
    __slots__ = ("sem", "key", "val")

    def __init__(self, sem, key, val):
        self.sem, self.key, self.val = sem, key, val


class _Rec:
    def __getattr__(self, name):
        return lambda *a, **k: (name, a, k)


_REC = _Rec()


class Ctx:
    def __init__(self, nc, es):
        self.nc, self.es = nc, es
        self.esem = {e: es.enter_context(nc.semaphore("m_" + e)) for e in ["pe", "act", "dve", "pool"]}
        self.ecnt = {e: 0 for e in self.esem}
        self.dsem = {}
        self.bar = es.enter_context(nc.semaphore("bar"))
        self.barcnt = 0
        self.seen = {e: {} for e in ENGS}
        self.scr = es.enter_context(nc.sbuf_tensor("scr", [128, 8], F32))
        self.first = True

    def dma_sem(self, name):
        if name not in self.dsem:
            self.dsem[name] = [self.es.enter_context(self.nc.semaphore("d_" + name)), 0]
        return self.dsem[name]


class Phase:
    def __init__(self, ctx):
        self.c = ctx
        self.ops = {e: [] for e in ENGS}
        self.dtoks = {e: {} for e in ENGS}

    def op(self, eng, fn, waits=(), sig=False, chain=True):
        tok = None
        waits = list(waits)
        if eng != "pe":
            sig = True
            if chain and self.c.ecnt[eng] > 0:
                waits.append(Tok(self.c.esem[eng], "m_" + eng, self.c.ecnt[eng]))
        if sig:
            self.c.ecnt[eng] += 1
            tok = Tok(self.c.esem[eng], "m_" + eng, self.c.ecnt[eng])
        name, a, k = fn(_REC)
        self.ops[eng].append((lambda e, name=name, a=a, k=k: getattr(e, name)(*a, **k),
                              tuple(w for w in waits if w is not None), tok, 1))
        return tok

    def dma(self, eng, out, in_, sem, waits=()):
        s = self.c.dma_sem(sem)
        s[1] += 16
        tok = Tok(s[0], "d_" + sem, s[1])
        self.ops[eng].append((lambda e, o=out, i=in_: e.dma_start(out=o, in_=i),
                              tuple(w for w in waits if w is not None), tok, 16))
        self.dtoks[eng][tok.key] = tok
        return tok

    def coll(self, ins, outs, waits=()):
        s = self.c.dma_sem("cc")
        s[1] += 1
        tok = Tok(s[0], "d_cc", s[1])
        fn = lambda e, i=ins, o=outs: e.collective_compute(
            "AllGather", ALU.bypass, replica_groups=[[0, 1, 2, 3], [4, 5, 6, 7]], ins=[i], outs=[o])
        self.ops["pool"].append((fn, tuple(w for w in waits if w is not None), tok, 1))
        self.dtoks["pool"][tok.key] = tok
        return tok

    def run(self):
        c = self.c
        nc = c.nc
        c.barcnt += 4
        barv = c.barcnt

        def emit(eng, e):
            seen = c.seen[eng]

            def wait(t):
                if seen.get(t.key, 0) < t.val:
                    e.wait_ge(t.sem, t.val)
                    seen[t.key] = t.val

            for fn, waits, tok, inc in self.ops[eng]:
                for w in waits:
                    wait(w)
                ins = fn(e)
                if tok is not None:
                    ins.then_inc(tok.sem, inc)
            for t in self.dtoks[eng].values():
                wait(t)
            if eng == "sp":
                e.sem_inc(c.bar, 1)
            elif eng == "act":
                e.memzero(c.scr[:, 0:1]).then_inc(c.bar, 1)
            elif eng == "dve":
                e.memset(c.scr[:, 1:2], 0.0).then_inc(c.bar, 1)
            elif eng == "pool":
                e.memset(c.scr[:, 2:3], 0.0).then_inc(c.bar, 1)
            e.wait_ge(c.bar, barv)

        with nc.Block() as block:
            @block.tensor
            def _(e):
                emit("pe", e)

            @block.scalar
            def _(e):
                emit("act", e)

            @block.vector
            def _(e):
                emit("dve", e)

            @block.gpsimd
            def _(e):
                emit("pool", e)

            @block.sync
            def _(e):
                emit("sp", e)


class Ring:
    def __init__(self, aps):
        self.aps = list(aps)
        self.free = [None] * len(self.aps)
        self.i = 0

    def get(self):
        k = self.i % len(self.aps)
        self.i += 1
        return k, self.aps[k], self.free[k]


def gemm(ph, tiles, wring, wsem, psring, pe_waits=()):
    active = []

    def advance():
        for g in list(active):
            try:
                next(g)
            except StopIteration:
                active.remove(g)

    tiles = list(tiles)

    def issue(tile):
        ws, wap, wfree = wring.get()
        wt = None
        for dst_fn, src in tile["pieces"]:
            wt = ph.dma("pool", dst_fn(wap), src, f"{wsem}{ws}", waits=[wfree])
        return ws, wap, wt

    pending = issue(tiles[0]) if tiles else None
    for ti, tile in enumerate(tiles):
        ws, wap, wt = pending
        if ti + 1 < len(tiles):
            pending = issue(tiles[ti + 1])
        units = tile["units"]
        last_tok = None
        for ui, unit in enumerate(units):
            pk, pap, pfree = psring.get()
            mms = unit["mm"]
            tok = None
            nmm = sum(len(kl) for _, kl in mms)
            cnt = 0
            for bank, klist in mms:
                for ki, kent in enumerate(klist):
                    lhs_fn, rhs = kent[0], kent[1]
                    kw = [kent[2]] if len(kent) > 2 else []
                    cnt += 1
                    lastmm = cnt == nmm
                    tok = ph.op("pe", lambda e, o=pap[:, bank, 0:rhs.shape[-1] if len(rhs.shape) == 2 else 512], l=lhs_fn(wap), r=rhs,
                                a=(ki == 0), b=(ki == len(klist) - 1): e.matmul(o, lhsT=l, rhs=r, start=a, stop=b),
                                waits=([wt, pfree] + list(pe_waits) if cnt == 1 else []) + kw, sig=lastmm)
            last_tok = tok

            def setfree(t, k=pk):
                psring.free[k] = t

            advance()
            g = unit["evac"](pap, tok, setfree)
            active.append(g)
            try:
                next(g)
            except StopIteration:
                active.remove(g)
        wring.free[ws] = last_tok
    while active:
        advance()


def build(L, dbg=False, stop_after=None, skip=()):
    nc = bass.Bass("TRN2", target_bir_lowering=False)
    es = ExitStack()
    uid = [0]

    def uname(n):
        uid[0] += 1
        return f"{n}_{uid[0]}"

    def din(name, shape, dt=F32):
        return nc.dram_tensor(name, list(shape), dt, kind="ExternalInput").ap()

    def dscr(name, shape, dt):
        if dbg and name in ("brT_d", "mT_d", "xmid", "qTg_d", "qTna_d", "kTna_d", "vna_d", "hT_d", "uT_d", "bgT_d"):
            return nc.dram_tensor(name, list(shape), dt, kind="ExternalOutput").ap()
        return nc.dram_tensor(name, list(shape), dt).ap()

    xT_in = din("xT", [KC, 128, NTOK])
    w_in = din("w_in", [L, D, INC])
    w_br = din("w_br", [L, 3, 1024, D])
    w_out = din("w_out", [L, D, D])
    small_ffn = stop_after is not None and stop_after <= 9
    w_gate = din("w_gate", [L, 128, 128] if small_ffn else [L, D, FF])
    w_up = din("w_up", [L, 128, 128] if small_ffn else [L, D, FF])
    w_down = din("w_down", [L, 128, 128] if small_ffn else [L, FF, D])
    gains_d = din("gains", [L, 128, 4 * KC])
    qkg_d = din("qkg", [L, 128, 2])
    convp_d = din("convp", [L, 128, 32])
    nab_d = din("nab", [L, 5, 8, 128, 768], BF16)
    ropeC_d = din("ropeC", [128, NTOK])
    ropeS_d = din("ropeS", [128, NTOK])
    ident_d = din("ident", [128, 128], BF16)
    rotT_d = din("rotT", [128, 128])
    sel_d = din("sel", [128, 8])
    yT = nc.dram_tensor("yT", [KC, 128, NTOK], F32, kind="ExternalOutput").ap()

    xmid = dscr("xmid", [KC, 128, NTOK], F32)
    xres = dscr("xres", [KC, 128, NTOK], F32)
    hT_d = dscr("hT_d", [KC, 128, NTOK], BF16)
    qTna_d = dscr("qTna_d", [8, 128, NTOK], BF16)
    kTna_d = dscr("kTna_d", [8, 128, NTOK], BF16)
    vna_d = dscr("vna_d", [NTOK, 1024], BF16)
    qTg_d = dscr("qTg_d", [8, 128, NTOK], BF16)
    send_t = [nc.dram_tensor(f"send{i}_d", [256, 2048], BF16) for i in range(4)]
    recv_t = [nc.dram_tensor(f"recv{i}_d", [1024, 2048], BF16) for i in range(4)]
    ub_d = nc.dram_tensor("ub_d", [256, 8], F32)
    uball_d = nc.dram_tensor("uball_d", [1024, 8], F32)
    uT_d = dscr("uT_d", [8, 128, NTOK], F32)
    bgT_d = dscr("bgT_d", [8, 128, NTOK], F32)
    brT_d = dscr("brT_d", [3, 8, 128, NTOK], BF16)
    mT_d = dscr("mT_d", [KC, 128, NTOK], BF16)
    aT_d = dscr("aT_d", [FC, 128, NTOK], BF16)
    kTg_send = send_t[0].ap()
    vg_send = send_t[1].ap().rearrange("r (a c) -> (r a) c", c=256)
    hk_send = send_t[2].ap().rearrange("r (a c) -> (r a) c", c=256)
    hv_send = send_t[3].ap().rearrange("r (a c) -> (r a) c", c=1024)

    def recv_view(i, r):
        return recv_t[i].ap()[r * 256:(r + 1) * 256, :]

    ctx = Ctx(nc, es)
    S = lambda name, shape, dt: es.enter_context(nc.sbuf_tensor("s_" + name, list(shape), dt))
    psf = es.enter_context(nc.psum_tensor("psf", [128, 6, 512], F32))
    pst = [es.enter_context(nc.psum_tensor("pst0", [128, 1024], BF16)), es.enter_context(nc.psum_tensor("pst1", [128, 1024], BF16))]

    ident = S("ident", [128, 128], BF16)
    rotT = S("rotT", [128, 128], F32)
    onesD = S("onesD", [128, 128], F32)
    onesH = S("onesH", [128, 128], F32)
    sel = S("sel", [128, 8], F32)
    gains = S("gains", [128, L, 4 * KC], F32)
    qkg = S("qkg", [128, L, 2], F32)
    convp = S("convp", [128, L, 32], F32)
    epsT = S("epsT", [128, 1], F32)

    ph = Phase(ctx)
    t0 = ph.dma("sp", ident[:], ident_d, "c0")
    ph.dma("sp", rotT[:], rotT_d, "c0")
    ph.dma("sp", sel[:], sel_d, "c0")
    ph.dma("sp", gains[:], gains_d.rearrange("l p k -> p l k"), "c0")
    ph.dma("sp", qkg[:], qkg_d.rearrange("l p k -> p l k"), "c0")
    ph.dma("sp", convp[:], convp_d.rearrange("l p k -> p l k"), "c0")
    ph.op("pool", lambda e: e.memset(onesD[:], 1.0 / D))
    ph.op("pool", lambda e: e.memset(onesH[:], 1.0 / 128))
    ph.op("pool", lambda e: e.memset(epsT[:], EPS))
    ph.run()

    def rstd_ops(ph, out_ap, in_ap, waits):
        a = ph.op("act", lambda e: e.activation(out=out_ap, in_=in_ap, func=AF.Sqrt, bias=epsT[:, 0:1], scale=1.0), waits=waits, sig=True)
        r = ph.op("dve", lambda e: e.reciprocal(out=out_ap, in_=out_ap), waits=[a], sig=True)
        return a, r

    def phase_norm(X, g_ap):
        with ExitStack() as es2:
            T = lambda name, shape, dt: es2.enter_context(nc.sbuf_tensor(uname(name), list(shape), dt))
            xt = T("n_xt", [128, 2, KC, 512], F32)
            sq = T("n_sq", [128, 2, 4, 512], F32)
            rs = T("n_rs", [128, 2, 512], F32)
            hb = T("n_hb", [128, 2, KC, 512], BF16)
            ph = Phase(ctx)
            xfree = [None, None]
            sqfree = [None, None]
            hbfree = [None, None]
            rsfree = [None, None]
            psfree = [None, None]
            def issue_ld(tg_):
                s_ = tg_ % 2
                ts_ = slice(tg_ * 512, (tg_ + 1) * 512)
                return [ph.dma("sp", xt[:, s_, 4 * q:4 * q + 4, :], X[4 * q:4 * q + 4, :, ts_].rearrange("k p t -> p k t"),
                               f"nx{s_}{q}", waits=[xfree[s_]]) for q in range(4)]

            lds = {0: issue_ld(0), 1: issue_ld(1)}
            for tg in range(4):
                s = tg % 2
                ts = slice(tg * 512, (tg + 1) * 512)
                ld = lds[tg]
                mmtok = None
                for q in range(4):
                    qs = (tg * 4 + q) % 2
                    a = ph.op("act", lambda e, o=sq[:, qs], i=xt[:, s, 4 * q:4 * q + 4, :]: e.activation(out=o, in_=i, func=AF.Square),
                              waits=[ld[q], sqfree[qs]], sig=True)
                    for k in range(4):
                        kc = 4 * q + k
                        mmtok = ph.op("pe", lambda e, o=psf[:, s, :], r=sq[:, qs, k, :], st=(kc == 0), sp=(kc == KC - 1):
                                      e.matmul(o, lhsT=onesD[:], rhs=r, start=st, stop=sp),
                                      waits=[a, psfree[s]] if k == 0 else (), sig=(k == 3))
                    sqfree[qs] = mmtok
                ra, r = rstd_ops(ph, rs[:, s, :], psf[:, s, :], [mmtok, rsfree[s]])
                psfree[s] = ra
                last = {}
                for kc in range(KC):
                    eng = "dve"
                    last[eng] = ph.op(eng, lambda e, o=hb[:, s, kc, :], i=xt[:, s, kc, :], g=g_ap[:, kc:kc + 1], rr=rs[:, s, :]:
                                      e.scalar_tensor_tensor(out=o, in0=i, scalar=g, in1=rr, op0=ALU.mult, op1=ALU.mult),
                                      waits=[r, hbfree[s], ld[3]] if kc < 2 else (), sig=(kc >= KC - 1))
                xfree[s] = None
                st = ph.dma("sp", hT_d[:, :, ts].rearrange("k p t -> p k t"), hb[:, s], f"nh{s}", waits=[last["dve"]])
                hbfree[s] = st
                rsfree[s] = st
                xfree[s] = st
                if tg + 2 < 4:
                    lds[tg + 2] = issue_ld(tg + 2)
            ph.run()

    def phase_proj(l):
        with ExitStack() as es2:
            T = lambda name, shape, dt: es2.enter_context(nc.sbuf_tensor(uname(name), list(shape), dt))
            hT = T("p_hT", [128, KC, NTOK], BF16)
            rC = T("p_rC", [128, NTOK], F32)
            rS = T("p_rS", [128, NTOK], F32)
            wt = T("p_wt", [128, 2, KC, 512], BF16)
            stg = T("p_stg", [128, 3, 1024], BF16)
            stf = T("p_stf", [128, 3, 1024], F32)
            htmp = T("p_htmp", [128, 2, 1024], F32)
            sqb = T("p_sqb", [128, 2, 512], F32)
            rb = T("p_rb", [128, 2, 512], F32)
            qn = T("p_qn", [128, 2, 512], F32)
            t1 = T("p_t1", [128, 2, 512], F32)
            t2 = T("p_t2", [128, 2, 512], F32)
            ub = T("p_ub", [128, 2, 8], F32)
            vst = T("p_vst", [128, 3, 512], BF16)
            ph = Phase(ctx)
            hld = []
            for q in range(4):
                hld.append(ph.dma("sp", hT[:, 4 * q:4 * q + 4, :], hT_d[4 * q:4 * q + 4].rearrange("k p t -> p k t"), f"phq{q}"))
            ph.dma("sp", rC[:], ropeC_d, "ph")
            hld_all = ph.dma("sp", rS[:], ropeS_d, "ph")
            wring = Ring([wt[:, 0], wt[:, 1]])
            psring = Ring([psf[:, 0:2, :], psf[:, 2:4, :]])
            stgR = Ring([stg[:, i, :] for i in range(3)])
            stfR = Ring([stf[:, i, :] for i in range(3)])
            auxA = [None]
            auxB = [None]
            cnt = [0]
            pmul = [None, None]
            padd = [None, None]
            W = w_in[l]

            def wpiece(c0, ncols, dst0=0):
                return (lambda wap, d=dst0, n=ncols: wap[:, :, d:d + n],
                        W[:, c0:c0 + ncols].rearrange("(k p) n -> p k n", p=128))

            def mm_units(ncc, evac_of):
                units = []
                for cc in range(ncc):
                    for tb in range(2):
                        mm = []
                        for b in range(2):
                            ts = slice(tb * 1024 + b * 512, tb * 1024 + (b + 1) * 512)
                            mm.append((b, [(lambda wap, k=k, cc=cc: wap[:, k, cc * 128:(cc + 1) * 128], hT[:, k, ts], hld[k // 4]) for k in range(KC)]))
                        units.append(dict(mm=mm, evac=evac_of(cc, tb)))
                return units

            def ev_simple(dst_of, scale, eng):
                def evac_of(cc, tb):
                    def gen(pap, tok, setfree):
                        k, sap, sfree = stgR.get()
                        if eng == "act":
                            t = ph.op("act", lambda e: e.activation(out=sap, in_=pap.rearrange("p b t -> p (b t)"), func=AF.Copy, scale=scale),
                                      waits=[tok, sfree, hld_all], sig=True)
                        else:
                            t = ph.op("dve", lambda e: e.tensor_copy(out=sap, in_=pap.rearrange("p b t -> p (b t)")),
                                      waits=[tok, sfree, hld_all], sig=True)
                        setfree(t)
                        stgR.free[k] = ph.dma("sp", dst_of(cc)[:, tb * 1024:(tb + 1) * 1024], sap, f"ps{k}", waits=[t])
                        yield
                    return gen
                return evac_of

            def ev_rope(dst_of, gcol):
                def evac_of(cc, tb):
                    def gen(pap, tok, setfree):
                        k, sap, sfree = stgR.get()
                        lastd = None
                        for b in range(2):
                            i = cnt[0] % 2
                            cnt[0] += 1
                            ts = slice(tb * 1024 + b * 512, tb * 1024 + (b + 1) * 512)
                            a = ph.op("act", lambda e, o=sqb[:, i, :], p=pap[:, b, :]: e.activation(out=o, in_=p, func=AF.Square),
                                      waits=[tok, hld_all], sig=True)
                            m = ph.op("pe", lambda e, r=sqb[:, i, :]: e.matmul(psf[:, 4, :], lhsT=onesH[:], rhs=r, start=True, stop=True),
                                      waits=[a, auxA[0]], sig=True)
                            ra_, r_ = rstd_ops(ph, rb[:, i, :], psf[:, 4, :], [m])
                            auxA[0] = ra_
                            q_ = ph.op("dve", lambda e, o=qn[:, i, :], p=pap[:, b, :], rr=rb[:, i, :]:
                                       e.scalar_tensor_tensor(out=o, in0=p, scalar=qkg[:, l, gcol:gcol + 1], in1=rr, op0=ALU.mult, op1=ALU.mult),
                                       waits=[pmul[i]], sig=True)
                            lastd = q_
                            pq = ph.op("pe", lambda e, r=qn[:, i, :]: e.matmul(psf[:, 5, :], lhsT=rotT[:], rhs=r, start=True, stop=True),
                                       waits=[q_, auxB[0]], sig=True)
                            pmul[i] = ph.op("pool", lambda e, o=t1[:, i, :], a_=qn[:, i, :], c_=rC[:, ts]: e.tensor_tensor(out=o, in0=a_, in1=c_, op=ALU.mult),
                                            waits=[q_], sig=True)
                            g_ = ph.op("dve", lambda e, o=t2[:, i, :], s_=rS[:, ts]: e.tensor_tensor(out=o, in0=psf[:, 5, :], in1=s_, op=ALU.mult),
                                       waits=[pq, padd[i]], sig=True)
                            auxB[0] = g_
                            lastp = ph.op("pool", lambda e, o=sap[:, b * 512:(b + 1) * 512], a_=t1[:, i, :], b_=t2[:, i, :]:
                                          e.tensor_tensor(out=o, in0=a_, in1=b_, op=ALU.add), waits=[g_, sfree], sig=True)
                            padd[i] = lastp
                        setfree(lastd)
                        stgR.free[k] = ph.dma("sp", dst_of(cc)[:, tb * 1024:(tb + 1) * 1024], sap, f"ps{k}", waits=[lastp])
                        yield
                    return gen
                return evac_of

            hslot = {}

            def ev_conv(ci):
                def evac_of(cc, tb):
                    def gen(pap, tok, setfree):
                        pflat = pap.rearrange("p b t -> p (b t)")
                        if cc == 0:
                            t = ph.op("act", lambda e: e.activation(out=htmp[:, tb, :], in_=pflat, func=AF.Copy),
                                      waits=[tok, hslot.get(tb)], sig=True)
                            hslot[("h", tb)] = t
                            setfree(t)
                        elif cc == 1:
                            k, sap, sfree = stfR.get()
                            t = ph.op("dve", lambda e: e.tensor_tensor(out=sap, in0=pflat, in1=htmp[:, tb, :], op=ALU.mult),
                                      waits=[tok, sfree, hslot[("h", tb)]], sig=True)
                            hslot[tb] = t
                            setfree(t)
                            col = 0 if tb == 0 else 1023
                            t2_ = ph.op("dve", lambda e: e.tensor_copy(out=ub[:, tb, ci:ci + 1], in_=sap[:, col:col + 1]), sig=True)
                            stfR.free[k] = ph.dma("sp", uT_d[ci][:, tb * 1024:(tb + 1) * 1024], sap, f"pf{k}", waits=[t2_])
                        else:
                            k, sap, sfree = stfR.get()
                            t = ph.op("act", lambda e: e.activation(out=sap, in_=pflat, func=AF.Copy), waits=[tok, sfree], sig=True)
                            setfree(t)
                            stfR.free[k] = ph.dma("sp", bgT_d[ci][:, tb * 1024:(tb + 1) * 1024], sap, f"pf{k}", waits=[t])
                        yield
                    return gen
                return evac_of

            tiles = []
            for j in range(2):
                tiles.append(dict(pieces=[wpiece(j * 512, 512)],
                                  units=mm_units(4, ev_simple(lambda cc, j=j: qTna_d[4 * j + cc], SCALE, "act"))))
            for j in range(2):
                tiles.append(dict(pieces=[wpiece(1024 + j * 512, 512)],
                                  units=mm_units(4, ev_simple(lambda cc, j=j: kTna_d[4 * j + cc], 1.0, "dve"))))
            for j in range(2):
                tiles.append(dict(pieces=[wpiece(3072 + j * 512, 512)],
                                  units=mm_units(4, ev_rope(lambda cc, j=j: qTg_d[4 * j + cc], 0))))
            tiles.append(dict(pieces=[wpiece(4096, 256)],
                              units=mm_units(2, ev_rope(lambda cc: kTg_send[cc * 128:(cc + 1) * 128, :], 1))))
            for ci in range(8):
                tiles.append(dict(pieces=[wpiece(4608 + ci * 128, 128, 0), wpiece(6656 + ci * 128, 128, 128), wpiece(5632 + ci * 128, 128, 256)],
                                  units=mm_units(3, ev_conv(ci))))
            gemm(ph, tiles, wring, "pw", psring)

            vring = Ring([vst[:, i, :] for i in range(3)])
            vps = Ring([psf[:, 0, :], psf[:, 1, :], psf[:, 2, :], psf[:, 3, :]])
            vps.free = [psring.free[0], psring.free[0], psring.free[1], psring.free[1]]
            for (c0, ncols, dst) in [(2048, 512, vna_d[:, 0:512]), (2560, 512, vna_d[:, 512:1024]), (4352, 256, vg_send)]:
                ws, wap, wfree = wring.get()
                wtk = ph.dma("pool", wap[:, :, 0:ncols], W[:, c0:c0 + ncols].rearrange("(k p) n -> p k n", p=128), f"pw{ws}", waits=[wfree])
                tok = None
                for t in range(16):
                    pk, pap, pfree = vps.get()
                    for k in range(KC):
                        tok = ph.op("pe", lambda e, o=pap[:, 0:ncols], l_=hT[:, k, t * 128:(t + 1) * 128], r=wap[:, k, 0:ncols], a=(k == 0), b=(k == KC - 1):
                                    e.matmul(o, lhsT=l_, rhs=r, start=a, stop=b), waits=[wtk, pfree] if k == 0 else (), sig=(k == KC - 1))
                    sk, sap, sfree = vring.get()
                    if t % 2 == 0:
                        ev = ph.op("act", lambda e, o=sap[:, 0:ncols], i=pap[:, 0:ncols]: e.activation(out=o, in_=i, func=AF.Copy), waits=[tok, sfree], sig=True)
                    else:
                        ev = ph.op("dve", lambda e, o=sap[:, 0:ncols], i=pap[:, 0:ncols]: e.tensor_copy(out=o, in_=i), waits=[tok, sfree], sig=True)
                    vps.free[pk] = ev
                    vring.free[sk] = ph.dma("sp", dst[t * 128:(t + 1) * 128, :], sap[:, 0:ncols], f"pv{sk}", waits=[ev])
                wring.free[ws] = tok
            ph.dma("sp", ub_d.ap().rearrange("(w p) c -> p w c", p=128), ub[:], "pub",
                   waits=[Tok(ctx.esem["dve"], "m_dve", ctx.ecnt["dve"])])
            ph.run()

    def phase_exchange():
        ph = Phase(ctx)
        a = ph.dma("sp", hk_send[0:1024, :].rearrange("(h d) t -> h d t", d=128), kTna_d[:, :, 0:256], "xh")
        a = ph.dma("sp", hk_send[1024:2048, :].rearrange("(h d) t -> h d t", d=128), kTna_d[:, :, NTOK - 256:NTOK], "xh")
        a = ph.dma("sp", hv_send[0:256, :], vna_d[0:256, :], "xh")
        a = ph.dma("sp", hv_send[256:512, :], vna_d[NTOK - 256:NTOK, :], "xh")
        c1 = a
        for i in range(4):
            c1 = ph.coll(send_t[i].ap(), recv_t[i].ap(), waits=[c1])
        ph.coll(ub_d.ap(), uball_d.ap(), waits=[c1])
        ph.run()

    def phase_conv(l):
        with ExitStack() as es2:
            T = lambda name, shape, dt: es2.enter_context(nc.sbuf_tensor(uname(name), list(shape), dt))
            ubs = T("c_ub", [128, 4, 2, 8], F32)
            prv = T("c_prv", [128, 8], F32)
            nxt = T("c_nxt", [128, 8], F32)
            uh = T("c_uh", [128, 2, NTOK + 2], F32)
            bg = T("c_bg", [128, 2, NTOK], F32)
            y = T("c_y", [128, 2, NTOK], F32)
            yb = T("c_yb", [128, 2, NTOK], BF16)
            ph = Phase(ctx)
            ld = ph.dma("sp", ubs[:], uball_d.ap().rearrange("(r w p) c -> p r w c", r=4, w=2), "cu")
            tk = None
            for r in range(4):
                if r == 0:
                    ph.op("dve", lambda e: e.tensor_scalar(prv[:], ubs[:, 0, 1, :], sel[:, 0:1], None, ALU.mult), waits=[ld])
                    tk = ph.op("dve", lambda e: e.tensor_scalar(nxt[:], ubs[:, 0, 0, :], sel[:, 4:5], None, ALU.mult), sig=True)
                else:
                    ph.op("dve", lambda e, r=r: e.scalar_tensor_tensor(out=prv[:], in0=ubs[:, r, 1, :], scalar=sel[:, r:r + 1], in1=prv[:], op0=ALU.mult, op1=ALU.add), waits=[tk])
                    tk = ph.op("dve", lambda e, r=r: e.scalar_tensor_tensor(out=nxt[:], in0=ubs[:, r, 0, :], scalar=sel[:, 4 + r:5 + r], in1=nxt[:], op0=ALU.mult, op1=ALU.add), sig=True)
            free = [None, None]
            for ci in range(8):
                s = ci % 2
                eng = "dve"
                l1 = ph.dma("sp", uh[:, s, 1:NTOK + 1], uT_d[ci], f"cl{s}", waits=[free[s]])
                l2 = ph.dma("sp", bg[:, s, :], bgT_d[ci], f"cl{s}", waits=[free[s]])
                cp = lambda k: convp[:, l, ci * 4 + k:ci * 4 + k + 1]
                a = ph.op("dve", lambda e: e.tensor_copy(out=uh[:, s, 0:1], in_=prv[:, ci:ci + 1]), waits=[tk, l2])
                a = ph.op("dve", lambda e: e.tensor_copy(out=uh[:, s, NTOK + 1:NTOK + 2], in_=nxt[:, ci:ci + 1]), sig=True)
                ph.op(eng, lambda e: e.tensor_scalar(y[:, s, :], uh[:, s, 1:NTOK + 1], cp(1), cp(3), ALU.mult, ALU.add), waits=[a, l2])
                b = ph.op(eng, lambda e: e.scalar_tensor_tensor(out=y[:, s, :], in0=uh[:, s, 0:NTOK], scalar=cp(0), in1=y[:, s, :], op0=ALU.mult, op1=ALU.add), sig=True)
                b = ph.op(eng, lambda e: e.scalar_tensor_tensor(out=y[:, s, :], in0=uh[:, s, 2:NTOK + 2], scalar=cp(2), in1=y[:, s, :], op0=ALU.mult, op1=ALU.add), waits=[b], sig=True)
                b = ph.op(eng, lambda e: e.tensor_tensor(out=yb[:, s, :], in0=y[:, s, :], in1=bg[:, s, :], op=ALU.mult), waits=[b], sig=True)
                free[s] = ph.dma("sp", brT_d[2, ci], yb[:, s, :], f"cs{s}", waits=[b])
            ph.run()

    def halo_select(ph, eng, dst, src_of, selbase, ld):
        tk = ph.op(eng, lambda e: e.tensor_scalar(dst, src_of(0), sel[:, selbase:selbase + 1], None, ALU.mult),
                   waits=[ld, Tok(ctx.esem["pool"], "m_pool", ctx.ecnt["pool"])], sig=True)
        for r in range(1, 4):
            tk = ph.op(eng, lambda e, r=r: e.scalar_tensor_tensor(out=dst, in0=src_of(r), scalar=sel[:, selbase + r:selbase + r + 1], in1=dst,
                                                                   op0=ALU.mult, op1=ALU.add), waits=[tk], sig=True)
        return tk

    def attn_finish(ph, o_ps, pe_tok, ob_ap, ob_free, tp_ap, tp_free, dst_ap, dst_free, evac_eng):
        ph.op("dve", lambda e: e.reciprocal(out=ob_ap["r"], in_=o_ps[:, 128:129]), waits=[pe_tok, ob_free])
        n = ph.op("dve", lambda e: e.tensor_scalar(ob_ap["o"], o_ps[:, 0:128], ob_ap["r"], None, ALU.mult), sig=True)
        t = ph.op("pe", lambda e: e.transpose(out=tp_ap, in_=ob_ap["o"], identity=ident[:]), waits=[n, tp_free], sig=True)
        if evac_eng == "act":
            c = ph.op("act", lambda e: e.activation(out=dst_ap, in_=tp_ap, func=AF.Copy), waits=[t, dst_free], sig=True)
        else:
            c = ph.op("dve", lambda e: e.tensor_copy(out=dst_ap, in_=tp_ap), waits=[t, dst_free], sig=True)
        return n, t, c

    def phase_na(l):
        with ExitStack() as es2:
            T = lambda name, shape, dt: es2.enter_context(nc.sbuf_tensor(uname(name), list(shape), dt))
            qT = T("a_qT", [128, 8, NTOK], BF16)
            Kb = T("a_K", [128, 8, 20 * 128], BF16)
            Vb = T("a_V", [128, 20, 8, 129], BF16)
            tmpk = T("a_tk", [128, 4, 8, 256], BF16)
            tmpv = T("a_tv", [128, 4, 2, 1024], BF16)
            bias = T("a_bias", [128, 3, 768], BF16)
            PT = T("a_PT", [128, 2, 768], BF16)
            ob = T("a_ob", [128, 2, 128], BF16)
            rr = T("a_rr", [128, 2, 1], F32)
            ost = T("a_ost", [128, 2, 8, 128], BF16)
            ph = Phase(ctx)
            ms = ph.op("pool", lambda e: e.memset(Vb[:], 1.0), sig=True)
            lq = ph.dma("sp", qT[:], qTna_d.rearrange("h d t -> d h t"), "aq")
            lk = ph.dma("sp", Kb[:, :, 256:256 + NTOK], kTna_d.rearrange("h d t -> d h t"), "aq")
            lv = None
            for h8 in range(8):
                lv = ph.dma("sp", Vb[:, 2:18, h8, 0:128],
                            vna_d[:, 128 * h8:128 * h8 + 128].rearrange("(c p) d -> p c d", p=128), "aq", waits=[ms])
            for (w, selb, kdst, vdst) in [(1, 0, Kb[:, :, 0:256], Vb[:, 0:2, :, 0:128]), (0, 4, Kb[:, :, 2304:2560], Vb[:, 18:20, :, 0:128])]:
                ldk = ldv = None
                for r in range(4):
                    hk_r = recv_view(2, r).rearrange("r (a c) -> (r a) c", c=256)
                    hv_r = recv_view(3, r).rearrange("r (a c) -> (r a) c", c=1024)
                    ldk = ph.dma("sp", tmpk[:, r], hk_r[w * 1024:(w + 1) * 1024, :].rearrange("(h d) t -> d h t", d=128), f"ah{w}",
                                 waits=[Tok(ctx.esem["dve"], "m_dve", ctx.ecnt["dve"]), Tok(ctx.esem["pool"], "m_pool", ctx.ecnt["pool"])])
                    ldv = ph.dma("sp", tmpv[:, r], hv_r[w * 256:(w + 1) * 256, :].rearrange("(c p) n -> p c n", p=128), f"ah{w}",
                                 waits=[Tok(ctx.esem["dve"], "m_dve", ctx.ecnt["dve"]), Tok(ctx.esem["pool"], "m_pool", ctx.ecnt["pool"])])
                halo_select(ph, "dve", kdst, lambda r: tmpk[:, r], selb, ldv)
                halo_select(ph, "dve", vdst, lambda r: tmpv[:, r].rearrange("p c (h d) -> p c h d", d=128), selb, ldv)
            ready = [lv, Tok(ctx.esem["dve"], "m_dve", ctx.ecnt["dve"]), Tok(ctx.esem["pool"], "m_pool", ctx.ecnt["pool"])]

            units = [(lp, h) for lp in range(16) for h in range(8)]
            N = len(units)
            slot_of = lambda lp: 0 if lp == 0 else 1 if lp == 1 else 3 if lp == 14 else 4 if lp == 15 else 2
            bfree = [None] * 3
            btok = [None] * N
            sfree = [None, None]
            ptfree = [None, None]
            ofree = [None, None]
            obfree = [None, None]
            tpfree = [None, None]
            ostfree = [None, None]
            qk_tok = [None] * N
            ex_tok = [None] * N
            pv_tok = [None] * N

            def load_bias(u):
                lp, h = units[u]
                btok[u] = ph.dma("sp", bias[:, u % 3, :], nab_d[l, slot_of(lp), h], f"ab{u % 3}", waits=[bfree[u % 3]])

            load_bias(0)
            load_bias(1)
            nrm_tok = [None] * N
            for step in range(N + 3):
                if step + 2 < N:
                    load_bias(step + 2)
                if step < N:
                    u = step
                    lp, h = units[u]
                    s = u % 2
                    wlo = max(lp - 1, 0)
                    tok = None
                    for j in range(6):
                        o = psf[:, 2 * s + j // 4, (j % 4) * 128:(j % 4 + 1) * 128]
                        ph.op("pe", lambda e, o=o, k=Kb[:, h, (wlo + j) * 128:(wlo + j + 1) * 128], q=qT[:, h, lp * 128:(lp + 1) * 128]:
                              e.matmul(o, lhsT=k, rhs=q, start=True, stop=False), waits=ready + [sfree[s], btok[u]] if j == 0 else ())
                        tok = ph.op("pe", lambda e, o=o, b_=bias[:, u % 3, j * 128:(j + 1) * 128]: e.matmul(o, lhsT=ident[:], rhs=b_, start=False, stop=True),
                                    sig=(j == 5))
                    qk_tok[u] = tok
                    bfree[u % 3] = tok
                    ph.op("act", lambda e, s=s: e.activation(out=PT[:, s, 0:512], in_=psf[:, 2 * s, :], func=AF.Exp), waits=[tok, ptfree[s]])
                    ex_tok[u] = ph.op("act", lambda e, s=s: e.activation(out=PT[:, s, 512:768], in_=psf[:, 2 * s + 1, 0:256], func=AF.Exp), sig=True)
                    sfree[s] = ex_tok[u]
                if 0 <= step - 1 < N:
                    u = step - 1
                    lp, h = units[u]
                    s = u % 2
                    wlo = max(lp - 1, 0)
                    tok = None
                    for j in range(6):
                        tok = ph.op("pe", lambda e, s=s, j=j, v=Vb[:, wlo + j, h, :]: e.matmul(psf[:, 4 + s, 0:129], lhsT=PT[:, s, j * 128:(j + 1) * 128], rhs=v,
                                                                                             start=(j == 0), stop=(j == 5)),
                                    waits=[ex_tok[u], ofree[s]] if j == 0 else (), sig=(j == 5))
                    pv_tok[u] = tok
                    ptfree[s] = tok
                    ph.op("dve", lambda e, s=s: e.reciprocal(out=rr[:, s, :], in_=psf[:, 4 + s, 128:129]), waits=[tok, obfree[s]])
                    nrm_tok[u] = ph.op("dve", lambda e, s=s: e.tensor_scalar(ob[:, s, :], psf[:, 4 + s, 0:128], rr[:, s, :], None, ALU.mult), sig=True)
                    ofree[s] = nrm_tok[u]
                if 0 <= step - 2 < N:
                    u = step - 2
                    lp, h = units[u]
                    s = u % 2
                    tp_ap = pst[s][:, 0:128]
                    t = ph.op("pe", lambda e, s=s, tp_ap=tp_ap: e.transpose(out=tp_ap, in_=ob[:, s, :], identity=ident[:]), waits=[nrm_tok[u], tpfree[s]], sig=True)
                    obfree[s] = t
                    dst_ap = ost[:, lp % 2, h, :]
                    dfree = ostfree[lp % 2] if h <= 1 else None
                    if u % 2 == 0:
                        c = ph.op("dve", lambda e, d=dst_ap, tp_ap=tp_ap: e.tensor_copy(out=d, in_=tp_ap), waits=[t, dfree], sig=True)
                    else:
                        c = ph.op("act", lambda e, d=dst_ap, tp_ap=tp_ap: e.activation(out=d, in_=tp_ap, func=AF.Copy), waits=[t, dfree], sig=True)
                    tpfree[s] = c
                    if h == 7:
                        cprev = Tok(ctx.esem["dve"], "m_dve", ctx.ecnt["dve"])
                        cact = Tok(ctx.esem["act"], "m_act", ctx.ecnt["act"])
                        ostfree[lp % 2] = ph.dma("sp", brT_d[0].rearrange("h d t -> d h t")[:, :, lp * 128:(lp + 1) * 128], ost[:, lp % 2], f"ao{lp % 2}",
                                                 waits=[c, cprev, cact])
            ph.run()

    def phase_gqa():
        with ExitStack() as es2:
            T = lambda name, shape, dt: es2.enter_context(nc.sbuf_tensor(uname(name), list(shape), dt))
            qT = T("g_qT", [128, 8, NTOK], BF16)
            KT = T("g_KT", [128, 4, 2, NTOK], BF16)
            Vb = T("g_V", [128, 64, 2, 129], BF16)
            PT = T("g_PT", [128, 3, 512], BF16)
            ob = T("g_ob", [128, 2, 128], BF16)
            rr = T("g_rr", [128, 2, 1], F32)
            ost = T("g_ost", [128, 2, 4, 128], BF16)
            ph = Phase(ctx)
            ms = ph.op("pool", lambda e: e.memset(Vb[:], 1.0), sig=True)
            lds = [ph.dma("sp", qT[:], qTg_d.rearrange("h d t -> d h t"), "gq")]
            for r in range(4):
                lds.append(ph.dma("sp", KT[:, r], recv_view(0, r).rearrange("(k d) t -> d k t", d=128), "gq"))
                vg_r = recv_view(1, r).rearrange("r (a c) -> (r a) c", c=256)
                for k2 in range(2):
                    lds.append(ph.dma("sp", Vb[:, 16 * r:16 * r + 16, k2, 0:128], vg_r[:, k2 * 128:(k2 + 1) * 128].rearrange("(c p) d -> p c d", p=128), "gq", waits=[ms]))
            ready = [lds[-1]]
            blocks = [(g, t, kc) for g in range(2) for t in range(16) for kc in range(64)]
            N = len(blocks)
            sfree = [None, None]
            ptfree = [None] * 3
            ofree = [None] * 4
            obfree = [None, None]
            tpfree = [None, None]
            ostfree = [None, None]
            qk_tok = [None] * N
            ex_tok = [None] * N
            fin = 0
            for step in range(N + 1):
                if step < N:
                    g, t, kc = blocks[step]
                    s = step % 2
                    qk_tok[step] = ph.op("pe", lambda e, s=s, k=KT[:, kc // 16, g, (kc % 16) * 128:(kc % 16 + 1) * 128], q=qT[:, 4 * g:4 * g + 4, t * 128:(t + 1) * 128]:
                                         e.matmul(psf[:, s, :], lhsT=k, rhs=q, start=True, stop=True), waits=ready + [sfree[s]], sig=True)
                    p = step % 3
                    ex_tok[step] = ph.op("act", lambda e, s=s, p=p: e.activation(out=PT[:, p, :], in_=psf[:, s, :], func=AF.Exp, scale=SCALE),
                                         waits=[qk_tok[step], ptfree[p]], sig=True, chain=False)
                    sfree[s] = ex_tok[step]
                if step >= 1:
                    b = step - 1
                    g, t, kc = blocks[b]
                    p = b % 3
                    tok = None
                    for hh in range(4):
                        tok = ph.op("pe", lambda e, hh=hh, p=p, v=Vb[:, kc, g, :]: e.matmul(psf[:, 2 + hh, 0:129], lhsT=PT[:, p, hh * 128:(hh + 1) * 128], rhs=v,
                                                                                          start=(kc == 0), stop=(kc == 63)),
                                    waits=[ex_tok[b]] + ([ofree[hh]] if kc == 0 else []) if hh == 0 or kc == 0 else (), sig=(hh == 3))
                    ptfree[p] = tok
                    if kc == 63:
                        gi = g * 16 + t
                        for hh in range(4):
                            s2 = fin % 2
                            fin += 1
                            obd = dict(o=ob[:, s2, :], r=rr[:, s2, :])
                            n, tt, c = attn_finish(ph, psf[:, 2 + hh, :], tok, obd, obfree[s2], pst[s2][:, 0:128], tpfree[s2],
                                                   ost[:, gi % 2, hh, :], ostfree[gi % 2] if hh == 0 else None, "dve")
                            ofree[hh] = n
                            obfree[s2] = tt
                            tpfree[s2] = c
                        ostfree[gi % 2] = ph.dma("sp", brT_d[1].rearrange("h d t -> d h t")[:, 4 * g:4 * g + 4, t * 128:(t + 1) * 128], ost[:, gi % 2], f"go{gi % 2}",
                                                 waits=[c])
            ph.run()

    def phase_merge(l):
        for half in range(2):
            with ExitStack() as es2:
                T = lambda name, shape, dt: es2.enter_context(nc.sbuf_tensor(uname(name), list(shape), dt))
                hT = T("m_hT", [128, KC, 1024], BF16)
                bT = T("m_bT", [128, 3, 8, 1024], BF16)
                wt = T("m_wt", [128, 2, 24, 256], BF16)
                sg = T("m_sg", [128, 2, 1024], F32)
                tmp = T("m_tmp", [128, 2, 1024], F32)
                macc = T("m_acc", [128, 2, 2, 1024], F32)
                mb = T("m_mb", [128, 2, 1024], BF16)
                ph = Phase(ctx)
                hs = slice(half * 1024, (half + 1) * 1024)
                ld = None
                for q in range(4):
                    ld = ph.dma("sp", hT[:, 4 * q:4 * q + 4, :], hT_d[4 * q:4 * q + 4, :, hs].rearrange("k p t -> p k t"), "mh")
                for b in range(3):
                    ld = ph.dma("sp", bT[:, b], brT_d[b, :, :, hs].rearrange("k p t -> p k t"), "mh")
                wring = Ring([wt[:, 0], wt[:, 1]])
                psring = Ring([psf[:, 0:2, :], psf[:, 2:4, :], psf[:, 4:6, :]])
                sgR = Ring([sg[:, 0, :], sg[:, 1, :]])
                tmpR = Ring([tmp[:, 0, :], tmp[:, 1, :]])
                mbR = Ring([mb[:, 0, :], mb[:, 1, :]])
                state = {}
                accfree = {}
                tiles = []
                for ct in range(8):
                    for b in range(3):
                        gc0 = 7680 + b * 2048 + ct * 256
                        pieces = [(lambda wap: wap[:, 0:16, :], w_in[l][:, gc0:gc0 + 256].rearrange("(k p) n -> p k n", p=128)),
                                  (lambda wap: wap[:, 16:24, :], w_br[l, b][:, ct * 256:(ct + 1) * 256].rearrange("(k p) n -> p k n", p=128))]
                        units = []
                        for cc in range(2):
                            def evG(pap, tok, setfree, ct=ct, b=b, cc=cc):
                                k, sap, sfree_ = sgR.get()
                                t = ph.op("act", lambda e: e.activation(out=sap, in_=pap.rearrange("p b t -> p (b t)"), func=AF.Sigmoid),
                                          waits=[tok, sfree_, ld], sig=True)
                                setfree(t)
                                state[(ct, b, cc)] = (k, sap, t)
                                yield

                            def evY(pap, tok, setfree, ct=ct, b=b, cc=cc):
                                k, sap, st = state[(ct, b, cc)]
                                pflat = pap.rearrange("p b t -> p (b t)")
                                acc = macc[:, ct % 2, cc, :]
                                if b == 0:
                                    t = ph.op("dve", lambda e: e.tensor_tensor(out=acc, in0=pflat, in1=sap, op=ALU.mult),
                                              waits=[tok, st, accfree.get((ct % 2, cc))], sig=True)
                                    setfree(t)
                                    sgR.free[k] = t
                                else:
                                    tk_, tap, tfree = tmpR.get()
                                    t = ph.op("dve", lambda e: e.tensor_tensor(out=tap, in0=pflat, in1=sap, op=ALU.mult),
                                              waits=[tok, st, tfree], sig=True)
                                    setfree(t)
                                    sgR.free[k] = t
                                    if b == 1:
                                        t2_ = ph.op("pool", lambda e: e.tensor_tensor(out=acc, in0=acc, in1=tap, op=ALU.add), waits=[t], sig=True)
                                        tmpR.free[tk_] = t2_
                                    else:
                                        mk, map_, mfree = mbR.get()
                                        t2_ = ph.op("pool", lambda e: e.tensor_tensor(out=map_, in0=acc, in1=tap, op=ALU.add), waits=[t, mfree], sig=True)
                                        tmpR.free[tk_] = t2_
                                        accfree[(ct % 2, cc)] = t2_
                                        mbR.free[mk] = ph.dma("sp", mT_d[ct * 2 + cc][:, hs], map_, f"mm{mk}", waits=[t2_])
                                yield

                            mmG = [(bk, [(lambda wap, k=k, cc=cc: wap[:, k, cc * 128:(cc + 1) * 128], hT[:, k, bk * 512:(bk + 1) * 512]) for k in range(16)])
                                   for bk in range(2)]
                            mmY = [(bk, [(lambda wap, k=k, cc=cc: wap[:, 16 + k, cc * 128:(cc + 1) * 128], bT[:, b, k, bk * 512:(bk + 1) * 512]) for k in range(8)])
                                   for bk in range(2)]
                            units.append(dict(mm=mmG, evac=evG))
                            units.append(dict(mm=mmY, evac=evY))
                        tiles.append(dict(pieces=pieces, units=units))
                gemm(ph, tiles, wring, "mw", psring, pe_waits=[ld])
                ph.run()

    def phase_proj_res(l, KCn, act_d, Wd, TB, gcol, Xsrc, Xdst, wcols):
        nb = TB // 512
        for blk in range(NTOK // TB):
            with ExitStack() as es2:
                T = lambda name, shape, dt: es2.enter_context(nc.sbuf_tensor(uname(name), list(shape), dt))
                ksplit = (KCn == FC)
                NSQ = 4 if ksplit else 2
                aT = T("r_aT", [128, KCn, TB], BF16)
                if ksplit:
                    wtk = T("r_wtk", [128, 3, 11, 512], BF16)
                    wt = T("r_wt", [128, 2, 1, 128], BF16)
                else:
                    wt = T("r_wt", [128, 2, KCn, wcols], BF16)
                z = T("r_z", [128, KC, TB], F32)
                sq = T("r_sq", [128, NSQ, TB], F32)
                rs = T("r_rs", [128, TB], F32)
                NX = 4
                xt = T("r_xt", [128, NX, TB], F32)
                t1 = T("r_t1", [128, 2, TB], F32)
                xo = T("r_xo", [128, 2, TB], F32)
                ph = Phase(ctx)
                bs = slice(blk * TB, (blk + 1) * TB)
                ld = None
                ldk = []
                if ksplit:
                    for kg_ in range(4):
                        ldk.append(ph.dma("sp", aT[:, kg_ * 11:(kg_ + 1) * 11, :], act_d[kg_ * 11:(kg_ + 1) * 11, :, bs].rearrange("k p t -> p k t"), f"rak{kg_}"))
                else:
                    for q in range(0, KCn, 4):
                        ld = ph.dma("sp", aT[:, q:q + 4, :], act_d[q:q + 4, :, bs].rearrange("k p t -> p k t"), "ra")
                xfree = [None] * NX
                lxs = {}

                def load_x(c_):
                    s4_ = c_ % NX
                    lxs[c_] = ph.dma("sp", xt[:, s4_, :], Xsrc[c_][:, bs], f"rx{s4_}", waits=[xfree[s4_]])

                for c_ in range(NX):
                    load_x(c_)
                wring = Ring([wt[:, 0], wt[:, 1]])
                psring = Ring([psf[:, 0:nb, :], psf[:, 2:2 + nb, :]])
                sqR = Ring([sq[:, i, :] for i in range(NSQ)])
                ssfree = [None]
                sstok = [None]
                tiles = []
                cpt = wcols // 128
                if ksplit:
                    KPG, NKG = 11, 4
                    wslot_free = [None] * 3
                    bankfree = [None] * 4
                    pend = []
                    wtiles = [(cg, kg) for cg in range(4) for kg in range(NKG)]
                    wtoks = {}

                    def issue_w(i):
                        cg, kg = wtiles[i]
                        sl_ = i % 3
                        wtoks[i] = ph.dma("pool", wtk[:, sl_], Wd[kg * KPG * 128:(kg + 1) * KPG * 128, cg * 512:(cg + 1) * 512].rearrange("(k p) n -> p k n", p=128),
                                          f"rk{sl_}", waits=[wslot_free[sl_]])

                    def flush():
                        for (c_, k_, sap_, a1_) in pend:
                            tk_ = ph.op("pe", lambda e: e.matmul(psf[:, 4, :], lhsT=onesD[:], rhs=sap_, start=(c_ == 0), stop=(c_ == KC - 1)), waits=[a1_], sig=True)
                            sqR.free[k_] = tk_
                            sstok[0] = tk_
                        pend.clear()

                    issue_w(0)
                    issue_w(1)
                    for ti, (cg, kg) in enumerate(wtiles):
                        if ti + 2 < len(wtiles):
                            issue_w(ti + 2)
                        sl_ = ti % 3
                        last = None
                        for cc in range(4):
                            for k in range(KPG):
                                first = (kg == 0 and k == 0)
                                lastmm = (kg == NKG - 1 and k == KPG - 1)
                                w_ = []
                                if k == 0 and cc == 0:
                                    w_ += [wtoks[ti], ldk[kg]]
                                if first:
                                    w_.append(bankfree[cc])
                                last = ph.op("pe", lambda e: e.matmul(psf[:, cc, :], lhsT=wtk[:, sl_, k, cc * 128:(cc + 1) * 128], rhs=aT[:, kg * KPG + k, :],
                                                                      start=first, stop=lastmm), waits=w_, sig=(k == KPG - 1))
                            if kg == NKG - 1:
                                c = cg * 4 + cc
                                a2 = ph.op("dve", lambda e: e.tensor_copy(out=z[:, c, :], in_=psf[:, cc, :]), waits=[last], sig=True)
                                bankfree[cc] = a2
                                k_, sap, sfree_ = sqR.get()
                                a1 = ph.op("act", lambda e: e.activation(out=sap, in_=z[:, c, :], func=AF.Square), waits=[a2, sfree_], sig=True)
                                pend.append((c, k_, sap, a1))
                        wslot_free[sl_] = last
                        if kg == 0 and pend:
                            flush()
                    flush()
                for ct in range(0 if ksplit else D // wcols):
                    units = []
                    for cc in range(cpt):
                        c = ct * cpt + cc

                        def ev(pap, tok, setfree, c=c):
                            pflat = pap.rearrange("p b t -> p (b t)")
                            k, sap, sfree_ = sqR.get()
                            a2 = ph.op("dve", lambda e: e.tensor_copy(out=z[:, c, :], in_=pflat), waits=[tok], sig=True)
                            a1 = ph.op("act", lambda e: e.activation(out=sap, in_=z[:, c, :], func=AF.Square), waits=[a2, sfree_, ld], sig=True)
                            setfree(a2)
                            psring.free[(psring.i - 1) % 2] = a2
                            yield
                            tk = None
                            for bk in range(nb):
                                tk = ph.op("pe", lambda e, bk=bk: e.matmul(psf[:, 4 + bk, :], lhsT=onesD[:], rhs=sap[:, bk * 512:(bk + 1) * 512],
                                                                         start=(c == 0), stop=(c == KC - 1)), waits=[a1, a2], sig=(bk == nb - 1))
                            sqR.free[k] = tk
                            sstok[0] = tk
                            yield

                        mm = [(bk, [(lambda wap, k=k, cc=cc: wap[:, k, cc * 128:(cc + 1) * 128], aT[:, k, bk * 512:(bk + 1) * 512]) for k in range(KCn)])
                              for bk in range(nb)]
                        units.append(dict(mm=mm, evac=ev))
                    tiles.append(dict(pieces=[(lambda wap: wap, Wd[:, ct * wcols:(ct + 1) * wcols].rearrange("(k p) n -> p k n", p=128))], units=units))
                if tiles:
                    gemm(ph, tiles, wring, "rw", psring, pe_waits=[ld])
                ra, r = rstd_ops(ph, rs[:], psf[:, 4:4 + nb, :].rearrange("p b t -> p (b t)"), [sstok[0]])
                ofree = [None, None]
                t1free = [None, None]
                for c in range(KC):
                    s = c % 2
                    s4 = c % NX
                    a = ph.op("dve", lambda e: e.scalar_tensor_tensor(out=t1[:, s, :], in0=z[:, c, :], scalar=gains[:, l, gcol * KC + c:gcol * KC + c + 1],
                                                                      in1=rs[:], op0=ALU.mult, op1=ALU.mult), waits=[r, t1free[s]], sig=True)
                    b = ph.op("pool" if c % 2 == 0 else "dve", lambda e: e.tensor_tensor(out=xo[:, s, :], in0=t1[:, s, :], in1=xt[:, s4, :], op=ALU.add),
                              waits=[a, lxs[c], ofree[s]], sig=True)
                    t1free[s] = b
                    xfree[s4] = b
                    ofree[s] = ph.dma("act", Xdst[c][:, bs], xo[:, s, :], f"ro{s}", waits=[b])
                    if c + NX < KC:
                        load_x(c + NX)
                ph.run()

    def phase_ffn_up(l):
        with ExitStack() as es2:
            T = lambda name, shape, dt: es2.enter_context(nc.sbuf_tensor(uname(name), list(shape), dt))
            hT = T("f_hT", [128, KC, NTOK], BF16)
            wt = T("f_wt", [128, 2, 32, 256], BF16)
            sl = T("f_sl", [128, 2, 1024], F32)
            ab = T("f_ab", [128, 3, 1024], BF16)
            ph = Phase(ctx)
            ld = None
            fld = []
            for q in range(4):
                fld.append(ph.dma("sp", hT[:, 4 * q:4 * q + 4, :], hT_d[4 * q:4 * q + 4].rearrange("k p t -> p k t"), f"fhq{q}"))
            wring = Ring([wt[:, 0], wt[:, 1]])
            psring = Ring([psf[:, 0:2, :], psf[:, 2:4, :], psf[:, 4:6, :]])
            slR = Ring([sl[:, 0, :], sl[:, 1, :]])
            abR = Ring([ab[:, i, :] for i in range(3)])
            state = {}
            tiles = []
            for jt in range(FF // 256):
                pieces = [(lambda wap: wap[:, 0:16, :], w_gate[l][:, jt * 256:(jt + 1) * 256].rearrange("(k p) n -> p k n", p=128)),
                          (lambda wap: wap[:, 16:32, :], w_up[l][:, jt * 256:(jt + 1) * 256].rearrange("(k p) n -> p k n", p=128))]
                units = []
                for tb in range(2):
                    for cc in range(2):
                        j = jt * 2 + cc

                        def evG(pap, tok, setfree, key=(jt, tb, cc)):
                            k, sap, sfree_ = slR.get()
                            t = ph.op("act", lambda e: e.activation(out=sap, in_=pap.rearrange("p b t -> p (b t)"), func=AF.Silu), waits=[tok, sfree_, ld], sig=True)
                            setfree(t)
                            state[key] = (k, sap, t)
                            yield

                        def evU(pap, tok, setfree, key=(jt, tb, cc), j=j, tb=tb):
                            k, sap, st = state[key]
                            ak, aap, afree = abR.get()
                            t = ph.op("dve", lambda e: e.tensor_tensor(out=aap, in0=pap.rearrange("p b t -> p (b t)"), in1=sap, op=ALU.mult),
                                      waits=[tok, st, afree], sig=True)
                            setfree(t)
                            slR.free[k] = t
                            abR.free[ak] = ph.dma("sp", aT_d[j][:, tb * 1024:(tb + 1) * 1024], aap, f"fa{ak}", waits=[t])
                            yield

                        mmG = [(bk, [(lambda wap, k=k, cc=cc: wap[:, k, cc * 128:(cc + 1) * 128], hT[:, k, tb * 1024 + bk * 512:tb * 1024 + (bk + 1) * 512], fld[k // 4])
                                     for k in range(16)]) for bk in range(2)]
                        mmU = [(bk, [(lambda wap, k=k, cc=cc: wap[:, 16 + k, cc * 128:(cc + 1) * 128], hT[:, k, tb * 1024 + bk * 512:tb * 1024 + (bk + 1) * 512], fld[k // 4])
                                     for k in range(16)]) for bk in range(2)]
                        units.append(dict(mm=mmG, evac=evG))
                        units.append(dict(mm=mmU, evac=evU))
                tiles.append(dict(pieces=pieces, units=units))
            gemm(ph, tiles, wring, "fw", psring)
            ph.run()

    for l in range(L):
        X = xT_in if l == 0 else xres
        Xout = yT if l == L - 1 else xres
        plist = [
            lambda: phase_norm(X, gains[:, l, 0:KC]),
            lambda: phase_proj(l),
            lambda: phase_exchange(),
            lambda: phase_conv(l),
            lambda: phase_na(l),
            lambda: phase_gqa(),
            lambda: phase_merge(l),
            lambda: phase_proj_res(l, KC, mT_d, w_out[l], 1024, 1, X, xmid, 512),
            lambda: phase_norm(xmid, gains[:, l, 2 * KC:3 * KC]),
            lambda: phase_ffn_up(l),
            lambda: phase_proj_res(l, FC, aT_d, w_down[l], 512, 3, xmid, Xout, 128),
        ]
        for pi, pf in enumerate(plist):
            if (stop_after is None or pi < stop_after) and pi not in skip:
                pf()
    es.close()
    return nc


def _fmajor_vec(v):
    return np.ascontiguousarray(v.reshape(-1, 128).T)


def _na_bias_tables(rpb, qtr):
    out = np.full((5, 8, 128, 768), NEG, dtype=np.float32)
    kk = np.arange(128)
    qq = np.arange(128)
    for si, lp in enumerate([0, 1, 2, 14, 15]):
        p = qtr * 16 + lp
        wlo = max(lp - 1, 0)
        g0 = qtr * 16 - 2 + wlo
        qrow = 2 * p + qq // 64
        qcol = qq % 64
        rstart = np.clip(qrow - 4, 0, 120)
        cstart = np.clip(qcol - 8, 0, 48)
        for j in range(6):
            gc = g0 + j
            if gc < 0 or gc > 63:
                continue
            krow = 2 * gc + kk // 64
            kcol = kk % 64
            dr = krow[:, None] - qrow[None, :]
            dc = kcol[:, None] - qcol[None, :]
            valid = ((krow[:, None] >= rstart[None, :]) & (krow[:, None] < rstart[None, :] + 8)
                     & (kcol[:, None] >= cstart[None, :]) & (kcol[:, None] < cstart[None, :] + 16))
            ri = np.clip(dr + 7, 0, 14)
            ci = np.clip(dc + 15, 0, 30)
            vals = rpb[:, ri, ci]
            out[si, :, :, j * 128:(j + 1) * 128] = np.where(valid[None], vals, NEG)
    return out.astype(NPBF)


def _rope_tables(qtr):
    t = qtr * NTOK + np.arange(NTOK)
    prow = (t // 64).astype(np.float32)
    pcol = (t % 64).astype(np.float32)
    inv = (1.0 / (10000.0 ** (np.arange(32, dtype=np.float32) / 32))).astype(np.float32)
    C = np.zeros((128, NTOK), np.float32)
    Sn = np.zeros((128, NTOK), np.float32)
    for d in range(128):
        pos = prow if d < 64 else pcol
        ang = pos * inv[d % 32]
        C[d] = np.cos(ang)
        Sn[d] = np.sin(ang)
    return C, Sn


def _rot_lhsT():
    Pm = np.zeros((128, 128), np.float32)
    for i in range(128):
        if (i % 64) < 32:
            Pm[i, i + 32] = -1.0
        else:
            Pm[i, i - 32] = 1.0
    return np.ascontiguousarray(Pm.T)


_CACHE = {}


def _prep_common(inputs, layers):
    L = len(layers)
    sl = lambda k: np.ascontiguousarray(inputs[k][layers])
    com = {
        "w_in": sl("w_in"),
        "w_br": np.ascontiguousarray(np.stack([inputs["w_br_na"][layers], inputs["w_br_gqa"][layers], inputs["w_br_conv"][layers]], axis=1)),
        "w_out": sl("w_out"), "w_gate": sl("w_ffn_gate"), "w_up": sl("w_ffn_up"), "w_down": sl("w_ffn_down"),
        "ident": np.eye(128, dtype=np.float32).astype(NPBF),
        "rotT": _rot_lhsT(),
    }
    gains = np.zeros((L, 128, 64), np.float32)
    qkg = np.zeros((L, 128, 2), np.float32)
    convp = np.zeros((L, 128, 32), np.float32)
    for i, l in enumerate(layers):
        for gi, k in enumerate(["pre_mix_g", "post_mix_g", "pre_ffn_g", "post_ffn_g"]):
            gains[i, :, gi * 16:(gi + 1) * 16] = _fmajor_vec(inputs[k][l])
        qkg[i, :, 0] = inputs["q_norm_g"][l]
        qkg[i, :, 1] = inputs["k_norm_g"][l]
        cw = inputs["conv_w"][l]
        cb = inputs["conv_b"][l]
        for ci in range(8):
            for k in range(3):
                convp[i, :, ci * 4 + k] = cw[k, ci * 128:(ci + 1) * 128]
            convp[i, :, ci * 4 + 3] = cb[ci * 128:(ci + 1) * 128]
    com["gains"], com["qkg"], com["convp"] = gains, qkg, convp
    return com


def _prep_core(inputs, layers, c):
    qtr = c % 4
    C, Sn = _rope_tables(qtr)
    sel = np.zeros((128, 8), np.float32)
    if qtr > 0:
        sel[:, qtr - 1] = 1.0
    if qtr < 3:
        sel[:, 4 + qtr + 1] = 1.0
    nab = np.stack([_na_bias_tables(np.asarray(inputs["na_rpb"][l]), qtr) for l in layers], axis=0)
    return {"ropeC": C, "ropeS": Sn, "sel": sel, "nab": nab}


def _x_to_fmajor(x, c):
    b, qtr = c // 4, c % 4
    xs = x[b, qtr * NTOK:(qtr + 1) * NTOK, :]
    return np.ascontiguousarray(xs.T.reshape(KC, 128, NTOK))


def _run(xTs, inputs, layers):
    L = len(layers)
    if L not in _CACHE:
        _CACHE[L] = build(L)
    nc = _CACHE[L]
    com = _prep_common(inputs, layers)
    in_maps = []
    for c in range(NCORE):
        m = dict(com)
        m.update(_prep_core(inputs, layers, c))
        m["xT"] = xTs[c]
        in_maps.append(m)
    res = run_bass_kernel_spmd(nc, in_maps, core_ids=list(range(NCORE)))
    return [r["yT"] for r in res.results]


LAYERS_PER_LAUNCH = 4


def kernel(**inputs):
    inputs = {k: np.asarray(v) for k, v in inputs.items()}
    x = inputs["x"]
    xTs = [_x_to_fmajor(x, c) for c in range(NCORE)]
    for l0 in range(0, DEPTH, LAYERS_PER_LAUNCH):
        xTs = _run(xTs, inputs, list(range(l0, l0 + LAYERS_PER_LAUNCH)))
    out = np.zeros((2, 4 * NTOK, D), np.float32)
    for c in range(NCORE):
        b, qtr = c // 4, c % 4
        out[b, qtr * NTOK:(qtr + 1) * NTOK, :] = xTs[c].reshape(D, NTOK).T
    return out
```
